# Optimizing a Trainium2 kernel written in Bass

```python
import math
import jax
import jax.numpy as jnp
from jax import lax
import numpy as np

D_MODEL = 1024
BATCH = 2
SEQ = 16384
DEPTH = 2

A_HEADS = 8
A_HEAD_DIM = 64
IDX_HEADS = 8
IDX_DIM = 64
TOPK_MAX = 256
B_HEADS = 8
Q_RANK = 256
KV_RANK = 128
NOPE_DIM = 64
ROPE_DIM = 32
V_DIM = 64
ROPE_THETA = 10000.0
NUM_BUCKETS = 32
MAX_DISTANCE = 128
D_FF = 2816
CONV_WIDTH = 3
Q_BLOCK = 128
LN_EPS = 1e-5
RMS_EPS = 1e-6
DEEPNORM_ALPHA = (2 * DEPTH) ** 0.25
DEEPNORM_BETA = (8 * DEPTH) ** -0.25
A_WIDTH = A_HEADS * A_HEAD_DIM
B_WIDTH = B_HEADS * V_DIM
SPLIT_SIZES = (A_WIDTH, A_WIDTH, A_WIDTH, IDX_HEADS * IDX_DIM, IDX_DIM, IDX_HEADS,
               Q_RANK, KV_RANK, ROPE_DIM, D_MODEL, D_MODEL)
IN_WIDTH = sum(SPLIT_SIZES)

kernel_name = "hybrid_dsa_mla_convffn_deepnorm"


def layer_norm(x, g, b):
    xf = x.astype(jnp.float32)
    mu = jnp.mean(xf, axis=-1, keepdims=True)
    var = jnp.mean(jnp.square(xf - mu), axis=-1, keepdims=True)
    return ((xf - mu) * lax.rsqrt(var + LN_EPS)).astype(x.dtype) * g + b


def rms_norm(x, g):
    xf = x.astype(jnp.float32)
    return (xf * lax.rsqrt(jnp.mean(xf * xf, axis=-1, keepdims=True) + RMS_EPS)).astype(x.dtype) * g


def apply_rope(x, positions):
    half = ROPE_DIM // 2
    inv_freq = ROPE_THETA ** (-jnp.arange(half, dtype=jnp.float32) * (2.0 / ROPE_DIM))
    ang = positions.astype(jnp.float32)[:, :, None] * inv_freq
    ang = ang.reshape(ang.shape[:2] + (1,) * (x.ndim - 3) + (half,))
    cos, sin = jnp.cos(ang), jnp.sin(ang)
    x1 = x[..., :half].astype(jnp.float32)
    x2 = x[..., half:].astype(jnp.float32)
    return jnp.concatenate([x1 * cos - x2 * sin, x1 * sin + x2 * cos], axis=-1).astype(x.dtype)


def t5_bucket(rel):
    n = jnp.maximum(rel, 0)
    max_exact = NUM_BUCKETS // 2
    log_ratio = jnp.log(jnp.maximum(n, 1).astype(jnp.float32) / max_exact) / math.log(MAX_DISTANCE / max_exact)
    large = max_exact + (log_ratio * (NUM_BUCKETS - max_exact)).astype(jnp.int32)
    large = jnp.minimum(large, NUM_BUCKETS - 1)
    return jnp.where(n < max_exact, n, large)


def dsa_attention(qa, ka, va, iq, ik, iw, positions, rel_bias):
    bsz, seq = qa.shape[0], qa.shape[1]
    topk = min(TOPK_MAX, seq // 4)
    n_blocks = seq // Q_BLOCK
    key_idx = jnp.arange(seq, dtype=jnp.int32)
    a_scale = A_HEAD_DIM ** -0.5
    i_scale = IDX_DIM ** -0.5
    gather = jax.vmap(lambda t, ii: t[ii])

    def block(i):
        start = i * Q_BLOCK
        q = lax.dynamic_slice_in_dim(qa, start, Q_BLOCK, axis=1)
        q_i = lax.dynamic_slice_in_dim(iq, start, Q_BLOCK, axis=1)
        w_i = lax.dynamic_slice_in_dim(iw, start, Q_BLOCK, axis=1)
        q_pos = lax.dynamic_slice_in_dim(positions, start, Q_BLOCK, axis=1)
        q_idx = start + jnp.arange(Q_BLOCK, dtype=jnp.int32)
        causal = key_idx[None, :] <= q_idx[:, None]
        head_scores = jax.nn.relu(jnp.einsum('bqhd,bsd->bqhs', q_i, ik) * i_scale)
        scores = jnp.einsum('bqhs,bqh->bqs', head_scores, w_i).astype(jnp.float32)
        scores = jnp.where(causal[None], scores, -jnp.inf)
        _, sel = lax.top_k(scores, topk)
        valid = sel <= q_idx[None, :, None]
        k_sel = gather(ka, sel)
        v_sel = gather(va, sel)
        k_pos = gather(positions, sel)
        bias = rel_bias[t5_bucket(q_pos[:, :, None] - k_pos)]
        logits = (jnp.einsum('bqhd,bqkhd->bqkh', q, k_sel).astype(jnp.float32) * a_scale
                  + bias.astype(jnp.float32))
        logits = jnp.where(valid[..., None], logits, -jnp.inf)
        p = jax.nn.softmax(logits, axis=2).astype(va.dtype)
        return jnp.einsum('bqkh,bqkhd->bqhd', p, v_sel)

    out = lax.map(block, jnp.arange(n_blocks, dtype=jnp.int32))
    return out.transpose(1, 0, 2, 3, 4).reshape(bsz, seq, A_WIDTH)


def mla_attention(q_nope, q_rope, k_nope, k_rope, v):
    bsz, seq = q_nope.shape[0], q_nope.shape[1]
    n_blocks = seq // Q_BLOCK
    key_idx = jnp.arange(seq, dtype=jnp.int32)
    scale = (NOPE_DIM + ROPE_DIM) ** -0.5

    def block(i):
        start = i * Q_BLOCK
        qn = lax.dynamic_slice_in_dim(q_nope, start, Q_BLOCK, axis=1)
        qr = lax.dynamic_slice_in_dim(q_rope, start, Q_BLOCK, axis=1)
        q_idx = start + jnp.arange(Q_BLOCK, dtype=jnp.int32)
        causal = key_idx[None, :] <= q_idx[:, None]
        logits = (jnp.einsum('bqhd,bshd->bhqs', qn, k_nope)
                  + jnp.einsum('bqhr,bsr->bhqs', qr, k_rope)).astype(jnp.float32) * scale
        logits = jnp.where(causal[None, None], logits, -jnp.inf)
        p = jax.nn.softmax(logits, axis=-1).astype(v.dtype)
        return jnp.einsum('bhqs,bshd->bqhd', p, v)

    out = lax.map(block, jnp.arange(n_blocks, dtype=jnp.int32))
    return out.transpose(1, 0, 2, 3, 4).reshape(bsz, seq, B_WIDTH)


def token_mixer(h, positions, rel_bias, w_in, q_norm_g, w_uq, kv_norm_g, w_ukv,
                w_branch_a, w_branch_b, w_out):
    bsz, seq = h.shape[0], h.shape[1]
    proj = h @ w_in
    split_at = np.cumsum(SPLIT_SIZES)[:-1].tolist()
    qa, ka, va, iq, ik, iw, cq, ckv, kr, ga, gb = jnp.split(proj, split_at, axis=-1)
    heads_a = (bsz, seq, A_HEADS, A_HEAD_DIM)
    ya = dsa_attention(qa.reshape(heads_a), ka.reshape(heads_a), va.reshape(heads_a),
                       iq.reshape(bsz, seq, IDX_HEADS, IDX_DIM), ik,
                       iw * (IDX_HEADS ** -0.5), positions, rel_bias)
    q = (rms_norm(cq, q_norm_g) @ w_uq).reshape(bsz, seq, B_HEADS, NOPE_DIM + ROPE_DIM)
    q_nope, q_rope = q[..., :NOPE_DIM], apply_rope(q[..., NOPE_DIM:], positions)
    kv = (rms_norm(ckv, kv_norm_g) @ w_ukv).reshape(bsz, seq, B_HEADS, NOPE_DIM + V_DIM)
    k_nope, v = kv[..., :NOPE_DIM], kv[..., NOPE_DIM:]
    k_rope = apply_rope(kr, positions)
    yb = mla_attention(q_nope, q_rope, k_nope, k_rope, v)
    merged = jax.nn.sigmoid(ga) * (ya @ w_branch_a) + jax.nn.sigmoid(gb) * (yb @ w_branch_b)
    return merged @ w_out


def conv_ffn(h, w_up, conv_w, conv_b, w_down):
    u = h @ w_up
    ch = u.shape[-1]
    u = lax.conv_general_dilated(u, conv_w[:, None, :], window_strides=(1,),
                                 padding=[(CONV_WIDTH - 1, 0)],
                                 dimension_numbers=('NWC', 'WIO', 'NWC'),
                                 feature_group_count=ch) + conv_b
    g, val = jnp.split(u, 2, axis=-1)
    return (jax.nn.silu(g) * val) @ w_down


def setup_inputs(seed: int = 0) -> dict:
    key = jax.random.key(seed)
    ks = jax.random.split(key, 24)
    f32 = jnp.float32

    def nrm(k, shape, scale):
        return jax.random.normal(k, shape, f32) * scale

    x = nrm(ks[0], (BATCH, SEQ, D_MODEL), 1.0)
    c = nrm(ks[1], (BATCH, D_MODEL), 1.0)
    offset = jax.random.randint(ks[2], (BATCH, 1), 0, 1024, dtype=jnp.int32)
    positions = offset + jnp.arange(SEQ, dtype=jnp.int32)[None, :]
    rel_bias = nrm(ks[3], (NUM_BUCKETS, A_HEADS), 0.2)
    w_ada = nrm(ks[4], (DEPTH, D_MODEL, 6 * D_MODEL), 0.5 * D_MODEL ** -0.5)
    b_ada = nrm(ks[5], (DEPTH, 6 * D_MODEL), 0.01)
    w_in = nrm(ks[6], (DEPTH, D_MODEL, IN_WIDTH), D_MODEL ** -0.5)
    q_norm_g = 1.0 + nrm(ks[7], (DEPTH, Q_RANK), 0.01)
    w_uq = nrm(ks[8], (DEPTH, Q_RANK, B_HEADS * (NOPE_DIM + ROPE_DIM)), Q_RANK ** -0.5)
    kv_norm_g = 1.0 + nrm(ks[9], (DEPTH, KV_RANK), 0.01)
    w_ukv = nrm(ks[10], (DEPTH, KV_RANK, B_HEADS * (NOPE_DIM + V_DIM)), KV_RANK ** -0.5)
    w_branch_a = nrm(ks[11], (DEPTH, A_WIDTH, D_MODEL), A_WIDTH ** -0.5)
    w_branch_b = nrm(ks[12], (DEPTH, B_WIDTH, D_MODEL), B_WIDTH ** -0.5)
    w_out = nrm(ks[13], (DEPTH, D_MODEL, D_MODEL), DEEPNORM_BETA * D_MODEL ** -0.5)
    ln1_g = 1.0 + nrm(ks[14], (DEPTH, D_MODEL), 0.01)
    ln1_b = nrm(ks[15], (DEPTH, D_MODEL), 0.01)
    w_up = nrm(ks[16], (DEPTH, D_MODEL, 2 * D_FF), D_MODEL ** -0.5)
    conv_w = nrm(ks[17], (DEPTH, CONV_WIDTH, 2 * D_FF), CONV_WIDTH ** -0.5)
    conv_b = nrm(ks[18], (DEPTH, 2 * D_FF), 0.01)
    w_down = nrm(ks[19], (DEPTH, D_FF, D_MODEL), DEEPNORM_BETA * D_FF ** -0.5)
    ln2_g = 1.0 + nrm(ks[20], (DEPTH, D_MODEL), 0.01)
    ln2_b = nrm(ks[21], (DEPTH, D_MODEL), 0.01)
    return {"x": x, "c": c, "positions": positions, "rel_bias": rel_bias,
            "w_ada": w_ada, "b_ada": b_ada, "w_in": w_in,
            "q_norm_g": q_norm_g, "w_uq": w_uq, "kv_norm_g": kv_norm_g, "w_ukv": w_ukv,
            "w_branch_a": w_branch_a, "w_branch_b": w_branch_b, "w_out": w_out,
            "ln1_g": ln1_g, "ln1_b": ln1_b,
            "w_up": w_up, "conv_w": conv_w, "conv_b": conv_b, "w_down": w_down,
            "ln2_g": ln2_g, "ln2_b": ln2_b}


def reference(x, c, positions, rel_bias, w_ada, b_ada, w_in, q_norm_g, w_uq, kv_norm_g, w_ukv,
              w_branch_a, w_branch_b, w_out, ln1_g, ln1_b, w_up, conv_w, conv_b, w_down,
              ln2_g, ln2_b):
    for l in range(DEPTH):
        mod = jax.nn.silu(c) @ w_ada[l] + b_ada[l]
        sh1, sc1, g1, sh2, sc2, g2 = jnp.split(mod[:, None, :], 6, axis=-1)
        h = x * (1.0 + sc1) + sh1
        y = token_mixer(h, positions, rel_bias, w_in[l], q_norm_g[l], w_uq[l], kv_norm_g[l], w_ukv[l],
                        w_branch_a[l], w_branch_b[l], w_out[l])
        x = layer_norm(DEEPNORM_ALPHA * x + g1 * y, ln1_g[l], ln1_b[l])
        h = x * (1.0 + sc2) + sh2
        y = conv_ffn(h, w_up[l], conv_w[l], conv_b[l], w_down[l])
        x = layer_norm(DEEPNORM_ALPHA * x + g2 * y, ln2_g[l], ln2_b[l])
    return x
```

```python
import numpy as np
import ml_dtypes
from contextlib import ExitStack
import concourse.bass as bass
import concourse.mybir as mybir
from concourse.bass_utils import run_bass_kernel_spmd

F32 = mybir.dt.float32
BF16 = mybir.dt.bfloat16
I32 = mybir.dt.int32
ALU = mybir.AluOpType
AF = mybir.ActivationFunctionType
AX = mybir.AxisListType
NPBF = ml_dtypes.bfloat16

D = 1024
NHEAD = 8
DFF = 2816
NEG = -30000.0
NIT = 22
LN_EPS = 1e-5
RMS_EPS = 1e-6
ALPHA = 4 ** 0.25
TWO_PI = float(2 * np.pi)
MAGIC = 12582912.0


class Buf:
    def __init__(self, t, name):
        self.t = t
        self.name = name
        self.w = None
        self.r = []
        self.dsem = None
        self.ssem = None
        self.fresh = True

    def __getitem__(self, idx):
        return self.t[idx]


class KB:
    def __init__(self, nc, es):
        self.nc = nc
        self.es = es
        self.E = dict(pe=nc.tensor, act=nc.scalar, dve=nc.vector, pool=nc.gpsimd, sp=nc.sync)
        self.sem = {k: es.enter_context(nc.semaphore("s_" + k)) for k in self.E}
        self.cnt = {k: 0 for k in self.E}
        self.waited = {}
        self.nsem = 0
        self.stores = {}

    def sbuf(self, name, shape, dt, es=None):
        return Buf((es or self.es).enter_context(self.nc.sbuf_tensor(name, shape, dt)), name)

    def psum(self, name):
        return Buf(self.es.enter_context(self.nc.psum_tensor(name, [128, 512], F32)), name)

    def newsem(self, name):
        self.nsem += 1
        key = "%s_%d" % (name, self.nsem)
        return [self.es.enter_context(self.nc.semaphore(key)), 0, key]

    def wait(self, eng, toks):
        best = {}
        for t in toks:
            if t is None:
                continue
            if t[2] not in best or best[t[2]][1] < t[1]:
                best[t[2]] = t
        for t in best.values():
            sem, val, key = t
            if eng == "pe" and key == "pe":
                continue
            if self.waited.get((eng, key), 0) >= val:
                continue
            self.E[eng].wait_ge(sem, val)
            self.waited[(eng, key)] = val

    def deps(self, reads, writes):
        d = []
        for b in reads:
            d.append(b.w)
        for b in writes:
            d.append(b.w)
            d.extend(b.r)
        return d

    def commit(self, tok, reads, writes):
        for b in reads:
            b.r.append(tok)
        for b in writes:
            b.w = tok
            b.r = []

    def op(self, eng, fn, reads=(), writes=()):
        self.wait(eng, self.deps(reads, writes))
        ins = fn()
        self.cnt[eng] += 1
        ins.then_inc(self.sem[eng], 1)
        tok = (self.sem[eng], self.cnt[eng], eng)
        self.commit(tok, reads, writes)
        return tok

    def mm(self, out_ap, pairs, reads, ps):
        self.wait("pe", self.deps(reads, [ps]))
        n = len(pairs)
        ins = None
        for i, (l, r) in enumerate(pairs):
            ins = self.mm_raw(ps, out_ap, l, r)
        self.cnt["pe"] += 1
        ins.then_inc(self.sem["pe"], 1)
        tok = (self.sem["pe"], self.cnt["pe"], "pe")
        self.commit(tok, reads, [ps])
        return tok

    def mm_raw(self, ps, out_ap, l, r):
        st = ps.fresh
        ps.fresh = False
        return self.nc.tensor.matmul(out_ap, lhsT=l, rhs=r, start=st, stop=True, skip_group_check=True)

    def mark_pe(self, ins, reads, writes):
        self.cnt["pe"] += 1
        ins.then_inc(self.sem["pe"], 1)
        tok = (self.sem["pe"], self.cnt["pe"], "pe")
        self.commit(tok, reads, writes)
        return tok

    def load(self, q, buf, out_ap, in_ap, extra_reads=()):
        self.wait(q, self.deps(extra_reads, [buf]))
        if buf.dsem is None:
            buf.dsem = self.newsem("d")
        self.E[q].dma_start(out=out_ap, in_=in_ap).then_inc(buf.dsem[0], 16)
        buf.dsem[1] += 16
        tok = (buf.dsem[0], buf.dsem[1], buf.dsem[2])
        self.commit(tok, extra_reads, [buf])
        return tok

    def load_more(self, q, buf, out_ap, in_ap):
        self.E[q].dma_start(out=out_ap, in_=in_ap).then_inc(buf.dsem[0], 16)
        buf.dsem[1] += 16
        tok = (buf.dsem[0], buf.dsem[1], buf.dsem[2])
        buf.w = tok
        return tok

    def store(self, q, buf, out_ap, in_ap):
        self.wait(q, [buf.w])
        if buf.ssem is None:
            buf.ssem = self.newsem("s")
        self.E[q].dma_start(out=out_ap, in_=in_ap).then_inc(buf.ssem[0], 16)
        buf.ssem[1] += 16
        tok = (buf.ssem[0], buf.ssem[1], buf.ssem[2])
        buf.r.append(tok)
        self.stores[buf.ssem[2]] = tok
        return tok

    def finish(self):
        self.wait("sp", list(self.stores.values()))

    def barrier(self):
        toks = [(self.sem[k], self.cnt[k], k) for k in self.E if self.cnt[k] > 0]
        toks += list(self.stores.values())
        for k in self.E:
            self.wait(k, toks)


def _dram_in(nc, name, shape, dt):
    return nc.dram_tensor(name, list(shape), dt, kind="ExternalInput").ap()


def _dram_out(nc, name, shape, dt):
    return nc.dram_tensor(name, list(shape), dt, kind="ExternalOutput").ap()


def load_weight_bf16(kb, stg, dst_buf, dst_ap_fn, src_ap_fn, nrows, ncols, state):
    c0 = 0
    while c0 < ncols:
        w = min(2048, ncols - c0)
        s = stg[state[0] % 2]
        state[0] += 1
        kb.load("sp", s, s[0:nrows, 0:w], src_ap_fn(c0, w))
        eng = "dve"
        e = kb.E[eng]
        kb.op(eng, lambda e=e, s=s, c0=c0, w=w: e.tensor_copy(out=dst_ap_fn(c0, w), in_=s[0:nrows, 0:w]),
              reads=[s], writes=[dst_buf])
        c0 += w


NC1 = 512 * 3 + 1024 * 2 + 256 + 128 + 64 + 32 + 32 + 8
P_TT = 256


def build_P(T):
    nc = bass.Bass("TRN2", target_bir_lowering=False)
    TT = P_TT
    NT = T // TT
    NS = TT // 128
    x = _dram_in(nc, "x", [T, D], F32)
    ccol = _dram_in(nc, "ccol", [128, 8], F32)
    wada = _dram_in(nc, "wada", [D, 2048], F32)
    bada = _dram_in(nc, "bada", [128, 16], F32)
    pos = _dram_in(nc, "pos", [1, T], I32)
    cst = _dram_in(nc, "cst", [128, 4], F32)
    identd = _dram_in(nc, "ident", [128, 128], F32)
    onesd = _dram_in(nc, "ones", [128, 128], F32)
    w1 = _dram_in(nc, "w1", [D, NC1], F32)
    wva = _dram_in(nc, "wva", [D, 512], F32)
    wuq = _dram_in(nc, "wuq", [256, 1024], F32)
    qg = _dram_in(nc, "qg", [128, 2], F32)
    wukv = _dram_in(nc, "wukv", [128, 1024], F32)
    kvg = _dram_in(nc, "kvg", [128, 1], F32)
    o_qa = _dram_out(nc, "o_qa", [512, T], BF16)
    o_ka = _dram_out(nc, "o_ka", [512, T], BF16)
    o_iq = _dram_out(nc, "o_iq", [512, T], BF16)
    o_ga = _dram_out(nc, "o_ga", [1024, T], F32)
    o_gb = _dram_out(nc, "o_gb", [1024, T], F32)
    o_ik = _dram_out(nc, "o_ik", [64, T], BF16)
    o_kr = _dram_out(nc, "o_kr", [32, T], BF16)
    o_iw = _dram_out(nc, "o_iw", [8, T], F32)
    o_qn = _dram_out(nc, "o_qn", [512, T], BF16)
    o_qr = _dram_out(nc, "o_qr", [256, T], BF16)
    o_kn = _dram_out(nc, "o_kn", [512, T], BF16)
    o_va = _dram_out(nc, "o_va", [T, 512], BF16)
    o_vm = _dram_out(nc, "o_vm", [T, 512], BF16)

    with ExitStack() as es:
        kb = KB(nc, es)
        V, A, G = nc.vector, nc.scalar, nc.gpsimd
        banks = [kb.psum("bank%d" % i) for i in range(8)]
        bi = [0]

        def nb():
            b = banks[bi[0] % 8]
            bi[0] += 1
            b.fresh = True
            return b

        cst_s = kb.sbuf("cst_s", [128, 4], F32)
        kb.load("sp", cst_s, cst_s[:], cst)
        ident = kb.sbuf("ident_s", [128, 128], F32)
        kb.load("sp", ident, ident[:], identd)
        ones = kb.sbuf("ones_s", [128, 128], F32)
        kb.load("sp", ones, ones[:], onesd)
        qg_s = kb.sbuf("qg_s", [128, 2], F32)
        kb.load("sp", qg_s, qg_s[:], qg)
        kvg_s = kb.sbuf("kvg_s", [128, 1], F32)
        kb.load("sp", kvg_s, kvg_s[:], kvg)
        bada_s = kb.sbuf("bada_s", [128, 16], F32)
        kb.load("sp", bada_s, bada_s[:], bada)
        ccol_s = kb.sbuf("ccol_s", [128, 8], F32)
        kb.load("sp", ccol_s, ccol_s[:], ccol)
        silc = kb.sbuf("silc", [128, 8], F32)
        kb.op("act", lambda: A.activation(out=silc[:], in_=ccol_s[:], func=AF.Silu), reads=[ccol_s], writes=[silc])

        silc16 = kb.sbuf("silc16", [128, 8, 16], F32)
        for kc in range(8):
            kb.op("dve", lambda kc=kc: V.tensor_scalar(out=silc16[:, kc, :], in0=ones[:, 0:16], scalar1=silc[:, kc:kc + 1], scalar2=None, op0=ALU.mult),
                  reads=[ones, silc], writes=[silc16])
        stg = [kb.sbuf("stg%d" % i, [128, 2048], F32) for i in range(2)]
        wst = [0]
        mod1 = kb.sbuf("mod1", [128, 16], F32)
        mb = nb()
        for q4 in range(4):
            for half in range(2):
                s = stg[wst[0] % 2]
                wst[0] += 1
                kb.load("sp", s, s[:].rearrange("p (k c) -> p k c", k=4),
                        wada[half * 512:(half + 1) * 512, q4 * 512:(q4 + 1) * 512].rearrange("(k p) c -> p k c", p=128))
                for dc in range(4):
                    j = q4 * 4 + dc
                    kb.wait("pe", kb.deps([s, silc16], [mb]))
                    for k4 in range(4):
                        kc = half * 4 + k4
                        ins = kb.mm_raw(mb, mb[:, j * 16:(j + 1) * 16], s[:, k4 * 512 + dc * 128:k4 * 512 + (dc + 1) * 128], silc16[:, kc, :])
                    kb.mark_pe(ins, [s, silc16], [mb])
        kb.op("dve", lambda: V.tensor_tensor(out=mod1[:], in0=mb[:, 0:256].rearrange("p (j r) -> p j r", r=16)[:, :, 0], in1=bada_s[:], op=ALU.add),
              reads=[mb, bada_s], writes=[mod1])
        sc1p = kb.sbuf("sc1p", [128, 8], F32)
        kb.op("dve", lambda: V.tensor_scalar(out=sc1p[:], in0=mod1[:, 8:16], scalar1=1.0, scalar2=None, op0=ALU.add),
              reads=[mod1], writes=[sc1p])

        w1_s = kb.sbuf("w1_s", [128, 8, NC1], BF16)
        for kc in range(8):
            load_weight_bf16(kb, stg, w1_s, lambda c0, w, kc=kc: w1_s[:, kc, c0:c0 + w],
                             lambda c0, w, kc=kc: w1[kc * 128:(kc + 1) * 128, c0:c0 + w], 128, NC1, wst)
        wva_s = kb.sbuf("wva_s", [128, 8, 512], BF16)
        for kc in range(8):
            load_weight_bf16(kb, stg, wva_s, lambda c0, w, kc=kc: wva_s[:, kc, c0:c0 + w],
                             lambda c0, w, kc=kc: wva[kc * 128:(kc + 1) * 128, c0:c0 + w], 128, 512, wst)
        wuq_s = kb.sbuf("wuq_s", [128, 2, 1024], BF16)
        for kc in range(2):
            s = stg[wst[0] % 2]
            wst[0] += 1
            kb.load("sp", s, s[:, 0:1024], wuq[kc * 128:(kc + 1) * 128, :])
            kb.op("dve", lambda s=s, kc=kc: V.tensor_scalar(out=wuq_s[:, kc, :], in0=s[:, 0:1024], scalar1=qg_s[:, kc:kc + 1],
                                                            scalar2=None, op0=ALU.mult), reads=[s, qg_s], writes=[wuq_s])
        wukv_s = kb.sbuf("wukv_s", [128, 1024], BF16)
        s = stg[wst[0] % 2]
        wst[0] += 1
        kb.load("sp", s, s[:, 0:1024], wukv)
        kb.op("dve", lambda s=s: V.tensor_scalar(out=wukv_s[:], in0=s[:, 0:1024], scalar1=kvg_s[:, 0:1], scalar2=None, op0=ALU.mult),
              reads=[s, kvg_s], writes=[wukv_s])

        xt = kb.sbuf("xt", [128, NS, D], F32)
        hT = kb.sbuf("hT", [128, 8, TT], BF16)
        st_bf = kb.sbuf("st_bf", [128, 12, TT], BF16)
        st_g = [kb.sbuf("st_g%d" % i, [128, 8, TT], F32) for i in range(2)]
        st_ik = kb.sbuf("st_ik", [64, TT], BF16)
        st_kr = kb.sbuf("st_kr", [32, TT], BF16)
        st_iw = kb.sbuf("st_iw", [8, TT], F32)
        st_qn = kb.sbuf("st_qn", [128, 4, TT], BF16)
        st_qr = kb.sbuf("st_qr", [128, 2, TT], BF16)
        st_kn = kb.sbuf("st_kn", [128, 4, TT], BF16)
        st_va = kb.sbuf("st_va", [128, NS, 512], BF16)
        st_vm = kb.sbuf("st_vm", [128, NS, 512], BF16)
        posi = kb.sbuf("posi", [128, TT], I32)
        ang = kb.sbuf("ang", [128, TT], F32)
        kk = kb.sbuf("kk", [128, TT], F32)
        aab = kb.sbuf("aab", [128, TT], F32)
        Cc = kb.sbuf("Cc", [128, TT], F32)
        Ss = kb.sbuf("Ss", [128, TT], F32)
        cq_raw = kb.sbuf("cq_raw", [128, 3, TT], F32)
        sq = kb.sbuf("sq", [128, 3, TT], BF16)
        rstd = kb.sbuf("rstd", [128, 2, TT], F32)
        cqn = kb.sbuf("cqn", [128, 3, TT], BF16)
        rt1 = kb.sbuf("rt1", [128, TT], F32)
        rt2 = kb.sbuf("rt2", [128, TT], F32)
        ones_bf = kb.sbuf("ones_bf", [128, 128], BF16)
        kb.op("dve", lambda: V.tensor_copy(out=ones_bf[:], in_=ones[:]), reads=[ones], writes=[ones_bf])
        half_pi = kb.sbuf("half_pi", [128, 1], F32)
        kb.op("dve", lambda: V.memset(half_pi[:], float(np.pi / 2)), writes=[half_pi])
        evi = [0]

        def evac_copy(dst_buf, dst_ap, src_buf, src_ap):
            evi[0] += 1
            if evi[0] % 2 == 0:
                return kb.op("act", lambda: A.copy(out=dst_ap, in_=src_ap), reads=[src_buf], writes=[dst_buf])
            return kb.op("dve", lambda: V.tensor_copy(out=dst_ap, in_=src_ap), reads=[src_buf], writes=[dst_buf])

        import os
        PSTOP = int(os.environ.get("P_STOP", "99"))
        for it in range(NT if PSTOP > 1 else 0):
            t0 = it * TT
            kb.load("sp", xt, xt[:], x[t0:t0 + TT, :].rearrange("(s p) d -> p s d", p=128))
            kb.load("sp", posi, posi[:], pos[0:1, t0:t0 + TT].partition_broadcast(128))
            for kc in range(8):
                b = nb()
                kb.wait("pe", kb.deps([xt, ident], [b]))
                for s_ in range(NS):
                    ins = nc.tensor.transpose(out=b[:, s_ * 128:(s_ + 1) * 128], in_=xt[:, s_, kc * 128:(kc + 1) * 128], identity=ident[:])
                kb.mark_pe(ins, [xt, ident], [b])
                kb.op("act", lambda b=b, kc=kc: A.activation(out=hT[:, kc, :], in_=b[:, 0:TT], func=AF.Identity,
                                                             bias=mod1[:, kc:kc + 1], scale=sc1p[:, kc:kc + 1]),
                      reads=[b, mod1, sc1p], writes=[hT])
            kb.op("dve", lambda: V.tensor_copy(out=ang[:], in_=posi[:]), reads=[posi], writes=[ang])
            kb.op("dve", lambda: V.tensor_scalar(out=ang[:], in0=ang[:], scalar1=cst_s[:, 0:1], scalar2=None, op0=ALU.mult),
                  reads=[cst_s], writes=[ang])
            kb.op("dve", lambda: V.tensor_scalar(out=kk[:], in0=ang[:], scalar1=float(1.0 / TWO_PI), scalar2=MAGIC, op0=ALU.mult, op1=ALU.add),
                  reads=[ang], writes=[kk])
            kb.op("dve", lambda: V.tensor_scalar(out=kk[:], in0=kk[:], scalar1=-MAGIC, scalar2=None, op0=ALU.add), writes=[kk])
            kb.op("dve", lambda: V.scalar_tensor_tensor(out=ang[:], in0=kk[:], scalar=-6.28125, in1=ang[:], op0=ALU.mult, op1=ALU.add),
                  reads=[kk], writes=[ang])
            kb.op("dve", lambda: V.scalar_tensor_tensor(out=ang[:], in0=kk[:], scalar=-0.0019353071795864769, in1=ang[:], op0=ALU.mult, op1=ALU.add),
                  reads=[kk], writes=[ang])
            kb.op("dve", lambda: V.tensor_scalar(out=ang[:], in0=ang[:], scalar1=float(np.pi), scalar2=float(-np.pi), op0=ALU.min, op1=ALU.max),
                  writes=[ang])
            kb.op("dve", lambda: V.scalar_tensor_tensor(out=aab[:], in0=ang[:], scalar=-1.0, in1=ang[:], op0=ALU.mult, op1=ALU.max),
                  reads=[ang], writes=[aab])
            kb.op("act", lambda: A.activation(out=Ss[:], in_=ang[:], func=AF.Sin, scale=cst_s[:, 1:2]), reads=[ang, cst_s], writes=[Ss])
            kb.op("act", lambda: A.activation(out=Cc[:], in_=aab[:], func=AF.Sin, scale=-1.0, bias=half_pi[:]), reads=[aab, half_pi], writes=[Cc])

            if PSTOP == 2:
                continue

            def fm_block(c0, m, kcs=8, lhs=None, rhs=None):
                b = nb()
                lhs = lhs or (lambda kc: w1_s[:, kc, c0:c0 + m])
                rhs = rhs or (lambda kc: hT[:, kc, :])
                kb.mm(b[0:m, 0:TT], [(lhs(kc), rhs(kc)) for kc in range(kcs)], [hT, w1_s, wuq_s, wukv_s, cqn], b)
                return b

            for blk in range(12):
                b = fm_block(blk * 128, 128)
                evac_copy(st_bf, st_bf[:, blk, :], b, b[:, 0:TT])
            kb.store("sp", st_bf, o_qa[:, t0:t0 + TT].rearrange("(b p) t -> p b t", p=128), st_bf[:, 0:4, :])
            kb.store("sp", st_bf, o_ka[:, t0:t0 + TT].rearrange("(b p) t -> p b t", p=128), st_bf[:, 4:8, :])
            kb.store("sp", st_bf, o_iq[:, t0:t0 + TT].rearrange("(b p) t -> p b t", p=128), st_bf[:, 8:12, :])
            if PSTOP == 3:
                continue
            for gi, og in enumerate((o_ga, o_gb)):
                sg = st_g[gi]
                for blk in range(8):
                    b = fm_block(1536 + gi * 1024 + blk * 128, 128)
                    kb.op("act", lambda b=b, blk=blk, sg=sg: A.activation(out=sg[:, blk, :], in_=b[:, 0:TT], func=AF.Sigmoid),
                          reads=[b], writes=[sg])
                kb.store("sp", sg, og[:, t0:t0 + TT].rearrange("(b p) t -> p b t", p=128), sg[:])
            for j in range(3):
                b = fm_block(3584 + j * 128, 128)
                kb.op("dve", lambda b=b, j=j: V.tensor_copy(out=cq_raw[:, j, :], in_=b[:, 0:TT]), reads=[b], writes=[cq_raw])
                kb.op("dve", lambda j=j: V.tensor_tensor(out=sq[:, j, :], in0=cq_raw[:, j, :], in1=cq_raw[:, j, :], op=ALU.mult), reads=[cq_raw], writes=[sq])
            for j, (lo, hi, n) in enumerate(((0, 2, 256.0), (2, 3, 128.0))):
                b = nb()
                kb.mm(b[:, 0:TT], [(ones_bf[:], sq[:, c, :]) for c in range(lo, hi)], [ones_bf, sq], b)
                kb.op("dve", lambda b=b, n=n: V.tensor_scalar(out=rt1[:], in0=b[:, 0:TT], scalar1=float(1.0 / n), scalar2=float(RMS_EPS), op0=ALU.mult, op1=ALU.add),
                      reads=[b], writes=[rt1])
                kb.op("act", lambda: A.activation(out=rt2[:], in_=rt1[:], func=AF.Sqrt), reads=[rt1], writes=[rt2])
                kb.op("dve", lambda j=j: V.reciprocal(out=rstd[:, j, :], in_=rt2[:]), reads=[rt2], writes=[rstd])
                for c in range(lo, hi):
                    kb.op("dve", lambda c=c, j=j: V.tensor_tensor(out=cqn[:, c, :], in0=cq_raw[:, c, :], in1=rstd[:, j, :], op=ALU.mult),
                          reads=[cq_raw, rstd], writes=[cqn])
            if PSTOP == 5:
                continue
            b = fm_block(3968, 64)
            evac_copy(st_ik, st_ik[:], b, b[0:64, 0:TT])
            kb.store("sp", st_ik, o_ik[:, t0:t0 + TT], st_ik[:])
            b1 = fm_block(4032, 32)
            b2 = fm_block(4064, 32)
            kb.op("dve", lambda: V.tensor_tensor(out=rt1[0:32, :], in0=b1[0:32, 0:TT], in1=Cc[0:32, :], op=ALU.mult), reads=[b1, Cc], writes=[rt1])
            kb.op("dve", lambda: V.tensor_tensor(out=rt2[0:32, :], in0=b2[0:32, 0:TT], in1=Ss[0:32, :], op=ALU.mult), reads=[b2, Ss], writes=[rt2])
            kb.op("dve", lambda: V.tensor_tensor(out=st_kr[:], in0=rt1[0:32, :], in1=rt2[0:32, :], op=ALU.add), reads=[rt1, rt2], writes=[st_kr])
            kb.store("sp", st_kr, o_kr[:, t0:t0 + TT], st_kr[:])
            b = fm_block(4096, 8)
            evac_copy(st_iw, st_iw[:], b, b[0:8, 0:TT])
            kb.store("sp", st_iw, o_iw[:, t0:t0 + TT], st_iw[:])
            if PSTOP == 6:
                continue
            for s_ in range(NS):
                b = nb()
                kb.mm(b[:, :], [(hT[:, kc, s_ * 128:(s_ + 1) * 128], wva_s[:, kc, :]) for kc in range(8)], [hT, wva_s], b)
                evac_copy(st_va, st_va[:, s_, :], b, b[:, :])
            kb.store("sp", st_va, o_va[t0:t0 + TT, :].rearrange("(s p) c -> p s c", p=128), st_va[:])
            for blk in range(4):
                b = fm_block(0, 128, kcs=2, lhs=lambda kc, blk=blk: wuq_s[:, kc, blk * 128:(blk + 1) * 128], rhs=lambda kc: cqn[:, kc, :])
                evac_copy(st_qn, st_qn[:, blk, :], b, b[:, 0:TT])
            kb.store("sp", st_qn, o_qn[:, t0:t0 + TT].rearrange("(b p) t -> p b t", p=128), st_qn[:])
            for blk in range(2):
                b1 = fm_block(0, 128, kcs=2, lhs=lambda kc, blk=blk: wuq_s[:, kc, 512 + blk * 128:512 + (blk + 1) * 128], rhs=lambda kc: cqn[:, kc, :])
                b2 = fm_block(0, 128, kcs=2, lhs=lambda kc, blk=blk: wuq_s[:, kc, 768 + blk * 128:768 + (blk + 1) * 128], rhs=lambda kc: cqn[:, kc, :])
                kb.op("dve", lambda b1=b1: V.tensor_tensor(out=rt1[:], in0=b1[:, 0:TT], in1=Cc[:], op=ALU.mult), reads=[b1, Cc], writes=[rt1])
                kb.op("dve", lambda b2=b2: V.tensor_tensor(out=rt2[:], in0=b2[:, 0:TT], in1=Ss[:], op=ALU.mult), reads=[b2, Ss], writes=[rt2])
                kb.op("dve", lambda blk=blk: V.tensor_tensor(out=st_qr[:, blk, :], in0=rt1[:], in1=rt2[:], op=ALU.add), reads=[rt1, rt2], writes=[st_qr])
            kb.store("sp", st_qr, o_qr[:, t0:t0 + TT].rearrange("(b p) t -> p b t", p=128), st_qr[:])
            for blk in range(4):
                b = fm_block(0, 128, kcs=1, lhs=lambda kc, blk=blk: wukv_s[:, blk * 128:(blk + 1) * 128], rhs=lambda kc: cqn[:, 2, :])
                evac_copy(st_kn, st_kn[:, blk, :], b, b[:, 0:TT])
            kb.store("sp", st_kn, o_kn[:, t0:t0 + TT].rearrange("(b p) t -> p b t", p=128), st_kn[:])
            for s_ in range(NS):
                b = nb()
                kb.mm(b[:, :], [(cqn[:, 2, s_ * 128:(s_ + 1) * 128], wukv_s[:, 512:1024])], [cqn, wukv_s], b)
                evac_copy(st_vm, st_vm[:, s_, :], b, b[:, :])
            kb.store("sp", st_vm, o_vm[t0:t0 + TT, :].rearrange("(s p) c -> p s c", p=128), st_vm[:])
        kb.finish()
    return nc


def build_A(S):
    T = S // 4
    NSUB = T // 128
    NSB = T // 512
    KL = S + 1536
    NKT = KL // 128
    NKMAX = 2048 * NSB
    ISC = float(64 ** -0.5 * 8 ** -0.5)
    nc = bass.Bass("TRN2", target_bir_lowering=False)
    iq_d = _dram_in(nc, "iq", [64, NSUB, 8, 128], BF16)
    iw_d = _dram_in(nc, "iw", [T, 8], F32)
    qa_d = _dram_in(nc, "qa", [64, NSUB, 8, 128], BF16)
    qm_d = _dram_in(nc, "qm", [96, NSUB, 8, 128], BF16)
    ik_d = _dram_in(nc, "ik", [64, KL], BF16)
    ka_d = _dram_in(nc, "ka", [64, NKT, 8, 128], BF16)
    va_d = _dram_in(nc, "va", [128, NKT, 8, 65], BF16)
    km_d = _dram_in(nc, "km", [96, NKT, 8, 128], BF16)
    vm_d = _dram_in(nc, "vm", [128, NKT, 8, 65], BF16)
    padb_d = _dram_in(nc, "padb", [128, 1536], F32)
    cb_d = _dram_in(nc, "cb", [128, 4, 512], F32)
    tri_d = _dram_in(nc, "tri4", [128, 512], BF16)
    g0_d = _dram_in(nc, "g0", [128, 1024], F32)
    g1_d = _dram_in(nc, "g1", [128, 1024], F32)
    b31_d = _dram_in(nc, "b31", [128, 8], F32)
    negI_d = _dram_in(nc, "negI", [128, 128], BF16)
    pow2_d = _dram_in(nc, "pow2", [128, NIT], F32)
    ya_d = _dram_out(nc, "ya", [64, NSUB, 8, 128], BF16)
    yb_d = _dram_out(nc, "yb", [64, NSUB, 8, 128], BF16)
    scr = _dram_out(nc, "scr", [NSUB * 2, 1024], F32)

    with ExitStack() as es:
        kb = KB(nc, es)
        V, A, G = nc.vector, nc.scalar, nc.gpsimd
        wbanks = [kb.psum("wb%d" % i) for i in range(4)]
        accs = [kb.psum("acc%d" % i) for i in range(4)]
        wi = [0]

        def wb():
            b = wbanks[wi[0] % 4]
            wi[0] += 1
            b.fresh = True
            return b

        def ld_const(name, shape, dt, src):
            b = kb.sbuf(name, shape, dt)
            kb.load("sp", b, b[:], src)
            return b

        padb = ld_const("padb_s", [128, 1536], F32, padb_d)
        cb = ld_const("cb_s", [128, 4, 512], F32, cb_d)
        tri4 = ld_const("tri4_s", [128, 512], BF16, tri_d)
        g0 = ld_const("g0_s", [128, 1024], F32, g0_d)
        g1 = ld_const("g1_s", [128, 1024], F32, g1_d)
        b31 = ld_const("b31_s", [128, 8], F32, b31_d)
        negI = ld_const("negI_s", [128, 128], BF16, negI_d)
        pow2 = ld_const("pow2_s", [128, NIT], F32, pow2_d)
        nb31 = kb.sbuf("nb31", [128, 8], F32)
        kb.op("dve", lambda: V.tensor_scalar(out=nb31[:], in0=b31[:], scalar1=-1.0, scalar2=None, op0=ALU.mult), reads=[b31], writes=[nb31])
        E0 = kb.sbuf("E0", [128, 1024], BF16)
        E1 = kb.sbuf("E1", [128, 1024], BF16)
        for h in range(8):
            kb.op("act", lambda h=h: A.activation(out=E0[:, h * 128:(h + 1) * 128], in_=g0[:, h * 128:(h + 1) * 128], func=AF.Exp,
                                                  bias=nb31[:, h:h + 1], scale=1.0), reads=[g0, nb31], writes=[E0])
            kb.op("act", lambda h=h: A.activation(out=E1[:, h * 128:(h + 1) * 128], in_=g1[:, h * 128:(h + 1) * 128], func=AF.Exp,
                                                  bias=nb31[:, h:h + 1], scale=1.0), reads=[g1, nb31], writes=[E1])

        Ib = kb.sbuf("I", [128, NKMAX], F32)
        nm = kb.sbuf("nm", [128, NKMAX], BF16)
        ikc = [kb.sbuf("ikc%d" % i, [64, 2048], BF16) for i in range(2)]
        kvK = [kb.sbuf("kvK%d" % i, [96, 4, 8, 128], BF16) for i in range(3)]
        kvV = [kb.sbuf("kvV%d" % i, [128, 4, 8, 65], BF16) for i in range(3)]
        iq_s = kb.sbuf("iq_s", [64, 8, 128], BF16)
        qa_s = kb.sbuf("qa_s", [64, 8, 128], BF16)
        qm_s = kb.sbuf("qm_s", [96, 8, 128], BF16)
        iw_s = kb.sbuf("iw_s", [128, 8], F32)
        aw = kb.sbuf("aw", [128, 8], F32)
        sg = kb.sbuf("sg", [128, 8], F32)
        tmp = [kb.sbuf("tmp%d" % i, [128, 512], F32) for i in range(3)]
        Pb = [kb.sbuf("P%d" % i, [128, 512], BF16) for i in range(3)]
        mn = kb.sbuf("mn", [128, 1], F32)
        mx = kb.sbuf("mx", [128, 1], F32)
        wd = kb.sbuf("wd", [128, 1], F32)
        lo = kb.sbuf("lo", [128, 1], F32)
        mid = kb.sbuf("mid", [128, 1], F32)
        gg = kb.sbuf("gg", [128, 1], F32)
        steps = kb.sbuf("steps", [128, NIT], F32)
        cnt = kb.sbuf("cnt", [128, NIT], F32)
        rsum = [kb.sbuf("rsum%d" % i, [128, 1024], F32) for i in range(2)]
        bcs = [kb.sbuf("bcs%d" % i, [64, 1024], F32) for i in range(2)]
        ybuf = [kb.sbuf("ybuf%d" % i, [64, 1024], BF16) for i in range(2)]
        ctr = dict(t=0, p=0, g=0, kv=0)

        def attend(sb, nkt, q_s, krows, Kd, Vd, scale, masked, accA, accB, br):
            nch = (nkt + 3) // 4
            bufs = {}
            accA.fresh = True
            accB.fresh = True

            def issue(c):
                i = ctr["kv"] % 3
                ctr["kv"] += 1
                n = min(4, nkt - 4 * c)
                kb.load("sp", kvK[i], kvK[i][0:krows, 0:n], Kd[:, 4 * c:4 * c + n])
                kb.load("sp", kvV[i], kvV[i][:, 0:n], Vd[:, 4 * c:4 * c + n])
                bufs[c] = (kvK[i], kvV[i], n)

            for c in range(min(3, nch)):
                issue(c)
            for c in range(nch):
                kbuf, vbuf, n = bufs[c]
                for w in range(n):
                    kt = 4 * c + w
                    for half in range(2):
                        b = wb()
                        rds = [q_s, kbuf] + ([nm, negI] if masked else [])
                        kb.wait("pe", kb.deps(rds, [b]))
                        ins = None
                        for hh in range(4):
                            h = 4 * half + hh
                            ins = kb.mm_raw(b, b[:, hh * 128:(hh + 1) * 128], kbuf[0:krows, w, h, :], q_s[0:krows, h, :])
                            if masked:
                                ins = kb.mm_raw(b, b[:, hh * 128:(hh + 1) * 128], nm[:, kt * 128:(kt + 1) * 128], negI[:])
                        kb.mark_pe(ins, rds, [b])
                        p = Pb[ctr["p"] % 3]
                        ctr["p"] += 1
                        kb.op("act", lambda p=p, b=b: A.activation(out=p[:], in_=b[:], func=AF.Exp, scale=scale), reads=[b], writes=[p])
                        tab = None
                        if masked and kt == nkt - 1:
                            tab = E0
                        elif masked and kt == nkt - 2:
                            tab = E1
                        elif (not masked) and kt == nkt - 1:
                            tab = None
                            kb.op("dve", lambda p=p: V.tensor_tensor(out=p[:], in0=p[:], in1=tri4[:], op=ALU.mult), reads=[tri4], writes=[p])
                        if tab is not None:
                            kb.op("dve", lambda p=p, tab=tab, half=half: V.tensor_tensor(out=p[:], in0=p[:], in1=tab[:, half * 512:(half + 1) * 512], op=ALU.mult),
                                  reads=[tab], writes=[p])
                        acc = accA if half == 0 else accB
                        kb.wait("pe", kb.deps([p, vbuf], [acc]))
                        for hh in range(4):
                            h = 4 * half + hh
                            ins = kb.mm_raw(acc, acc[0:65, hh * 128:(hh + 1) * 128], vbuf[:, w, h, :], p[:, hh * 128:(hh + 1) * 128])
                        kb.mark_pe(ins, [p, vbuf], [acc])
                if c + 3 < nch:
                    issue(c + 3)
            rs, bc, y = rsum[br], bcs[br], ybuf[br]
            kb.op("dve", lambda: V.reciprocal(out=rs[64:65, 0:512], in_=accA[64:65, :]), reads=[accA], writes=[rs])
            kb.op("dve", lambda: V.reciprocal(out=rs[64:65, 512:1024], in_=accB[64:65, :]), reads=[accB], writes=[rs])
            row = sb * 2 + br
            stok = kb.store("sp", rs, scr[row:row + 1, :], rs[64:65, :])
            kb.wait("sp", [stok])
            kb.load("sp", bc, bc[:], scr[row:row + 1, :].partition_broadcast(64))
            kb.op("dve", lambda: V.tensor_tensor(out=y[:, 0:512], in0=accA[0:64, :], in1=bc[:, 0:512], op=ALU.mult), reads=[accA, bc], writes=[y])
            kb.op("dve", lambda: V.tensor_tensor(out=y[:, 512:1024], in0=accB[0:64, :], in1=bc[:, 512:1024], op=ALU.mult), reads=[accB, bc], writes=[y])
            yd = ya_d if br == 0 else yb_d
            kb.store("sp", y, yd[:, sb].rearrange("p h t -> p (h t)"), y[:])

        for sb in range(NSUB):
            j, tq = sb // 4, sb % 4
            Nk = 2048 * (j + 1)
            nkt = 16 * (j + 1) - 3 + tq
            kb.load("sp", iq_s, iq_s[:], iq_d[:, sb])
            kb.load("sp", iw_s, iw_s[:], iw_d[sb * 128:(sb + 1) * 128, :])
            kb.load("sp", qa_s, qa_s[:], qa_d[:, sb])
            kb.load("sp", qm_s, qm_s[:], qm_d[:, sb])
            kb.op("dve", lambda: V.tensor_scalar(out=sg[:], in0=iw_s[:], scalar1=0.0, scalar2=2.0, op0=ALU.is_ge, op1=ALU.mult), reads=[iw_s], writes=[sg])
            kb.op("dve", lambda: V.tensor_scalar(out=sg[:], in0=sg[:], scalar1=-1.0, scalar2=None, op0=ALU.add), writes=[sg])
            kb.op("dve", lambda: V.tensor_tensor(out=aw[:], in0=iw_s[:], in1=sg[:], op=ALU.mult), reads=[iw_s, sg], writes=[aw])
            kb.op("dve", lambda: V.tensor_scalar(out=aw[:], in0=aw[:], scalar1=ISC, scalar2=None, op0=ALU.mult), writes=[aw])
            for g in range(j + 1):
                ikb = ikc[ctr["g"] % 2]
                ctr["g"] += 1
                kb.load("sp", ikb, ikb[:], ik_d[:, g * 2048:(g + 1) * 2048])
                for c4 in range(4):
                    c = g * 4 + c4
                    for h in range(8):
                        b = wb()
                        kb.mm(b[:, :], [(iq_s[:, h, :], ikb[:, c4 * 512:(c4 + 1) * 512])], [iq_s, ikb], b)
                        t = tmp[ctr["t"] % 3]
                        ctr["t"] += 1
                        kb.op("act", lambda t=t, b=b, h=h: A.activation(out=t[:], in_=b[:], func=AF.Relu, scale=aw[:, h:h + 1]), reads=[b, aw], writes=[t])
                        if h == 0:
                            kb.op("dve", lambda t=t, c=c: V.tensor_scalar(out=Ib[:, c * 512:(c + 1) * 512], in0=t[:], scalar1=sg[:, 0:1], scalar2=None, op0=ALU.mult),
                                  reads=[t, sg], writes=[Ib])
                        else:
                            kb.op("dve", lambda t=t, c=c, h=h: V.scalar_tensor_tensor(out=Ib[:, c * 512:(c + 1) * 512], in0=t[:], scalar=sg[:, h:h + 1],
                                                                                     in1=Ib[:, c * 512:(c + 1) * 512], op0=ALU.mult, op1=ALU.add),
                                  reads=[t, sg], writes=[Ib])
            kb.op("dve", lambda: V.tensor_reduce(out=mn[:], in_=Ib[:, 0:Nk], axis=AX.X, op=ALU.min), reads=[Ib], writes=[mn])
            kb.op("dve", lambda: V.tensor_tensor(out=Ib[:, 0:1536], in0=Ib[:, 0:1536], in1=padb[:], op=ALU.add), reads=[padb], writes=[Ib])
            kb.op("dve", lambda: V.tensor_tensor(out=Ib[:, Nk - 512:Nk], in0=Ib[:, Nk - 512:Nk], in1=cb[:, tq, :], op=ALU.add), reads=[cb], writes=[Ib])
            kb.op("dve", lambda: V.tensor_reduce(out=mx[:], in_=Ib[:, 0:Nk], axis=AX.X, op=ALU.max), reads=[Ib], writes=[mx])
            kb.op("dve", lambda: V.tensor_tensor(out=wd[:], in0=mx[:], in1=mn[:], op=ALU.subtract), reads=[mx, mn], writes=[wd])
            kb.op("dve", lambda: V.tensor_scalar(out=steps[:], in0=pow2[:], scalar1=wd[:, 0:1], scalar2=None, op0=ALU.mult), reads=[pow2, wd], writes=[steps])
            kb.op("dve", lambda: V.tensor_copy(out=lo[:], in_=mn[:]), reads=[mn], writes=[lo])
            kb.op("dve", lambda: V.memset(cnt[:], 0.0), writes=[cnt])
            for k in range(NIT):
                kb.op("dve", lambda k=k: V.tensor_tensor(out=mid[:], in0=lo[:], in1=steps[:, k:k + 1], op=ALU.add), reads=[lo, steps], writes=[mid])
                kb.op("dve", lambda k=k: V.tensor_scalar(out=nm[:, 0:Nk], in0=Ib[:, 0:Nk], scalar1=mid[:, 0:1], scalar2=0.0, op0=ALU.is_ge, op1=ALU.add,
                                                         accum_out=cnt[:, k:k + 1]), reads=[Ib, mid], writes=[nm, cnt])
                kb.op("dve", lambda k=k: V.tensor_scalar(out=gg[:], in0=cnt[:, k:k + 1], scalar1=255.5, scalar2=None, op0=ALU.is_gt), reads=[cnt], writes=[gg])
                kb.op("dve", lambda k=k: V.scalar_tensor_tensor(out=lo[:], in0=gg[:], scalar=steps[:, k:k + 1], in1=lo[:], op0=ALU.mult, op1=ALU.add),
                      reads=[gg, steps], writes=[lo])
            kb.op("dve", lambda: V.tensor_scalar(out=nm[:, 0:nkt * 128], in0=Ib[:, 0:nkt * 128], scalar1=lo[:, 0:1], scalar2=None, op0=ALU.is_lt),
                  reads=[Ib, lo], writes=[nm])
            attend(sb, nkt, qm_s, 96, km_d, vm_d, float(96 ** -0.5), False, accs[2], accs[3], 1)
            attend(sb, nkt, qa_s, 64, ka_d, va_d, 0.125, True, accs[0], accs[1], 0)
        kb.finish()
    return nc


def adaln_cols(kb, nc, stg, wst, silc, wada2, bada_s, mb, mod_out):
    for q4 in range(4):
        for half in range(2):
            s = stg[wst[0] % 2]
            wst[0] += 1
            kb.load("sp", s, s[:].rearrange("p (k c) -> p k c", k=4),
                    wada2[half * 512:(half + 1) * 512, q4 * 512:(q4 + 1) * 512].rearrange("(k p) c -> p k c", p=128))
            for dc in range(4):
                j = q4 * 4 + dc
                kb.wait("pe", kb.deps([s, silc], [mb]))
                ins = None
                for k4 in range(4):
                    kc = half * 4 + k4
                    ins = kb.mm_raw(mb, mb[:, j * 16:(j + 1) * 16], s[:, k4 * 512 + dc * 128:k4 * 512 + (dc + 1) * 128], silc[:, kc, 0:16])
                kb.mark_pe(ins, [s, silc], [mb])
    kb.op("dve", lambda: nc.vector.tensor_tensor(out=mod_out[:], in0=mb[:, 0:256].rearrange("p (j r) -> p j r", r=16)[:, :, 0], in1=bada_s[:], op=ALU.add),
          reads=[mb, bada_s], writes=[mod_out])


def adaln_bcast(kb, nc, stg, wst, rep, wg, bg_bc, b0, b1, out_bc):
    for kc in range(8):
        s = stg[wst[0] % 2]
        wst[0] += 1
        kb.load("sp", s, s[:, 0:1024], wg[kc * 128:(kc + 1) * 128, :])
        for half, b in enumerate((b0, b1)):
            kb.wait("pe", kb.deps([s, rep], [b]))
            ins = kb.mm_raw(b, b[:, :], rep[:, kc, :], s[:, half * 512:(half + 1) * 512])
            kb.mark_pe(ins, [s, rep], [b])
    for half, b in enumerate((b0, b1)):
        kb.op("dve", lambda half=half, b=b: nc.vector.tensor_tensor(out=out_bc[:, half * 512:(half + 1) * 512], in0=b[:, :],
                                                                   in1=bg_bc[:, half * 512:(half + 1) * 512], op=ALU.add),
              reads=[b, bg_bc], writes=[out_bc])


def build_F(T):
    TH = T + 128
    NTH = TH // 128
    nc = bass.Bass("TRN2", target_bir_lowering=False)
    x_h = _dram_in(nc, "x_h", [TH, D], F32)
    ya_f = _dram_in(nc, "ya_f", [512, TH], BF16)
    yb_f = _dram_in(nc, "yb_f", [512, TH], BF16)
    sga = _dram_in(nc, "sga", [D, TH], F32)
    sgb = _dram_in(nc, "sgb", [D, TH], F32)
    ccol = _dram_in(nc, "ccol", [128, 8], F32)
    wada_g1 = _dram_in(nc, "wada_g1", [D, D], F32)
    bada_g1 = _dram_in(nc, "bada_g1", [1, D], F32)
    wada_2 = _dram_in(nc, "wada_2", [D, 2048], F32)
    bada_2 = _dram_in(nc, "bada_2", [128, 16], F32)
    wada_g2 = _dram_in(nc, "wada_g2", [D, D], F32)
    bada_g2 = _dram_in(nc, "bada_g2", [1, D], F32)
    wba = _dram_in(nc, "wba", [512, D], F32)
    wbb = _dram_in(nc, "wbb", [512, D], F32)
    wout = _dram_in(nc, "wout", [D, D], F32)
    ln1g = _dram_in(nc, "ln1g", [1, D], F32)
    ln1b = _dram_in(nc, "ln1b", [1, D], F32)
    wup = _dram_in(nc, "wup", [D, 2 * DFF], F32)
    cw = _dram_in(nc, "cw", [128, 44, 3], F32)
    cbias = _dram_in(nc, "cbias", [128, 44], F32)
    wdn = _dram_in(nc, "wdn", [DFF, D], F32)
    ln2g = _dram_in(nc, "ln2g", [1, D], F32)
    ln2b = _dram_in(nc, "ln2b", [1, D], F32)
    flag = _dram_in(nc, "flag", [128, 1], F32)
    identd = _dram_in(nc, "ident", [128, 128], F32)
    onesd = _dram_in(nc, "ones", [128, 128], F32)
    o_x1 = _dram_out(nc, "o_x1", [TH, D], F32)
    o_out = _dram_out(nc, "o_out", [T, D], F32)

    with ExitStack() as es:
        kb = KB(nc, es)
        V, A, G = nc.vector, nc.scalar, nc.gpsimd
        banks = [kb.psum("bank%d" % i) for i in range(8)]
        bi = [0]

        def nb():
            b = banks[bi[0] % 8]
            bi[0] += 1
            b.fresh = True
            return b

        def ld_const(name, shape, dt, src, es_=None):
            b = kb.sbuf(name, shape, dt, es_)
            kb.load("sp", b, b[:], src)
            return b

        ident = ld_const("ident_s", [128, 128], F32, identd)
        ones = ld_const("ones_s", [128, 128], F32, onesd)
        ccol_s = ld_const("ccol_s", [128, 8], F32, ccol)
        flag_s = ld_const("flag_s", [128, 1], F32, flag)
        bada2_s = ld_const("bada2_s", [128, 16], F32, bada_2)
        cw_s = ld_const("cw_s", [128, 44, 3], F32, cw)
        cbias_s = ld_const("cbias_s", [128, 44], F32, cbias)
        eps_s = kb.sbuf("eps_s", [128, 1], F32)
        kb.op("dve", lambda: V.memset(eps_s[:], LN_EPS), writes=[eps_s])
        silc = kb.sbuf("silc", [128, 8], F32)
        kb.op("act", lambda: A.activation(out=silc[:], in_=ccol_s[:], func=AF.Silu), reads=[ccol_s], writes=[silc])
        rep = kb.sbuf("rep", [128, 8, 128], F32)
        for kc in range(8):
            kb.op("dve", lambda kc=kc: V.tensor_scalar(out=rep[:, kc, :], in0=ones[:], scalar1=silc[:, kc:kc + 1], scalar2=None, op0=ALU.mult),
                  reads=[ones, silc], writes=[rep])
        stg = [kb.sbuf("stg%d" % i, [128, 2048], F32) for i in range(2)]
        wst = [0]
        mod2 = kb.sbuf("mod2", [128, 16], F32)
        adaln_cols(kb, nc, stg, wst, rep, wada_2, bada2_s, nb(), mod2)
        sc2p = kb.sbuf("sc2p", [128, 8], F32)
        kb.op("dve", lambda: V.tensor_scalar(out=sc2p[:], in0=mod2[:, 8:16], scalar1=1.0, scalar2=None, op0=ALU.add), reads=[mod2], writes=[sc2p])
        g1bc = kb.sbuf("g1bc", [128, D], F32)
        g2bc = kb.sbuf("g2bc", [128, D], F32)
        btmp = kb.sbuf("btmp", [128, D], F32)
        kb.load("sp", btmp, btmp[:], bada_g1.partition_broadcast(128))
        adaln_bcast(kb, nc, stg, wst, rep, wada_g1, btmp, nb(), nb(), g1bc)
        kb.load("sp", btmp, btmp[:], bada_g2.partition_broadcast(128))
        adaln_bcast(kb, nc, stg, wst, rep, wada_g2, btmp, nb(), nb(), g2bc)

        st = kb.sbuf("st", [128, 2, 6], F32)
        mv = kb.sbuf("mv", [128, 4], F32)
        tmpy = kb.sbuf("tmpy", [128, 512], F32)

        def deepnorm_ln(src_bank_fn, resid, gbc, lng, lnb, r, xn, dst):
            for half in range(2):
                b = src_bank_fn(half)
                sl = slice(half * 512, (half + 1) * 512)
                kb.op("dve", lambda b=b, sl=sl: V.tensor_tensor(out=tmpy[:], in0=b[:, :], in1=gbc[:, sl], op=ALU.mult), reads=[b, gbc], writes=[tmpy])
                kb.op("dve", lambda sl=sl: V.scalar_tensor_tensor(out=r[:, sl], in0=resid[:, sl], scalar=float(ALPHA), in1=tmpy[:], op0=ALU.mult, op1=ALU.add),
                      reads=[resid, tmpy], writes=[r])
                kb.op("dve", lambda half=half, sl=sl: V.bn_stats(out=st[:, half, :], in_=r[:, sl]), reads=[r], writes=[st])
            kb.op("dve", lambda: V.bn_aggr(out=mv[:, 0:2], in_=st[:]), reads=[st], writes=[mv])
            kb.op("act", lambda: A.activation(out=mv[:, 2:3], in_=mv[:, 1:2], func=AF.Sqrt, bias=eps_s[:], scale=1.0), reads=[eps_s], writes=[mv])
            kb.op("dve", lambda: V.reciprocal(out=mv[:, 3:4], in_=mv[:, 2:3]), writes=[mv])
            kb.op("dve", lambda: V.tensor_scalar(out=xn[:], in0=r[:], scalar1=mv[:, 0:1], scalar2=mv[:, 3:4], op0=ALU.subtract, op1=ALU.mult),
                  reads=[r, mv], writes=[xn])
            kb.op("dve", lambda: V.tensor_tensor(out=xn[:], in0=xn[:], in1=lng[:], op=ALU.mult), reads=[lng], writes=[xn])
            kb.op("dve", lambda: V.tensor_tensor(out=dst[:], in0=xn[:], in1=lnb[:], op=ALU.add), reads=[xn, lnb], writes=[dst])

        with ExitStack() as es1:
            wba_s = kb.sbuf("wba_s", [128, 4, D], BF16, es1)
            wbb_s = kb.sbuf("wbb_s", [128, 4, D], BF16, es1)
            wout_s = kb.sbuf("wout_s", [128, 8, D], BF16, es1)
            for kc in range(4):
                load_weight_bf16(kb, stg, wba_s, lambda c0, w, kc=kc: wba_s[:, kc, c0:c0 + w], lambda c0, w, kc=kc: wba[kc * 128:(kc + 1) * 128, c0:c0 + w], 128, D, wst)
                load_weight_bf16(kb, stg, wbb_s, lambda c0, w, kc=kc: wbb_s[:, kc, c0:c0 + w], lambda c0, w, kc=kc: wbb[kc * 128:(kc + 1) * 128, c0:c0 + w], 128, D, wst)
            for kc in range(8):
                load_weight_bf16(kb, stg, wout_s, lambda c0, w, kc=kc: wout_s[:, kc, c0:c0 + w], lambda c0, w, kc=kc: wout[kc * 128:(kc + 1) * 128, c0:c0 + w], 128, D, wst)
            lng = kb.sbuf("ln1g_s", [128, D], F32, es1)
            kb.load("sp", lng, lng[:], ln1g.partition_broadcast(128))
            lnb = kb.sbuf("ln1b_s", [128, D], F32, es1)
            kb.load("sp", lnb, lnb[:], ln1b.partition_broadcast(128))
            ya_t = kb.sbuf("ya_t", [128, 4, 128], BF16, es1)
            yb_t = kb.sbuf("yb_t", [128, 4, 128], BF16, es1)
            sga_t = kb.sbuf("sga_t", [128, 8, 128], F32, es1)
            sgb_t = kb.sbuf("sgb_t", [128, 8, 128], F32, es1)
            x_t = kb.sbuf("x_t", [128, D], F32, es1)
            mT = kb.sbuf("mT", [128, 8, 128], BF16, es1)
            t1 = kb.sbuf("t1", [128, 128], F32, es1)
            t2 = kb.sbuf("t2", [128, 128], F32, es1)
            r1 = kb.sbuf("r1", [128, D], F32, es1)
            xn1 = kb.sbuf("xn1", [128, D], F32, es1)
            x1o = kb.sbuf("x1o", [128, D], F32, es1)
            for i in range(NTH):
                t0 = i * 128
                kb.load("sp", ya_t, ya_t[:], ya_f[:, t0:t0 + 128].rearrange("(k p) t -> p k t", p=128))
                kb.load("sp", yb_t, yb_t[:], yb_f[:, t0:t0 + 128].rearrange("(k p) t -> p k t", p=128))
                kb.load("sp", sga_t, sga_t[:], sga[:, t0:t0 + 128].rearrange("(k p) t -> p k t", p=128))
                kb.load("sp", sgb_t, sgb_t[:], sgb[:, t0:t0 + 128].rearrange("(k p) t -> p k t", p=128))
                kb.load("sp", x_t, x_t[:], x_h[t0:t0 + 128, :])
                for cc in range(8):
                    bA = nb()
                    kb.mm(bA[:, 0:128], [(wba_s[:, k, cc * 128:(cc + 1) * 128], ya_t[:, k, :]) for k in range(4)], [wba_s, ya_t], bA)
                    bB = nb()
                    kb.mm(bB[:, 0:128], [(wbb_s[:, k, cc * 128:(cc + 1) * 128], yb_t[:, k, :]) for k in range(4)], [wbb_s, yb_t], bB)
                    kb.op("dve", lambda bA=bA, cc=cc: V.tensor_tensor(out=t1[:], in0=bA[:, 0:128], in1=sga_t[:, cc, :], op=ALU.mult), reads=[bA, sga_t], writes=[t1])
                    kb.op("dve", lambda bB=bB, cc=cc: V.tensor_tensor(out=t2[:], in0=bB[:, 0:128], in1=sgb_t[:, cc, :], op=ALU.mult), reads=[bB, sgb_t], writes=[t2])
                    kb.op("dve", lambda cc=cc: V.tensor_tensor(out=mT[:, cc, :], in0=t1[:], in1=t2[:], op=ALU.add), reads=[t1, t2], writes=[mT])
                ybanks = []
                for half in range(2):
                    b = nb()
                    kb.mm(b[:, :], [(mT[:, kc, :], wout_s[:, kc, half * 512:(half + 1) * 512]) for kc in range(8)], [mT, wout_s], b)
                    ybanks.append(b)
                deepnorm_ln(lambda half: ybanks[half], x_t, g1bc, lng, lnb, r1, xn1, x1o)
                kb.store("sp", x1o, o_x1[t0:t0 + 128, :], x1o[:])
            kb.barrier()

        wup_s = kb.sbuf("wup_s", [128, 8, 2 * DFF], BF16)
        wdn_s = kb.sbuf("wdn_s", [128, 22, D], BF16)
        for kc in range(8):
            load_weight_bf16(kb, stg, wup_s, lambda c0, w, kc=kc: wup_s[:, kc, c0:c0 + w], lambda c0, w, kc=kc: wup[kc * 128:(kc + 1) * 128, c0:c0 + w], 128, 2 * DFF, wst)
        for fb in range(22):
            load_weight_bf16(kb, stg, wdn_s, lambda c0, w, fb=fb: wdn_s[:, fb, c0:c0 + w], lambda c0, w, fb=fb: wdn[fb * 128:(fb + 1) * 128, c0:c0 + w], 128, D, wst)
        lng2 = kb.sbuf("ln2g_s", [128, D], F32)
        kb.load("sp", lng2, lng2[:], ln2g.partition_broadcast(128))
        lnb2 = kb.sbuf("ln2b_s", [128, D], F32)
        kb.load("sp", lnb2, lnb2[:], ln2b.partition_broadcast(128))
        x1_t = kb.sbuf("x1_t", [128, D], F32)
        h2T = kb.sbuf("h2T", [128, 8, 128], BF16)
        ub = [kb.sbuf("ub%d" % i, [128, 130], F32) for i in range(4)]
        cg = kb.sbuf("cg", [128, 128], F32)
        cv = kb.sbuf("cv", [128, 128], F32)
        sil = kb.sbuf("sil", [128, 128], F32)
        act = kb.sbuf("act", [128, 22, 128], BF16)
        carry = kb.sbuf("carry", [128, 44, 2], F32)
        kb.op("dve", lambda: V.memset(carry[:], 0.0), writes=[carry])
        r2 = kb.sbuf("r2", [128, D], F32)
        xn2 = kb.sbuf("xn2", [128, D], F32)
        oo = kb.sbuf("oo", [128, D], F32)
        ui = [0]
        for i in range(NTH):
            t0 = i * 128
            kb.load("sp", x1_t, x1_t[:], o_x1[t0:t0 + 128, :])
            for kc in range(8):
                b = nb()
                kb.wait("pe", kb.deps([x1_t, ident], [b]))
                ins = nc.tensor.transpose(out=b[:, 0:128], in_=x1_t[:, kc * 128:(kc + 1) * 128], identity=ident[:])
                kb.mark_pe(ins, [x1_t, ident], [b])
                kb.op("act", lambda b=b, kc=kc: A.activation(out=h2T[:, kc, :], in_=b[:, 0:128], func=AF.Identity, bias=mod2[:, kc:kc + 1], scale=sc2p[:, kc:kc + 1]),
                      reads=[b, mod2, sc2p], writes=[h2T])
            for fb in range(22):
                for which, blk in ((0, fb), (1, fb + 22)):
                    b = nb()
                    kb.mm(b[:, 0:128], [(wup_s[:, kc, blk * 128:(blk + 1) * 128], h2T[:, kc, :]) for kc in range(8)], [wup_s, h2T], b)
                    u = ub[ui[0] % 4]
                    ui[0] += 1
                    kb.op("act", lambda b=b, u=u: A.copy(out=u[:, 2:130], in_=b[:, 0:128]), reads=[b], writes=[u])
                    kb.op("dve", lambda u=u, blk=blk: V.tensor_copy(out=u[:, 0:2], in_=carry[:, blk, :]), reads=[carry], writes=[u])
                    c = cg if which == 0 else cv
                    kb.op("dve", lambda u=u, blk=blk, c=c: V.tensor_scalar(out=c[:], in0=u[:, 2:130], scalar1=cw_s[:, blk, 2:3], scalar2=cbias_s[:, blk:blk + 1],
                                                                          op0=ALU.mult, op1=ALU.add), reads=[u, cw_s, cbias_s], writes=[c])
                    kb.op("dve", lambda u=u, blk=blk, c=c: V.scalar_tensor_tensor(out=c[:], in0=u[:, 1:129], scalar=cw_s[:, blk, 1:2], in1=c[:], op0=ALU.mult, op1=ALU.add),
                          reads=[u, cw_s], writes=[c])
                    kb.op("dve", lambda u=u, blk=blk, c=c: V.scalar_tensor_tensor(out=c[:], in0=u[:, 0:128], scalar=cw_s[:, blk, 0:1], in1=c[:], op0=ALU.mult, op1=ALU.add),
                          reads=[u, cw_s], writes=[c])
                    if i == 0:
                        kb.op("dve", lambda u=u, blk=blk: V.tensor_scalar(out=carry[:, blk, :], in0=u[:, 128:130], scalar1=flag_s[:, 0:1], scalar2=None, op0=ALU.mult),
                              reads=[u, flag_s], writes=[carry])
                    else:
                        kb.op("dve", lambda u=u, blk=blk: V.tensor_copy(out=carry[:, blk, :], in_=u[:, 128:130]), reads=[u], writes=[carry])
                kb.op("act", lambda: A.activation(out=sil[:], in_=cg[:], func=AF.Silu), reads=[cg], writes=[sil])
                kb.op("dve", lambda fb=fb: V.tensor_tensor(out=act[:, fb, :], in0=sil[:], in1=cv[:], op=ALU.mult), reads=[sil, cv], writes=[act])
            ybanks = []
            for half in range(2):
                b = nb()
                kb.mm(b[:, :], [(act[:, fb, :], wdn_s[:, fb, half * 512:(half + 1) * 512]) for fb in range(22)], [act, wdn_s], b)
                ybanks.append(b)
            deepnorm_ln(lambda half: ybanks[half], x1_t, g2bc, lng2, lnb2, r2, xn2, oo)
            if i > 0:
                kb.store("sp", oo, o_out[t0 - 128:t0, :], oo[:])
        kb.finish()
    return nc


_PROGS = {}
SPLITS = np.cumsum([0, 512, 512, 512, 512, 64, 8, 256, 128, 32, 1024, 1024])
PERM32 = np.concatenate([np.arange(16, 32), np.arange(0, 16)])


def _prog(kind, arg):
    key = (kind, arg)
    if key not in _PROGS:
        _PROGS[key] = {"P": build_P, "A": build_A, "F": build_F}[kind](arg)
    return _PROGS[key]


def _run(nc, in_maps):
    res = run_bass_kernel_spmd(nc, in_maps, core_ids=list(range(8)))
    return res.results


def _t5_bucket(rel):
    n = np.maximum(rel, 0)
    lr = np.log(np.maximum(n, 1).astype(np.float32) / np.float32(16)) / np.float32(np.log(128 / 16))
    large = 16 + (lr * np.float32(16)).astype(np.int32)
    large = np.minimum(large, 31)
    return np.where(n < 16, n, large)


def _consts():
    p = np.arange(128)
    cst = np.zeros((128, 4), np.float32)
    cst[:, 0] = (np.float32(10000.0) ** (-(p % 16).astype(np.float32) * np.float32(2.0 / 32))).astype(np.float32)
    cst[:, 1] = np.where((p % 32) < 16, -1.0, 1.0)
    cst[:, 2] = RMS_EPS
    return cst, np.eye(128, dtype=np.float32), np.ones((128, 128), np.float32)


def run_P(x, c, positions, w_ada, b_ada, w_in, q_norm_g, w_uq, kv_norm_g, w_ukv, S):
    T = S // 4
    cols = [w_in[:, SPLITS[i]:SPLITS[i + 1]] for i in range(11)]
    qa, ka, va, iq, ik, iw, cq, ckv, kr, ga, gb = cols
    w1 = np.ascontiguousarray(np.concatenate([qa, ka, iq, ga, gb, cq, ckv, ik, kr, kr[:, PERM32], iw], axis=1))
    wq = w_uq.reshape(256, 8, 96)
    wuq = np.ascontiguousarray(np.concatenate([wq[:, :, :64].reshape(256, 512), wq[:, :, 64:].reshape(256, 256),
                                               wq[:, :, 64:][:, :, PERM32].reshape(256, 256)], axis=1))
    wkv = w_ukv.reshape(128, 8, 128)
    wukv = np.ascontiguousarray(np.concatenate([wkv[:, :, :64].reshape(128, 512), wkv[:, :, 64:].reshape(128, 512)], axis=1))
    cst, ident, ones = _consts()
    maps = []
    for core in range(8):
        b, cc = divmod(core, 4)
        maps.append(dict(
            x=np.ascontiguousarray(x[b, cc * T:(cc + 1) * T]), ccol=np.ascontiguousarray(c[b].reshape(8, 128).T),
            wada=np.ascontiguousarray(w_ada[:, 0:2048]), bada=np.ascontiguousarray(b_ada[0:2048].reshape(16, 128).T),
            pos=np.ascontiguousarray(positions[b, cc * T:(cc + 1) * T].reshape(1, T)), cst=cst, ident=ident, ones=ones,
            w1=w1, wva=np.ascontiguousarray(va), wuq=wuq, qg=np.ascontiguousarray(q_norm_g.reshape(2, 128).T),
            wukv=wukv, kvg=np.ascontiguousarray(kv_norm_g.reshape(128, 1))))
    r = _run(_prog("P", T), maps)
    out = {}
    for name in ("o_qa", "o_ka", "o_iq", "o_ga", "o_gb", "o_ik", "o_kr", "o_iw", "o_qn", "o_qr", "o_kn"):
        out[name] = [np.concatenate([r[4 * b + cc][name] for cc in range(4)], axis=1) for b in range(2)]
    for name in ("o_va", "o_vm"):
        out[name] = [np.concatenate([r[4 * b + cc][name] for cc in range(4)], axis=0) for b in range(2)]
    return out


def _qtok(cc, S):
    NSB = S // 2048
    return np.concatenate([np.arange((4 * j + cc) * 512, (4 * j + cc + 1) * 512) for j in range(NSB)])


def run_A(P, rel_bias, S):
    T = S // 4
    NSUB = T // 128
    KL = S + 1536
    NKT = KL // 128
    bf = NPBF
    s_i = np.arange(128)[:, None]
    t_i = np.arange(128)[None, :]
    tri = (s_i <= t_i).astype(np.float32)
    tri4 = np.ascontiguousarray(np.tile(tri, (1, 4)).astype(bf))
    bk0 = _t5_bucket(t_i - s_i)
    bk1 = _t5_bucket(128 + t_i - s_i)
    g0 = np.ascontiguousarray(rel_bias[bk0].transpose(0, 2, 1).reshape(128, 1024))
    g1 = np.ascontiguousarray(rel_bias[bk1].transpose(0, 2, 1).reshape(128, 1024))
    b31 = np.ascontiguousarray(np.broadcast_to(rel_bias[31][None, :], (128, 8)))
    negI = (np.eye(128, dtype=np.float32) * NEG).astype(bf)
    pow2 = np.ascontiguousarray(np.broadcast_to((2.0 ** -(np.arange(NIT) + 1.0)).astype(np.float32)[None, :], (128, NIT)))
    cb = np.zeros((128, 4, 512), np.float32)
    sl = np.arange(512)[None, :]
    for tq in range(4):
        cb[:, tq, :] = np.where(sl <= tq * 128 + np.arange(128)[:, None], 0.0, NEG)
    maps = []
    for core in range(8):
        b, cc = divmod(core, 4)
        qt = _qtok(cc, S)
        pl, pr = (3 - cc) * 512, cc * 512

        def qside(arr, rows):
            a = arr.reshape(8, rows, S)[:, :, qt].reshape(8, rows, NSUB, 128)
            return np.ascontiguousarray(a.transpose(1, 2, 0, 3))

        def kside(arr, rows):
            a = np.pad(arr, ((0, 0), (0, 0), (pl, pr))).reshape(8, rows, NKT, 128)
            return np.ascontiguousarray(a.transpose(1, 2, 0, 3))

        def vside(arr):
            a = np.concatenate([arr.reshape(S, 8, 64), np.ones((S, 8, 1), arr.dtype)], axis=2)
            a = np.pad(a, ((pl, pr), (0, 0), (0, 0))).reshape(NKT, 128, 8, 65)
            return np.ascontiguousarray(a.transpose(1, 0, 2, 3))

        qn = P["o_qn"][b].reshape(8, 64, S)
        qr = P["o_qr"][b].reshape(8, 32, S)
        qm = np.concatenate([qn, qr], axis=1).reshape(8 * 96, S)
        kn = P["o_kn"][b].reshape(8, 64, S)
        krr = np.broadcast_to(P["o_kr"][b][None], (8, 32, S))
        km = np.concatenate([kn, krr], axis=1)
        padb = np.zeros((128, 1536), np.float32)
        padb[:, :pl] = NEG
        maps.append(dict(
            iq=qside(P["o_iq"][b], 64), iw=np.ascontiguousarray(P["o_iw"][b][:, qt].T), qa=qside(P["o_qa"][b], 64), qm=qside(qm, 96),
            ik=np.ascontiguousarray(np.pad(P["o_ik"][b], ((0, 0), (pl, pr)))), ka=kside(P["o_ka"][b].reshape(8, 64, S), 64),
            va=vside(P["o_va"][b]), km=kside(km, 96), vm=vside(P["o_vm"][b]),
            padb=padb, cb=cb, tri4=tri4, g0=g0, g1=g1, b31=b31, negI=negI, pow2=pow2))
    r = _run(_prog("A", S), maps)
    ya = [np.zeros((512, S), bf) for _ in range(2)]
    yb = [np.zeros((512, S), bf) for _ in range(2)]
    for core in range(8):
        b, cc = divmod(core, 4)
        qt = _qtok(cc, S)
        for name, dst in (("ya", ya), ("yb", yb)):
            a = r[core][name]
            dst[b][:, qt] = a.transpose(2, 0, 1, 3).reshape(512, T)
    return ya, yb


def run_F(x, c, P, ya, yb, w_ada, b_ada, w_branch_a, w_branch_b, w_out, ln1_g, ln1_b, w_up, conv_w, conv_b, w_down, ln2_g, ln2_b, S):
    T = S // 4
    _, ident, ones = _consts()
    maps = []
    for core in range(8):
        b, cc = divmod(core, 4)
        lo, hi = cc * T, (cc + 1) * T

        def halo_cols(a):
            h = a[:, lo - 128:lo] if cc > 0 else np.zeros((a.shape[0], 128), a.dtype)
            return np.ascontiguousarray(np.concatenate([h, a[:, lo:hi]], axis=1))

        xh = np.concatenate([x[b, lo - 128:lo] if cc > 0 else np.zeros((128, D), np.float32), x[b, lo:hi]], axis=0)
        maps.append(dict(
            x_h=np.ascontiguousarray(xh), ya_f=halo_cols(ya[b]), yb_f=halo_cols(yb[b]), sga=halo_cols(P["o_ga"][b]), sgb=halo_cols(P["o_gb"][b]),
            ccol=np.ascontiguousarray(c[b].reshape(8, 128).T),
            wada_g1=np.ascontiguousarray(w_ada[:, 2048:3072]), bada_g1=np.ascontiguousarray(b_ada[2048:3072].reshape(1, D)),
            wada_2=np.ascontiguousarray(w_ada[:, 3072:5120]), bada_2=np.ascontiguousarray(b_ada[3072:5120].reshape(16, 128).T),
            wada_g2=np.ascontiguousarray(w_ada[:, 5120:6144]), bada_g2=np.ascontiguousarray(b_ada[5120:6144].reshape(1, D)),
            wba=w_branch_a, wbb=w_branch_b, wout=w_out, ln1g=ln1_g.reshape(1, D), ln1b=ln1_b.reshape(1, D),
            wup=w_up, cw=np.ascontiguousarray(conv_w.T.reshape(44, 128, 3).transpose(1, 0, 2)),
            cbias=np.ascontiguousarray(conv_b.reshape(44, 128).T), wdn=w_down, ln2g=ln2_g.reshape(1, D), ln2b=ln2_b.reshape(1, D),
            flag=np.full((128, 1), 0.0 if cc == 0 else 1.0, np.float32), ident=ident, ones=ones))
    r = _run(_prog("F", T), maps)
    out = np.zeros((2, S, D), np.float32)
    for core in range(8):
        b, cc = divmod(core, 4)
        out[b, cc * T:(cc + 1) * T] = r[core]["o_out"]
    return out


def kernel(x, c, positions, rel_bias, w_ada, b_ada, w_in, q_norm_g, w_uq, kv_norm_g, w_ukv,
           w_branch_a, w_branch_b, w_out, ln1_g, ln1_b, w_up, conv_w, conv_b, w_down, ln2_g, ln2_b):
    f = lambda a: np.ascontiguousarray(np.asarray(a))
    x, c, positions, rel_bias = f(x), f(c), f(positions), f(rel_bias)
    S = x.shape[1]
    depth = np.asarray(w_in).shape[0]
    for l in range(depth):
        g = lambda a: f(np.asarray(a)[l])
        P = run_P(x, c, positions, g(w_ada), g(b_ada), g(w_in), g(q_norm_g), g(w_uq), g(kv_norm_g), g(w_ukv), S)
        ya, yb = run_A(P, rel_bias, S)
        x = run_F(x, c, P, ya, yb, g(w_ada), g(b_ada), g(w_branch_a), g(w_branch_b), g(w_out), g(ln1_g), g(ln1_b),
                  g(w_up), g(conv_w), g(conv_b), g(w_down), g(ln2_g), g(ln2_b), S)
    return x.astype(np.float32)
```

```python
import numpy as np
import ml_dtypes
from contextlib import ExitStack
import concourse.bass as bass
import concourse.mybir as mybir
from concourse.bass_utils import run_bass_kernel_spmd

F32 = mybir.dt.float32
BF16 = mybir.dt.bfloat16
I32 = mybir.dt.int32
ALU = mybir.AluOpType
AF = mybir.ActivationFunctionType
AX = mybir.AxisListType
NPBF = ml_dtypes.bfloat16

D = 1024
NHEAD = 8
DFF = 2816
NEG = -30000.0
NIT = 22
LN_EPS = 1e-5
RMS_EPS = 1e-6
ALPHA = 4 ** 0.25
TWO_PI = float(2 * np.pi)
MAGIC = 12582912.0


class Buf:
    def __init__(self, t, name):
        self.t = t
        self.name = name
        self.w = None
        self.r = []
        self.dsem = None
        self.ssem = None
        self.fresh = True

    def __getitem__(self, idx):
        return self.t[idx]


class KB:
    def __init__(self, nc, es):
        self.nc = nc
        self.es = es
        self.E = dict(pe=nc.tensor, act=nc.scalar, dve=nc.vector, pool=nc.gpsimd, sp=nc.sync)
        self.sem = {k: es.enter_context(nc.semaphore("s_" + k)) for k in self.E}
        self.cnt = {k: 0 for k in self.E}
        self.waited = {}
        self.nsem = 0
        self.stores = {}

    def sbuf(self, name, shape, dt, es=None):
        return Buf((es or self.es).enter_context(self.nc.sbuf_tensor(name, shape, dt)), name)

    def psum(self, name):
        return Buf(self.es.enter_context(self.nc.psum_tensor(name, [128, 512], F32)), name)

    def newsem(self, name):
        self.nsem += 1
        key = "%s_%d" % (name, self.nsem)
        return [self.es.enter_context(self.nc.semaphore(key)), 0, key]

    def wait(self, eng, toks):
        best = {}
        for t in toks:
            if t is None:
                continue
            if t[2] not in best or best[t[2]][1] < t[1]:
                best[t[2]] = t
        for t in best.values():
            sem, val, key = t
            if eng == "pe" and key == "pe":
                continue
            if self.waited.get((eng, key), 0) >= val:
                continue
            self.E[eng].wait_ge(sem, val)
            self.waited[(eng, key)] = val

    def deps(self, reads, writes):
        d = []
        for b in reads:
            d.append(b.w)
        for b in writes:
            d.append(b.w)
            d.extend(b.r)
        return d

    def commit(self, tok, reads, writes):
        for b in reads:
            b.r.append(tok)
        for b in writes:
            b.w = tok
            b.r = []

    def op(self, eng, fn, reads=(), writes=()):
        self.wait(eng, self.deps(reads, writes))
        ins = fn()
        self.cnt[eng] += 1
        ins.then_inc(self.sem[eng], 1)
        tok = (self.sem[eng], self.cnt[eng], eng)
        self.commit(tok, reads, writes)
        return tok

    def mm(self, out_ap, pairs, reads, ps):
        self.wait("pe", self.deps(reads, [ps]))
        n = len(pairs)
        ins = None
        for i, (l, r) in enumerate(pairs):
            ins = self.mm_raw(ps, out_ap, l, r)
        self.cnt["pe"] += 1
        ins.then_inc(self.sem["pe"], 1)
        tok = (self.sem["pe"], self.cnt["pe"], "pe")
        self.commit(tok, reads, [ps])
        return tok

    def mm_raw(self, ps, out_ap, l, r):
        st = ps.fresh
        ps.fresh = False
        return self.nc.tensor.matmul(out_ap, lhsT=l, rhs=r, start=st, stop=True, skip_group_check=True)

    def mark_pe(self, ins, reads, writes):
        self.cnt["pe"] += 1
        ins.then_inc(self.sem["pe"], 1)
        tok = (self.sem["pe"], self.cnt["pe"], "pe")
        self.commit(tok, reads, writes)
        return tok

    def load(self, q, buf, out_ap, in_ap, extra_reads=()):
        self.wait(q, self.deps(extra_reads, [buf]))
        if buf.dsem is None:
            buf.dsem = self.newsem("d")
        self.E[q].dma_start(out=out_ap, in_=in_ap).then_inc(buf.dsem[0], 16)
        buf.dsem[1] += 16
        tok = (buf.dsem[0], buf.dsem[1], buf.dsem[2])
        self.commit(tok, extra_reads, [buf])
        return tok

    def load_more(self, q, buf, out_ap, in_ap):
        self.E[q].dma_start(out=out_ap, in_=in_ap).then_inc(buf.dsem[0], 16)
        buf.dsem[1] += 16
        tok = (buf.dsem[0], buf.dsem[1], buf.dsem[2])
        buf.w = tok
        return tok

    def store(self, q, buf, out_ap, in_ap):
        self.wait(q, [buf.w])
        if buf.ssem is None:
            buf.ssem = self.newsem("s")
        self.E[q].dma_start(out=out_ap, in_=in_ap).then_inc(buf.ssem[0], 16)
        buf.ssem[1] += 16
        tok = (buf.ssem[0], buf.ssem[1], buf.ssem[2])
        buf.r.append(tok)
        self.stores[buf.ssem[2]] = tok
        return tok

    def finish(self):
        self.wait("sp", list(self.stores.values()))

    def barrier(self):
        toks = [(self.sem[k], self.cnt[k], k) for k in self.E if self.cnt[k] > 0]
        toks += list(self.stores.values())
        for k in self.E:
            self.wait(k, toks)


def _dram_in(nc, name, shape, dt):
    return nc.dram_tensor(name, list(shape), dt, kind="ExternalInput").ap()


def _dram_out(nc, name, shape, dt):
    return nc.dram_tensor(name, list(shape), dt, kind="ExternalOutput").ap()


def load_weight_bf16(kb, stg, dst_buf, dst_ap_fn, src_ap_fn, nrows, ncols, state):
    c0 = 0
    while c0 < ncols:
        w = min(2048, ncols - c0)
        s = stg[state[0] % 2]
        state[0] += 1
        kb.load("sp", s, s[0:nrows, 0:w], src_ap_fn(c0, w))
        eng = "dve"
        e = kb.E[eng]
        kb.op(eng, lambda e=e, s=s, c0=c0, w=w: e.tensor_copy(out=dst_ap_fn(c0, w), in_=s[0:nrows, 0:w]),
              reads=[s], writes=[dst_buf])
        c0 += w


NC1 = 512 * 3 + 1024 * 2 + 256 + 128 + 64 + 32 + 32 + 8
P_TT = 256


def build_P(T):
    nc = bass.Bass("TRN2", target_bir_lowering=False)
    TT = P_TT
    NT = T // TT
    NS = TT // 128
    x = _dram_in(nc, "x", [T, D], F32)
    ccol = _dram_in(nc, "ccol", [128, 8], F32)
    wada = _dram_in(nc, "wada", [D, 2048], F32)
    bada = _dram_in(nc, "bada", [128, 16], F32)
    pos = _dram_in(nc, "pos", [1, T], I32)
    cst = _dram_in(nc, "cst", [128, 4], F32)
    identd = _dram_in(nc, "ident", [128, 128], F32)
    onesd = _dram_in(nc, "ones", [128, 128], F32)
    w1 = _dram_in(nc, "w1", [D, NC1], F32)
    wva = _dram_in(nc, "wva", [D, 512], F32)
    wuq = _dram_in(nc, "wuq", [256, 1024], F32)
    qg = _dram_in(nc, "qg", [128, 2], F32)
    wukv = _dram_in(nc, "wukv", [128, 1024], F32)
    kvg = _dram_in(nc, "kvg", [128, 1], F32)
    o_qa = _dram_out(nc, "o_qa", [512, T], BF16)
    o_ka = _dram_out(nc, "o_ka", [512, T], BF16)
    o_iq = _dram_out(nc, "o_iq", [512, T], BF16)
    o_ga = _dram_out(nc, "o_ga", [1024, T], F32)
    o_gb = _dram_out(nc, "o_gb", [1024, T], F32)
    o_ik = _dram_out(nc, "o_ik", [64, T], BF16)
    o_kr = _dram_out(nc, "o_kr", [32, T], BF16)
    o_iw = _dram_out(nc, "o_iw", [8, T], F32)
    o_qn = _dram_out(nc, "o_qn", [512, T], BF16)
    o_qr = _dram_out(nc, "o_qr", [256, T], BF16)
    o_kn = _dram_out(nc, "o_kn", [512, T], BF16)
    o_va = _dram_out(nc, "o_va", [T, 512], BF16)
    o_vm = _dram_out(nc, "o_vm", [T, 512], BF16)

    with ExitStack() as es:
        kb = KB(nc, es)
        V, A, G = nc.vector, nc.scalar, nc.gpsimd
        banks = [kb.psum("bank%d" % i) for i in range(8)]
        bi = [0]

        def nb():
            b = banks[bi[0] % 8]
            bi[0] += 1
            b.fresh = True
            return b

        cst_s = kb.sbuf("cst_s", [128, 4], F32)
        kb.load("sp", cst_s, cst_s[:], cst)
        ident = kb.sbuf("ident_s", [128, 128], F32)
        kb.load("sp", ident, ident[:], identd)
        ones = kb.sbuf("ones_s", [128, 128], F32)
        kb.load("sp", ones, ones[:], onesd)
        qg_s = kb.sbuf("qg_s", [128, 2], F32)
        kb.load("sp", qg_s, qg_s[:], qg)
        kvg_s = kb.sbuf("kvg_s", [128, 1], F32)
        kb.load("sp", kvg_s, kvg_s[:], kvg)
        bada_s = kb.sbuf("bada_s", [128, 16], F32)
        kb.load("sp", bada_s, bada_s[:], bada)
        ccol_s = kb.sbuf("ccol_s", [128, 8], F32)
        kb.load("sp", ccol_s, ccol_s[:], ccol)
        silc = kb.sbuf("silc", [128, 8], F32)
        kb.op("act", lambda: A.activation(out=silc[:], in_=ccol_s[:], func=AF.Silu), reads=[ccol_s], writes=[silc])

        silc16 = kb.sbuf("silc16", [128, 8, 16], F32)
        for kc in range(8):
            kb.op("dve", lambda kc=kc: V.tensor_scalar(out=silc16[:, kc, :], in0=ones[:, 0:16], scalar1=silc[:, kc:kc + 1], scalar2=None, op0=ALU.mult),
                  reads=[ones, silc], writes=[silc16])
        stg = [kb.sbuf("stg%d" % i, [128, 2048], F32) for i in range(2)]
        wst = [0]
        mod1 = kb.sbuf("mod1", [128, 16], F32)
        mb = nb()
        for q4 in range(4):
            for half in range(2):
                s = stg[wst[0] % 2]
                wst[0] += 1
                kb.load("sp", s, s[:].rearrange("p (k c) -> p k c", k=4),
                        wada[half * 512:(half + 1) * 512, q4 * 512:(q4 + 1) * 512].rearrange("(k p) c -> p k c", p=128))
                for dc in range(4):
                    j = q4 * 4 + dc
                    kb.wait("pe", kb.deps([s, silc16], [mb]))
                    for k4 in range(4):
                        kc = half * 4 + k4
                        ins = kb.mm_raw(mb, mb[:, j * 16:(j + 1) * 16], s[:, k4 * 512 + dc * 128:k4 * 512 + (dc + 1) * 128], silc16[:, kc, :])
                    kb.mark_pe(ins, [s, silc16], [mb])
        kb.op("dve", lambda: V.tensor_tensor(out=mod1[:], in0=mb[:, 0:256].rearrange("p (j r) -> p j r", r=16)[:, :, 0], in1=bada_s[:], op=ALU.add),
              reads=[mb, bada_s], writes=[mod1])
        sc1p = kb.sbuf("sc1p", [128, 8], F32)
        kb.op("dve", lambda: V.tensor_scalar(out=sc1p[:], in0=mod1[:, 8:16], scalar1=1.0, scalar2=None, op0=ALU.add),
              reads=[mod1], writes=[sc1p])

        w1_s = kb.sbuf("w1_s", [128, 8, NC1], BF16)
        for kc in range(8):
            load_weight_bf16(kb, stg, w1_s, lambda c0, w, kc=kc: w1_s[:, kc, c0:c0 + w],
                             lambda c0, w, kc=kc: w1[kc * 128:(kc + 1) * 128, c0:c0 + w], 128, NC1, wst)
        wva_s = kb.sbuf("wva_s", [128, 8, 512], BF16)
        for kc in range(8):
            load_weight_bf16(kb, stg, wva_s, lambda c0, w, kc=kc: wva_s[:, kc, c0:c0 + w],
                             lambda c0, w, kc=kc: wva[kc * 128:(kc + 1) * 128, c0:c0 + w], 128, 512, wst)
        wuq_s = kb.sbuf("wuq_s", [128, 2, 1024], BF16)
        for kc in range(2):
            s = stg[wst[0] % 2]
            wst[0] += 1
            kb.load("sp", s, s[:, 0:1024], wuq[kc * 128:(kc + 1) * 128, :])
            kb.op("dve", lambda s=s, kc=kc: V.tensor_scalar(out=wuq_s[:, kc, :], in0=s[:, 0:1024], scalar1=qg_s[:, kc:kc + 1],
                                                            scalar2=None, op0=ALU.mult), reads=[s, qg_s], writes=[wuq_s])
        wukv_s = kb.sbuf("wukv_s", [128, 1024], BF16)
        s = stg[wst[0] % 2]
        wst[0] += 1
        kb.load("sp", s, s[:, 0:1024], wukv)
        kb.op("dve", lambda s=s: V.tensor_scalar(out=wukv_s[:], in0=s[:, 0:1024], scalar1=kvg_s[:, 0:1], scalar2=None, op0=ALU.mult),
              reads=[s, kvg_s], writes=[wukv_s])

        xt = kb.sbuf("xt", [128, NS, D], F32)
        hT = kb.sbuf("hT", [128, 8, TT], BF16)
        st_bf = kb.sbuf("st_bf", [128, 12, TT], BF16)
        st_g = [kb.sbuf("st_g%d" % i, [128, 8, TT], F32) for i in range(2)]
        st_ik = kb.sbuf("st_ik", [64, TT], BF16)
        st_kr = kb.sbuf("st_kr", [32, TT], BF16)
        st_iw = kb.sbuf("st_iw", [8, TT], F32)
        st_qn = kb.sbuf("st_qn", [128, 4, TT], BF16)
        st_qr = kb.sbuf("st_qr", [128, 2, TT], BF16)
        st_kn = kb.sbuf("st_kn", [128, 4, TT], BF16)
        st_va = kb.sbuf("st_va", [128, NS, 512], BF16)
        st_vm = kb.sbuf("st_vm", [128, NS, 512], BF16)
        posi = kb.sbuf("posi", [128, TT], I32)
        ang = kb.sbuf("ang", [128, TT], F32)
        kk = kb.sbuf("kk", [128, TT], F32)
        aab = kb.sbuf("aab", [128, TT], F32)
        Cc = kb.sbuf("Cc", [128, TT], F32)
        Ss = kb.sbuf("Ss", [128, TT], F32)
        cq_raw = kb.sbuf("cq_raw", [128, 3, TT], F32)
        sq = kb.sbuf("sq", [128, 3, TT], BF16)
        rstd = kb.sbuf("rstd", [128, 2, TT], F32)
        cqn = kb.sbuf("cqn", [128, 3, TT], BF16)
        rt1 = kb.sbuf("rt1", [128, TT], F32)
        rt2 = kb.sbuf("rt2", [128, TT], F32)
        ones_bf = kb.sbuf("ones_bf", [128, 128], BF16)
        kb.op("dve", lambda: V.tensor_copy(out=ones_bf[:], in_=ones[:]), reads=[ones], writes=[ones_bf])
        half_pi = kb.sbuf("half_pi", [128, 1], F32)
        kb.op("dve", lambda: V.memset(half_pi[:], float(np.pi / 2)), writes=[half_pi])
        evi = [0]

        def evac_copy(dst_buf, dst_ap, src_buf, src_ap):
            evi[0] += 1
            if evi[0] % 2 == 0:
                return kb.op("act", lambda: A.copy(out=dst_ap, in_=src_ap), reads=[src_buf], writes=[dst_buf])
            return kb.op("dve", lambda: V.tensor_copy(out=dst_ap, in_=src_ap), reads=[src_buf], writes=[dst_buf])

        import os
        PSTOP = int(os.environ.get("P_STOP", "99"))
        for it in range(NT if PSTOP > 1 else 0):
            t0 = it * TT
            kb.load("sp", xt, xt[:], x[t0:t0 + TT, :].rearrange("(s p) d -> p s d", p=128))
            kb.load("sp", posi, posi[:], pos[0:1, t0:t0 + TT].partition_broadcast(128))
            for kc in range(8):
                b = nb()
                kb.wait("pe", kb.deps([xt, ident], [b]))
                for s_ in range(NS):
                    ins = nc.tensor.transpose(out=b[:, s_ * 128:(s_ + 1) * 128], in_=xt[:, s_, kc * 128:(kc + 1) * 128], identity=ident[:])
                kb.mark_pe(ins, [xt, ident], [b])
                kb.op("act", lambda b=b, kc=kc: A.activation(out=hT[:, kc, :], in_=b[:, 0:TT], func=AF.Identity,
                                                             bias=mod1[:, kc:kc + 1], scale=sc1p[:, kc:kc + 1]),
                      reads=[b, mod1, sc1p], writes=[hT])
            kb.op("dve", lambda: V.tensor_copy(out=ang[:], in_=posi[:]), reads=[posi], writes=[ang])
            kb.op("dve", lambda: V.tensor_scalar(out=ang[:], in0=ang[:], scalar1=cst_s[:, 0:1], scalar2=None, op0=ALU.mult),
                  reads=[cst_s], writes=[ang])
            kb.op("dve", lambda: V.tensor_scalar(out=kk[:], in0=ang[:], scalar1=float(1.0 / TWO_PI), scalar2=MAGIC, op0=ALU.mult, op1=ALU.add),
                  reads=[ang], writes=[kk])
            kb.op("dve", lambda: V.tensor_scalar(out=kk[:], in0=kk[:], scalar1=-MAGIC, scalar2=None, op0=ALU.add), writes=[kk])
            kb.op("dve", lambda: V.scalar_tensor_tensor(out=ang[:], in0=kk[:], scalar=-6.28125, in1=ang[:], op0=ALU.mult, op1=ALU.add),
                  reads=[kk], writes=[ang])
            kb.op("dve", lambda: V.scalar_tensor_tensor(out=ang[:], in0=kk[:], scalar=-0.0019353071795864769, in1=ang[:], op0=ALU.mult, op1=ALU.add),
                  reads=[kk], writes=[ang])
            kb.op("dve", lambda: V.tensor_scalar(out=ang[:], in0=ang[:], scalar1=float(np.pi), scalar2=float(-np.pi), op0=ALU.min, op1=ALU.max),
                  writes=[ang])
            kb.op("dve", lambda: V.scalar_tensor_tensor(out=aab[:], in0=ang[:], scalar=-1.0, in1=ang[:], op0=ALU.mult, op1=ALU.max),
                  reads=[ang], writes=[aab])
            kb.op("act", lambda: A.activation(out=Ss[:], in_=ang[:], func=AF.Sin, scale=cst_s[:, 1:2]), reads=[ang, cst_s], writes=[Ss])
            kb.op("act", lambda: A.activation(out=Cc[:], in_=aab[:], func=AF.Sin, scale=-1.0, bias=half_pi[:]), reads=[aab, half_pi], writes=[Cc])

            if PSTOP == 2:
                continue

            def fm_block(c0, m, kcs=8, lhs=None, rhs=None):
                b = nb()
                lhs = lhs or (lambda kc: w1_s[:, kc, c0:c0 + m])
                rhs = rhs or (lambda kc: hT[:, kc, :])
                kb.mm(b[0:m, 0:TT], [(lhs(kc), rhs(kc)) for kc in range(kcs)], [hT, w1_s, wuq_s, wukv_s, cqn], b)
                return b

            for blk in range(12):
                b = fm_block(blk * 128, 128)
                evac_copy(st_bf, st_bf[:, blk, :], b, b[:, 0:TT])
            kb.store("sp", st_bf, o_qa[:, t0:t0 + TT].rearrange("(b p) t -> p b t", p=128), st_bf[:, 0:4, :])
            kb.store("sp", st_bf, o_ka[:, t0:t0 + TT].rearrange("(b p) t -> p b t", p=128), st_bf[:, 4:8, :])
            kb.store("sp", st_bf, o_iq[:, t0:t0 + TT].rearrange("(b p) t -> p b t", p=128), st_bf[:, 8:12, :])
            if PSTOP == 3:
                continue
            for gi, og in enumerate((o_ga, o_gb)):
                sg = st_g[gi]
                for blk in range(8):
                    b = fm_block(1536 + gi * 1024 + blk * 128, 128)
                    kb.op("act", lambda b=b, blk=blk, sg=sg: A.activation(out=sg[:, blk, :], in_=b[:, 0:TT], func=AF.Sigmoid),
                          reads=[b], writes=[sg])
                kb.store("sp", sg, og[:, t0:t0 + TT].rearrange("(b p) t -> p b t", p=128), sg[:])
            for j in range(3):
                b = fm_block(3584 + j * 128, 128)
                kb.op("dve", lambda b=b, j=j: V.tensor_copy(out=cq_raw[:, j, :], in_=b[:, 0:TT]), reads=[b], writes=[cq_raw])
                kb.op("dve", lambda j=j: V.tensor_tensor(out=sq[:, j, :], in0=cq_raw[:, j, :], in1=cq_raw[:, j, :], op=ALU.mult), reads=[cq_raw], writes=[sq])
            for j, (lo, hi, n) in enumerate(((0, 2, 256.0), (2, 3, 128.0))):
                b = nb()
                kb.mm(b[:, 0:TT], [(ones_bf[:], sq[:, c, :]) for c in range(lo, hi)], [ones_bf, sq], b)
                kb.op("dve", lambda b=b, n=n: V.tensor_scalar(out=rt1[:], in0=b[:, 0:TT], scalar1=float(1.0 / n), scalar2=float(RMS_EPS), op0=ALU.mult, op1=ALU.add),
                      reads=[b], writes=[rt1])
                kb.op("act", lambda: A.activation(out=rt2[:], in_=rt1[:], func=AF.Sqrt), reads=[rt1], writes=[rt2])
                kb.op("dve", lambda j=j: V.reciprocal(out=rstd[:, j, :], in_=rt2[:]), reads=[rt2], writes=[rstd])
                for c in range(lo, hi):
                    kb.op("dve", lambda c=c, j=j: V.tensor_tensor(out=cqn[:, c, :], in0=cq_raw[:, c, :], in1=rstd[:, j, :], op=ALU.mult),
                          reads=[cq_raw, rstd], writes=[cqn])
            if PSTOP == 5:
                continue
            b = fm_block(3968, 64)
            evac_copy(st_ik, st_ik[:], b, b[0:64, 0:TT])
            kb.store("sp", st_ik, o_ik[:, t0:t0 + TT], st_ik[:])
            b1 = fm_block(4032, 32)
            b2 = fm_block(4064, 32)
            kb.op("dve", lambda: V.tensor_tensor(out=rt1[0:32, :], in0=b1[0:32, 0:TT], in1=Cc[0:32, :], op=ALU.mult), reads=[b1, Cc], writes=[rt1])
            kb.op("dve", lambda: V.tensor_tensor(out=rt2[0:32, :], in0=b2[0:32, 0:TT], in1=Ss[0:32, :], op=ALU.mult), reads=[b2, Ss], writes=[rt2])
            kb.op("dve", lambda: V.tensor_tensor(out=st_kr[:], in0=rt1[0:32, :], in1=rt2[0:32, :], op=ALU.add), reads=[rt1, rt2], writes=[st_kr])
            kb.store("sp", st_kr, o_kr[:, t0:t0 + TT], st_kr[:])
            b = fm_block(4096, 8)
            evac_copy(st_iw, st_iw[:], b, b[0:8, 0:TT])
            kb.store("sp", st_iw, o_iw[:, t0:t0 + TT], st_iw[:])
            if PSTOP == 6:
                continue
            for s_ in range(NS):
                b = nb()
                kb.mm(b[:, :], [(hT[:, kc, s_ * 128:(s_ + 1) * 128], wva_s[:, kc, :]) for kc in range(8)], [hT, wva_s], b)
                evac_copy(st_va, st_va[:, s_, :], b, b[:, :])
            kb.store("sp", st_va, o_va[t0:t0 + TT, :].rearrange("(s p) c -> p s c", p=128), st_va[:])
            for blk in range(4):
                b = fm_block(0, 128, kcs=2, lhs=lambda kc, blk=blk: wuq_s[:, kc, blk * 128:(blk + 1) * 128], rhs=lambda kc: cqn[:, kc, :])
                evac_copy(st_qn, st_qn[:, blk, :], b, b[:, 0:TT])
            kb.store("sp", st_qn, o_qn[:, t0:t0 + TT].rearrange("(b p) t -> p b t", p=128), st_qn[:])
            for blk in range(2):
                b1 = fm_block(0, 128, kcs=2, lhs=lambda kc, blk=blk: wuq_s[:, kc, 512 + blk * 128:512 + (blk + 1) * 128], rhs=lambda kc: cqn[:, kc, :])
                b2 = fm_block(0, 128, kcs=2, lhs=lambda kc, blk=blk: wuq_s[:, kc, 768 + blk * 128:768 + (blk + 1) * 128], rhs=lambda kc: cqn[:, kc, :])
                kb.op("dve", lambda b1=b1: V.tensor_tensor(out=rt1[:], in0=b1[:, 0:TT], in1=Cc[:], op=ALU.mult), reads=[b1, Cc], writes=[rt1])
                kb.op("dve", lambda b2=b2: V.tensor_tensor(out=rt2[:], in0=b2[:, 0:TT], in1=Ss[:], op=ALU.mult), reads=[b2, Ss], writes=[rt2])
                kb.op("dve", lambda blk=blk: V.tensor_tensor(out=st_qr[:, blk, :], in0=rt1[:], in1=rt2[:], op=ALU.add), reads=[rt1, rt2], writes=[st_qr])
            kb.store("sp", st_qr, o_qr[:, t0:t0 + TT].rearrange("(b p) t -> p b t", p=128), st_qr[:])
            for blk in range(4):
                b = fm_block(0, 128, kcs=1, lhs=lambda kc, blk=blk: wukv_s[:, blk * 128:(blk + 1) * 128], rhs=lambda kc: cqn[:, 2, :])
                evac_copy(st_kn, st_kn[:, blk, :], b, b[:, 0:TT])
            kb.store("sp", st_kn, o_kn[:, t0:t0 + TT].rearrange("(b p) t -> p b t", p=128), st_kn[:])
            for s_ in range(NS):
                b = nb()
                kb.mm(b[:, :], [(cqn[:, 2, s_ * 128:(s_ + 1) * 128], wukv_s[:, 512:1024])], [cqn, wukv_s], b)
                evac_copy(st_vm, st_vm[:, s_, :], b, b[:, :])
            kb.store("sp", st_vm, o_vm[t0:t0 + TT, :].rearrange("(s p) c -> p s c", p=128), st_vm[:])
        kb.finish()
    return nc


def build_A(S):
    T = S // 4
    NSUB = T // 128
    NSB = T // 512
    KL = S + 1536
    NKT = KL // 128
    NKMAX = 2048 * NSB
    ISC = float(64 ** -0.5 * 8 ** -0.5)
    nc = bass.Bass("TRN2", target_bir_lowering=False)
    iq_d = _dram_in(nc, "iq", [64, NSUB, 8, 128], BF16)
    iw_d = _dram_in(nc, "iw", [T, 8], F32)
    qa_d = _dram_in(nc, "qa", [64, NSUB, 8, 128], BF16)
    qm_d = _dram_in(nc, "qm", [96, NSUB, 8, 128], BF16)
    ik_d = _dram_in(nc, "ik", [64, KL], BF16)
    ka_d = _dram_in(nc, "ka", [64, NKT, 8, 128], BF16)
    va_d = _dram_in(nc, "va", [128, NKT, 8, 65], BF16)
    km_d = _dram_in(nc, "km", [96, NKT, 8, 128], BF16)
    vm_d = _dram_in(nc, "vm", [128, NKT, 8, 65], BF16)
    padb_d = _dram_in(nc, "padb", [128, 1536], F32)
    cb_d = _dram_in(nc, "cb", [128, 4, 512], F32)
    tri_d = _dram_in(nc, "tri4", [128, 512], BF16)
    g0_d = _dram_in(nc, "g0", [128, 1024], F32)
    g1_d = _dram_in(nc, "g1", [128, 1024], F32)
    b31_d = _dram_in(nc, "b31", [128, 8], F32)
    negI_d = _dram_in(nc, "negI", [128, 128], BF16)
    pow2_d = _dram_in(nc, "pow2", [128, NIT], F32)
    ya_d = _dram_out(nc, "ya", [64, NSUB, 8, 128], BF16)
    yb_d = _dram_out(nc, "yb", [64, NSUB, 8, 128], BF16)
    scr = _dram_out(nc, "scr", [NSUB * 2, 1024], F32)

    with ExitStack() as es:
        kb = KB(nc, es)
        V, A, G = nc.vector, nc.scalar, nc.gpsimd
        wbanks = [kb.psum("wb%d" % i) for i in range(4)]
        accs = [kb.psum("acc%d" % i) for i in range(4)]
        wi = [0]

        def wb():
            b = wbanks[wi[0] % 4]
            wi[0] += 1
            b.fresh = True
            return b

        def ld_const(name, shape, dt, src):
            b = kb.sbuf(name, shape, dt)
            kb.load("sp", b, b[:], src)
            return b

        padb = ld_const("padb_s", [128, 1536], F32, padb_d)
        cb = ld_const("cb_s", [128, 4, 512], F32, cb_d)
        tri4 = ld_const("tri4_s", [128, 512], BF16, tri_d)
        g0 = ld_const("g0_s", [128, 1024], F32, g0_d)
        g1 = ld_const("g1_s", [128, 1024], F32, g1_d)
        b31 = ld_const("b31_s", [128, 8], F32, b31_d)
        negI = ld_const("negI_s", [128, 128], BF16, negI_d)
        pow2 = ld_const("pow2_s", [128, NIT], F32, pow2_d)
        nb31 = kb.sbuf("nb31", [128, 8], F32)
        kb.op("dve", lambda: V.tensor_scalar(out=nb31[:], in0=b31[:], scalar1=-1.0, scalar2=None, op0=ALU.mult), reads=[b31], writes=[nb31])
        E0 = kb.sbuf("E0", [128, 1024], BF16)
        E1 = kb.sbuf("E1", [128, 1024], BF16)
        for h in range(8):
            kb.op("act", lambda h=h: A.activation(out=E0[:, h * 128:(h + 1) * 128], in_=g0[:, h * 128:(h + 1) * 128], func=AF.Exp,
                                                  bias=nb31[:, h:h + 1], scale=1.0), reads=[g0, nb31], writes=[E0])
            kb.op("act", lambda h=h: A.activation(out=E1[:, h * 128:(h + 1) * 128], in_=g1[:, h * 128:(h + 1) * 128], func=AF.Exp,
                                                  bias=nb31[:, h:h + 1], scale=1.0), reads=[g1, nb31], writes=[E1])

        Ib = kb.sbuf("I", [128, NKMAX], F32)
        nm = kb.sbuf("nm", [128, NKMAX], BF16)
        ikc = [kb.sbuf("ikc%d" % i, [64, 2048], BF16) for i in range(2)]
        kvK = [kb.sbuf("kvK%d" % i, [96, 4, 8, 128], BF16) for i in range(2)]
        kvV = [kb.sbuf("kvV%d" % i, [128, 4, 8, 65], BF16) for i in range(2)]
        iq_s = kb.sbuf("iq_s", [64, 8, 128], BF16)
        qa_s = kb.sbuf("qa_s", [64, 8, 128], BF16)
        qm_s = kb.sbuf("qm_s", [96, 8, 128], BF16)
        iw_s = kb.sbuf("iw_s", [128, 8], F32)
        aw = kb.sbuf("aw", [128, 8], F32)
        sg = kb.sbuf("sg", [128, 8], F32)
        tmp = [kb.sbuf("tmp%d" % i, [128, 512], F32) for i in range(3)]
        Pb = [kb.sbuf("P%d" % i, [128, 512], BF16) for i in range(3)]
        mn = kb.sbuf("mn", [128, 1], F32)
        mx = kb.sbuf("mx", [128, 1], F32)
        wd = kb.sbuf("wd", [128, 1], F32)
        lo = kb.sbuf("lo", [128, 1], F32)
        mid = kb.sbuf("mid", [128, 1], F32)
        gg = kb.sbuf("gg", [128, 1], F32)
        steps = kb.sbuf("steps", [128, NIT], F32)
        cnt = kb.sbuf("cnt", [128, NIT], F32)
        rsum = [kb.sbuf("rsum0", [128, 1024], F32)] * 2
        bcs = [kb.sbuf("bcs0", [64, 1024], F32)] * 2
        junk8 = kb.sbuf("junk8", [128, NKMAX], mybir.dt.uint8)
        ybuf = [kb.sbuf("ybuf%d" % i, [64, 1024], BF16) for i in range(2)]
        ctr = dict(t=0, p=0, g=0, kv=0)

        def attend(sb, nkt, q_s, krows, Kd, Vd, scale, masked, accA, accB, br):
            nch = (nkt + 3) // 4
            bufs = {}
            accA.fresh = True
            accB.fresh = True

            def issue(c):
                i = ctr["kv"] % 2
                ctr["kv"] += 1
                n = min(4, nkt - 4 * c)
                kb.load("sp", kvK[i], kvK[i][0:krows, 0:n], Kd[:, 4 * c:4 * c + n])
                kb.load("sp", kvV[i], kvV[i][:, 0:n], Vd[:, 4 * c:4 * c + n])
                bufs[c] = (kvK[i], kvV[i], n)

            for c in range(min(2, nch)):
                issue(c)
            for c in range(nch):
                kbuf, vbuf, n = bufs[c]
                for w in range(n):
                    kt = 4 * c + w
                    for half in range(2):
                        b = wb()
                        rds = [q_s, kbuf] + ([nm, negI] if masked else [])
                        kb.wait("pe", kb.deps(rds, [b]))
                        ins = None
                        for hh in range(4):
                            h = 4 * half + hh
                            ins = kb.mm_raw(b, b[:, hh * 128:(hh + 1) * 128], kbuf[0:krows, w, h, :], q_s[0:krows, h, :])
                            if masked:
                                ins = kb.mm_raw(b, b[:, hh * 128:(hh + 1) * 128], nm[:, kt * 128:(kt + 1) * 128], negI[:])
                        kb.mark_pe(ins, rds, [b])
                        p = Pb[ctr["p"] % 3]
                        ctr["p"] += 1
                        kb.op("act", lambda p=p, b=b: A.activation(out=p[:], in_=b[:], func=AF.Exp, scale=scale), reads=[b], writes=[p])
                        tab = None
                        if masked and kt == nkt - 1:
                            tab = E0
                        elif masked and kt == nkt - 2:
                            tab = E1
                        elif (not masked) and kt == nkt - 1:
                            tab = None
                            kb.op("pool", lambda p=p: G.tensor_tensor(out=p[:], in0=p[:], in1=tri4[:], op=ALU.mult), reads=[tri4], writes=[p])
                        if tab is not None:
                            kb.op("pool", lambda p=p, tab=tab, half=half: G.tensor_tensor(out=p[:], in0=p[:], in1=tab[:, half * 512:(half + 1) * 512], op=ALU.mult),
                                  reads=[tab], writes=[p])
                        acc = accA if half == 0 else accB
                        kb.wait("pe", kb.deps([p, vbuf], [acc]))
                        for hh in range(4):
                            h = 4 * half + hh
                            ins = kb.mm_raw(acc, acc[0:65, hh * 128:(hh + 1) * 128], vbuf[:, w, h, :], p[:, hh * 128:(hh + 1) * 128])
                        kb.mark_pe(ins, [p, vbuf], [acc])
                if c + 2 < nch:
                    issue(c + 2)
            return lambda: attend_norm(sb, accA, accB, br)

        def attend_norm(sb, accA, accB, br):
            rs, bc, y = rsum[br], bcs[br], ybuf[br]
            kb.op("dve", lambda: V.reciprocal(out=rs[64:65, 0:512], in_=accA[64:65, :]), reads=[accA], writes=[rs])
            kb.op("dve", lambda: V.reciprocal(out=rs[64:65, 512:1024], in_=accB[64:65, :]), reads=[accB], writes=[rs])
            row = sb * 2 + br
            stok = kb.store("sp", rs, scr[row:row + 1, :], rs[64:65, :])
            kb.wait("sp", [stok])
            kb.load("sp", bc, bc[:], scr[row:row + 1, :].partition_broadcast(64))
            kb.op("dve", lambda: V.tensor_tensor(out=y[:, 0:512], in0=accA[0:64, :], in1=bc[:, 0:512], op=ALU.mult), reads=[accA, bc], writes=[y])
            kb.op("dve", lambda: V.tensor_tensor(out=y[:, 512:1024], in0=accB[0:64, :], in1=bc[:, 512:1024], op=ALU.mult), reads=[accB, bc], writes=[y])
            yd = ya_d if br == 0 else yb_d
            kb.store("sp", y, yd[:, sb].rearrange("p h t -> p (h t)"), y[:])

        def phase_idx(sb):
            j, tq = sb // 4, sb % 4
            Nk = 2048 * (j + 1)
            nkt = 16 * (j + 1) - 3 + tq
            kb.load("sp", iq_s, iq_s[:], iq_d[:, sb])
            kb.load("sp", iw_s, iw_s[:], iw_d[sb * 128:(sb + 1) * 128, :])
            kb.op("dve", lambda: V.tensor_scalar(out=sg[:], in0=iw_s[:], scalar1=0.0, scalar2=2.0, op0=ALU.is_ge, op1=ALU.mult), reads=[iw_s], writes=[sg])
            kb.op("dve", lambda: V.tensor_scalar(out=sg[:], in0=sg[:], scalar1=-1.0, scalar2=None, op0=ALU.add), writes=[sg])
            kb.op("dve", lambda: V.tensor_tensor(out=aw[:], in0=iw_s[:], in1=sg[:], op=ALU.mult), reads=[iw_s, sg], writes=[aw])
            kb.op("dve", lambda: V.tensor_scalar(out=aw[:], in0=aw[:], scalar1=ISC, scalar2=None, op0=ALU.mult), writes=[aw])
            for g in range(j + 1):
                ikb = ikc[ctr["g"] % 2]
                ctr["g"] += 1
                kb.load("sp", ikb, ikb[:], ik_d[:, g * 2048:(g + 1) * 2048])
                for c4 in range(4):
                    c = g * 4 + c4
                    for h in range(8):
                        b = wb()
                        kb.mm(b[:, :], [(iq_s[:, h, :], ikb[:, c4 * 512:(c4 + 1) * 512])], [iq_s, ikb], b)
                        t = tmp[ctr["t"] % 3]
                        ctr["t"] += 1
                        kb.op("act", lambda t=t, b=b, h=h: A.activation(out=t[:], in_=b[:], func=AF.Relu, scale=aw[:, h:h + 1]), reads=[b, aw], writes=[t])
                        if h == 0:
                            kb.op("dve", lambda t=t, c=c: V.tensor_scalar(out=Ib[:, c * 512:(c + 1) * 512], in0=t[:], scalar1=sg[:, 0:1], scalar2=None, op0=ALU.mult),
                                  reads=[t, sg], writes=[Ib])
                        else:
                            kb.op("dve", lambda t=t, c=c, h=h: V.scalar_tensor_tensor(out=Ib[:, c * 512:(c + 1) * 512], in0=t[:], scalar=sg[:, h:h + 1],
                                                                                     in1=Ib[:, c * 512:(c + 1) * 512], op0=ALU.mult, op1=ALU.add),
                                  reads=[t, sg], writes=[Ib])

        def phase_thr(sb):
            j, tq = sb // 4, sb % 4
            Nk = 2048 * (j + 1)
            nkt = 16 * (j + 1) - 3 + tq
            kb.op("dve", lambda: V.tensor_reduce(out=mn[:], in_=Ib[:, 0:Nk], axis=AX.X, op=ALU.min), reads=[Ib], writes=[mn])
            kb.op("dve", lambda: V.tensor_tensor(out=Ib[:, 0:1536], in0=Ib[:, 0:1536], in1=padb[:], op=ALU.add), reads=[padb], writes=[Ib])
            kb.op("dve", lambda: V.tensor_tensor(out=Ib[:, Nk - 512:Nk], in0=Ib[:, Nk - 512:Nk], in1=cb[:, tq, :], op=ALU.add), reads=[cb], writes=[Ib])
            kb.op("dve", lambda: V.tensor_reduce(out=mx[:], in_=Ib[:, 0:Nk], axis=AX.X, op=ALU.max), reads=[Ib], writes=[mx])
            kb.op("dve", lambda: V.tensor_tensor(out=wd[:], in0=mx[:], in1=mn[:], op=ALU.subtract), reads=[mx, mn], writes=[wd])
            kb.op("dve", lambda: V.tensor_scalar(out=steps[:], in0=pow2[:], scalar1=wd[:, 0:1], scalar2=None, op0=ALU.mult), reads=[pow2, wd], writes=[steps])
            kb.op("dve", lambda: V.tensor_copy(out=lo[:], in_=mn[:]), reads=[mn], writes=[lo])
            kb.op("dve", lambda: V.memset(cnt[:], 0.0), writes=[cnt])
            for k in range(NIT):
                kb.op("dve", lambda k=k: V.tensor_tensor(out=mid[:], in0=lo[:], in1=steps[:, k:k + 1], op=ALU.add), reads=[lo, steps], writes=[mid])
                kb.op("dve", lambda k=k: V.tensor_scalar(out=junk8[:, 0:Nk], in0=Ib[:, 0:Nk], scalar1=mid[:, 0:1], scalar2=0.0, op0=ALU.is_ge, op1=ALU.add,
                                                         accum_out=cnt[:, k:k + 1]), reads=[Ib, mid], writes=[junk8, cnt])
                kb.op("dve", lambda k=k: V.tensor_scalar(out=gg[:], in0=cnt[:, k:k + 1], scalar1=255.5, scalar2=None, op0=ALU.is_gt), reads=[cnt], writes=[gg])
                kb.op("dve", lambda k=k: V.scalar_tensor_tensor(out=lo[:], in0=gg[:], scalar=steps[:, k:k + 1], in1=lo[:], op0=ALU.mult, op1=ALU.add),
                      reads=[gg, steps], writes=[lo])

        def phase_nm(sb):
            j, tq = sb // 4, sb % 4
            Nk = 2048 * (j + 1)
            nkt = 16 * (j + 1) - 3 + tq
            kb.op("dve", lambda: V.tensor_scalar(out=nm[:, 0:nkt * 128], in0=Ib[:, 0:nkt * 128], scalar1=lo[:, 0:1], scalar2=None, op0=ALU.is_lt),
                  reads=[Ib, lo], writes=[nm])

        def phase_att(sb):
            j, tq = sb // 4, sb % 4
            Nk = 2048 * (j + 1)
            nkt = 16 * (j + 1) - 3 + tq
            kb.load("sp", qa_s, qa_s[:], qa_d[:, sb])
            kb.load("sp", qm_s, qm_s[:], qm_d[:, sb])
            n1 = attend(sb, nkt, qm_s, 96, km_d, vm_d, float(96 ** -0.5), False, accs[2], accs[3], 1)
            n0 = attend(sb, nkt, qa_s, 64, ka_d, va_d, 0.125, True, accs[0], accs[1], 0)
            n1()
            n0()

        phase_idx(0)
        phase_thr(0)
        phase_nm(0)
        for sb in range(NSUB):
            if sb + 1 < NSUB:
                phase_idx(sb + 1)
                phase_thr(sb + 1)
            phase_att(sb)
            if sb + 1 < NSUB:
                phase_nm(sb + 1)
        kb.finish()
    return nc


def adaln_cols(kb, nc, stg, wst, silc, wada2, bada_s, mb, mod_out):
    for q4 in range(4):
        for half in range(2):
            s = stg[wst[0] % 2]
            wst[0] += 1
            kb.load("sp", s, s[:].rearrange("p (k c) -> p k c", k=4),
                    wada2[half * 512:(half + 1) * 512, q4 * 512:(q4 + 1) * 512].rearrange("(k p) c -> p k c", p=128))
            for dc in range(4):
                j = q4 * 4 + dc
                kb.wait("pe", kb.deps([s, silc], [mb]))
                ins = None
                for k4 in range(4):
                    kc = half * 4 + k4
                    ins = kb.mm_raw(mb, mb[:, j * 16:(j + 1) * 16], s[:, k4 * 512 + dc * 128:k4 * 512 + (dc + 1) * 128], silc[:, kc, 0:16])
                kb.mark_pe(ins, [s, silc], [mb])
    kb.op("dve", lambda: nc.vector.tensor_tensor(out=mod_out[:], in0=mb[:, 0:256].rearrange("p (j r) -> p j r", r=16)[:, :, 0], in1=bada_s[:], op=ALU.add),
          reads=[mb, bada_s], writes=[mod_out])


def adaln_bcast(kb, nc, stg, wst, rep, wg, bg_bc, b0, b1, out_bc):
    for kc in range(8):
        s = stg[wst[0] % 2]
        wst[0] += 1
        kb.load("sp", s, s[:, 0:1024], wg[kc * 128:(kc + 1) * 128, :])
        for half, b in enumerate((b0, b1)):
            kb.wait("pe", kb.deps([s, rep], [b]))
            ins = kb.mm_raw(b, b[:, :], rep[:, kc, :], s[:, half * 512:(half + 1) * 512])
            kb.mark_pe(ins, [s, rep], [b])
    for half, b in enumerate((b0, b1)):
        kb.op("dve", lambda half=half, b=b: nc.vector.tensor_tensor(out=out_bc[:, half * 512:(half + 1) * 512], in0=b[:, :],
                                                                   in1=bg_bc[:, half * 512:(half + 1) * 512], op=ALU.add),
              reads=[b, bg_bc], writes=[out_bc])


def build_F(T):
    TH = T + 128
    NTH = TH // 128
    nc = bass.Bass("TRN2", target_bir_lowering=False)
    x_h = _dram_in(nc, "x_h", [TH, D], F32)
    ya_f = _dram_in(nc, "ya_f", [512, TH], BF16)
    yb_f = _dram_in(nc, "yb_f", [512, TH], BF16)
    sga = _dram_in(nc, "sga", [D, TH], F32)
    sgb = _dram_in(nc, "sgb", [D, TH], F32)
    ccol = _dram_in(nc, "ccol", [128, 8], F32)
    wada_g1 = _dram_in(nc, "wada_g1", [D, D], F32)
    bada_g1 = _dram_in(nc, "bada_g1", [1, D], F32)
    wada_2 = _dram_in(nc, "wada_2", [D, 2048], F32)
    bada_2 = _dram_in(nc, "bada_2", [128, 16], F32)
    wada_g2 = _dram_in(nc, "wada_g2", [D, D], F32)
    bada_g2 = _dram_in(nc, "bada_g2", [1, D], F32)
    wba = _dram_in(nc, "wba", [512, D], F32)
    wbb = _dram_in(nc, "wbb", [512, D], F32)
    wout = _dram_in(nc, "wout", [D, D], F32)
    ln1g = _dram_in(nc, "ln1g", [1, D], F32)
    ln1b = _dram_in(nc, "ln1b", [1, D], F32)
    wup = _dram_in(nc, "wup", [D, 2 * DFF], F32)
    cw = _dram_in(nc, "cw", [128, 44, 3], F32)
    cbias = _dram_in(nc, "cbias", [128, 44], F32)
    wdn = _dram_in(nc, "wdn", [DFF, D], F32)
    ln2g = _dram_in(nc, "ln2g", [1, D], F32)
    ln2b = _dram_in(nc, "ln2b", [1, D], F32)
    flag = _dram_in(nc, "flag", [128, 1], F32)
    identd = _dram_in(nc, "ident", [128, 128], F32)
    onesd = _dram_in(nc, "ones", [128, 128], F32)
    o_x1 = _dram_out(nc, "o_x1", [TH, D], F32)
    o_out = _dram_out(nc, "o_out", [T, D], F32)

    with ExitStack() as es:
        kb = KB(nc, es)
        V, A, G = nc.vector, nc.scalar, nc.gpsimd
        banks = [kb.psum("bank%d" % i) for i in range(8)]
        bi = [0]

        def nb():
            b = banks[bi[0] % 8]
            bi[0] += 1
            b.fresh = True
            return b

        def ld_const(name, shape, dt, src, es_=None):
            b = kb.sbuf(name, shape, dt, es_)
            kb.load("sp", b, b[:], src)
            return b

        ident = ld_const("ident_s", [128, 128], F32, identd)
        ones = ld_const("ones_s", [128, 128], F32, onesd)
        ccol_s = ld_const("ccol_s", [128, 8], F32, ccol)
        flag_s = ld_const("flag_s", [128, 1], F32, flag)
        bada2_s = ld_const("bada2_s", [128, 16], F32, bada_2)
        cw_s = ld_const("cw_s", [128, 44, 3], F32, cw)
        cbias_s = ld_const("cbias_s", [128, 44], F32, cbias)
        eps_s = kb.sbuf("eps_s", [128, 1], F32)
        kb.op("dve", lambda: V.memset(eps_s[:], LN_EPS), writes=[eps_s])
        silc = kb.sbuf("silc", [128, 8], F32)
        kb.op("act", lambda: A.activation(out=silc[:], in_=ccol_s[:], func=AF.Silu), reads=[ccol_s], writes=[silc])
        rep = kb.sbuf("rep", [128, 8, 128], F32)
        for kc in range(8):
            kb.op("dve", lambda kc=kc: V.tensor_scalar(out=rep[:, kc, :], in0=ones[:], scalar1=silc[:, kc:kc + 1], scalar2=None, op0=ALU.mult),
                  reads=[ones, silc], writes=[rep])
        stg = [kb.sbuf("stg%d" % i, [128, 2048], F32) for i in range(2)]
        wst = [0]
        mod2 = kb.sbuf("mod2", [128, 16], F32)
        adaln_cols(kb, nc, stg, wst, rep, wada_2, bada2_s, nb(), mod2)
        sc2p = kb.sbuf("sc2p", [128, 8], F32)
        kb.op("dve", lambda: V.tensor_scalar(out=sc2p[:], in0=mod2[:, 8:16], scalar1=1.0, scalar2=None, op0=ALU.add), reads=[mod2], writes=[sc2p])
        g1bc = kb.sbuf("g1bc", [128, D], F32)
        g2bc = kb.sbuf("g2bc", [128, D], F32)
        btmp = kb.sbuf("btmp", [128, D], F32)
        kb.load("sp", btmp, btmp[:], bada_g1.partition_broadcast(128))
        adaln_bcast(kb, nc, stg, wst, rep, wada_g1, btmp, nb(), nb(), g1bc)
        kb.load("sp", btmp, btmp[:], bada_g2.partition_broadcast(128))
        adaln_bcast(kb, nc, stg, wst, rep, wada_g2, btmp, nb(), nb(), g2bc)

        st = kb.sbuf("st", [128, 2, 6], F32)
        mv = kb.sbuf("mv", [128, 4], F32)
        tmpy = kb.sbuf("tmpy", [128, 512], F32)

        def deepnorm_ln(src_bank_fn, resid, gbc, lng, lnb, r, xn, dst):
            for half in range(2):
                b = src_bank_fn(half)
                sl = slice(half * 512, (half + 1) * 512)
                kb.op("dve", lambda b=b, sl=sl: V.tensor_tensor(out=tmpy[:], in0=b[:, :], in1=gbc[:, sl], op=ALU.mult), reads=[b, gbc], writes=[tmpy])
                kb.op("dve", lambda sl=sl: V.scalar_tensor_tensor(out=r[:, sl], in0=resid[:, sl], scalar=float(ALPHA), in1=tmpy[:], op0=ALU.mult, op1=ALU.add),
                      reads=[resid, tmpy], writes=[r])
                kb.op("dve", lambda half=half, sl=sl: V.bn_stats(out=st[:, half, :], in_=r[:, sl]), reads=[r], writes=[st])
            kb.op("dve", lambda: V.bn_aggr(out=mv[:, 0:2], in_=st[:]), reads=[st], writes=[mv])
            kb.op("act", lambda: A.activation(out=mv[:, 2:3], in_=mv[:, 1:2], func=AF.Sqrt, bias=eps_s[:], scale=1.0), reads=[eps_s], writes=[mv])
            kb.op("dve", lambda: V.reciprocal(out=mv[:, 3:4], in_=mv[:, 2:3]), writes=[mv])
            kb.op("dve", lambda: V.tensor_scalar(out=xn[:], in0=r[:], scalar1=mv[:, 0:1], scalar2=mv[:, 3:4], op0=ALU.subtract, op1=ALU.mult),
                  reads=[r, mv], writes=[xn])
            kb.op("dve", lambda: V.tensor_tensor(out=xn[:], in0=xn[:], in1=lng[:], op=ALU.mult), reads=[lng], writes=[xn])
            kb.op("dve", lambda: V.tensor_tensor(out=dst[:], in0=xn[:], in1=lnb[:], op=ALU.add), reads=[xn, lnb], writes=[dst])

        with ExitStack() as es1:
            wba_s = kb.sbuf("wba_s", [128, 4, D], BF16, es1)
            wbb_s = kb.sbuf("wbb_s", [128, 4, D], BF16, es1)
            wout_s = kb.sbuf("wout_s", [128, 8, D], BF16, es1)
            for kc in range(4):
                load_weight_bf16(kb, stg, wba_s, lambda c0, w, kc=kc: wba_s[:, kc, c0:c0 + w], lambda c0, w, kc=kc: wba[kc * 128:(kc + 1) * 128, c0:c0 + w], 128, D, wst)
                load_weight_bf16(kb, stg, wbb_s, lambda c0, w, kc=kc: wbb_s[:, kc, c0:c0 + w], lambda c0, w, kc=kc: wbb[kc * 128:(kc + 1) * 128, c0:c0 + w], 128, D, wst)
            for kc in range(8):
                load_weight_bf16(kb, stg, wout_s, lambda c0, w, kc=kc: wout_s[:, kc, c0:c0 + w], lambda c0, w, kc=kc: wout[kc * 128:(kc + 1) * 128, c0:c0 + w], 128, D, wst)
            lng = kb.sbuf("ln1g_s", [128, D], F32, es1)
            kb.load("sp", lng, lng[:], ln1g.partition_broadcast(128))
            lnb = kb.sbuf("ln1b_s", [128, D], F32, es1)
            kb.load("sp", lnb, lnb[:], ln1b.partition_broadcast(128))
            ya_t = kb.sbuf("ya_t", [128, 4, 128], BF16, es1)
            yb_t = kb.sbuf("yb_t", [128, 4, 128], BF16, es1)
            sga_t = kb.sbuf("sga_t", [128, 8, 128], F32, es1)
            sgb_t = kb.sbuf("sgb_t", [128, 8, 128], F32, es1)
            x_t = kb.sbuf("x_t", [128, D], F32, es1)
            mT = kb.sbuf("mT", [128, 8, 128], BF16, es1)
            t1 = kb.sbuf("t1", [128, 128], F32, es1)
            t2 = kb.sbuf("t2", [128, 128], F32, es1)
            r1 = kb.sbuf("r1", [128, D], F32, es1)
            xn1 = kb.sbuf("xn1", [128, D], F32, es1)
            x1o = kb.sbuf("x1o", [128, D], F32, es1)
            for i in range(NTH):
                t0 = i * 128
                kb.load("sp", ya_t, ya_t[:], ya_f[:, t0:t0 + 128].rearrange("(k p) t -> p k t", p=128))
                kb.load("sp", yb_t, yb_t[:], yb_f[:, t0:t0 + 128].rearrange("(k p) t -> p k t", p=128))
                kb.load("sp", sga_t, sga_t[:], sga[:, t0:t0 + 128].rearrange("(k p) t -> p k t", p=128))
                kb.load("sp", sgb_t, sgb_t[:], sgb[:, t0:t0 + 128].rearrange("(k p) t -> p k t", p=128))
                kb.load("sp", x_t, x_t[:], x_h[t0:t0 + 128, :])
                for cc in range(8):
                    bA = nb()
                    kb.mm(bA[:, 0:128], [(wba_s[:, k, cc * 128:(cc + 1) * 128], ya_t[:, k, :]) for k in range(4)], [wba_s, ya_t], bA)
                    bB = nb()
                    kb.mm(bB[:, 0:128], [(wbb_s[:, k, cc * 128:(cc + 1) * 128], yb_t[:, k, :]) for k in range(4)], [wbb_s, yb_t], bB)
                    kb.op("dve", lambda bA=bA, cc=cc: V.tensor_tensor(out=t1[:], in0=bA[:, 0:128], in1=sga_t[:, cc, :], op=ALU.mult), reads=[bA, sga_t], writes=[t1])
                    kb.op("dve", lambda bB=bB, cc=cc: V.tensor_tensor(out=t2[:], in0=bB[:, 0:128], in1=sgb_t[:, cc, :], op=ALU.mult), reads=[bB, sgb_t], writes=[t2])
                    kb.op("dve", lambda cc=cc: V.tensor_tensor(out=mT[:, cc, :], in0=t1[:], in1=t2[:], op=ALU.add), reads=[t1, t2], writes=[mT])
                ybanks = []
                for half in range(2):
                    b = nb()
                    kb.mm(b[:, :], [(mT[:, kc, :], wout_s[:, kc, half * 512:(half + 1) * 512]) for kc in range(8)], [mT, wout_s], b)
                    ybanks.append(b)
                deepnorm_ln(lambda half: ybanks[half], x_t, g1bc, lng, lnb, r1, xn1, x1o)
                kb.store("sp", x1o, o_x1[t0:t0 + 128, :], x1o[:])
            kb.barrier()

        wup_s = kb.sbuf("wup_s", [128, 8, 2 * DFF], BF16)
        wdn_s = kb.sbuf("wdn_s", [128, 22, D], BF16)
        for kc in range(8):
            load_weight_bf16(kb, stg, wup_s, lambda c0, w, kc=kc: wup_s[:, kc, c0:c0 + w], lambda c0, w, kc=kc: wup[kc * 128:(kc + 1) * 128, c0:c0 + w], 128, 2 * DFF, wst)
        for fb in range(22):
            load_weight_bf16(kb, stg, wdn_s, lambda c0, w, fb=fb: wdn_s[:, fb, c0:c0 + w], lambda c0, w, fb=fb: wdn[fb * 128:(fb + 1) * 128, c0:c0 + w], 128, D, wst)
        lng2 = kb.sbuf("ln2g_s", [128, D], F32)
        kb.load("sp", lng2, lng2[:], ln2g.partition_broadcast(128))
        lnb2 = kb.sbuf("ln2b_s", [128, D], F32)
        kb.load("sp", lnb2, lnb2[:], ln2b.partition_broadcast(128))
        x1_t = kb.sbuf("x1_t", [128, D], F32)
        h2T = kb.sbuf("h2T", [128, 8, 128], BF16)
        ub = [kb.sbuf("ub%d" % i, [128, 130], F32) for i in range(4)]
        cg = kb.sbuf("cg", [128, 128], F32)
        cv = kb.sbuf("cv", [128, 128], F32)
        sil = kb.sbuf("sil", [128, 128], F32)
        act = kb.sbuf("act", [128, 22, 128], BF16)
        carry = kb.sbuf("carry", [128, 44, 2], F32)
        kb.op("dve", lambda: V.memset(carry[:], 0.0), writes=[carry])
        r2 = kb.sbuf("r2", [128, D], F32)
        xn2 = kb.sbuf("xn2", [128, D], F32)
        oo = kb.sbuf("oo", [128, D], F32)
        ui = [0]
        for i in range(NTH):
            t0 = i * 128
            kb.load("sp", x1_t, x1_t[:], o_x1[t0:t0 + 128, :])
            for kc in range(8):
                b = nb()
                kb.wait("pe", kb.deps([x1_t, ident], [b]))
                ins = nc.tensor.transpose(out=b[:, 0:128], in_=x1_t[:, kc * 128:(kc + 1) * 128], identity=ident[:])
                kb.mark_pe(ins, [x1_t, ident], [b])
                kb.op("act", lambda b=b, kc=kc: A.activation(out=h2T[:, kc, :], in_=b[:, 0:128], func=AF.Identity, bias=mod2[:, kc:kc + 1], scale=sc2p[:, kc:kc + 1]),
                      reads=[b, mod2, sc2p], writes=[h2T])
            for fb in range(22):
                for which, blk in ((0, fb), (1, fb + 22)):
                    b = nb()
                    kb.mm(b[:, 0:128], [(wup_s[:, kc, blk * 128:(blk + 1) * 128], h2T[:, kc, :]) for kc in range(8)], [wup_s, h2T], b)
                    u = ub[ui[0] % 4]
                    ui[0] += 1
                    kb.op("act", lambda b=b, u=u: A.copy(out=u[:, 2:130], in_=b[:, 0:128]), reads=[b], writes=[u])
                    kb.op("dve", lambda u=u, blk=blk: V.tensor_copy(out=u[:, 0:2], in_=carry[:, blk, :]), reads=[carry], writes=[u])
                    c = cg if which == 0 else cv
                    kb.op("dve", lambda u=u, blk=blk, c=c: V.tensor_scalar(out=c[:], in0=u[:, 2:130], scalar1=cw_s[:, blk, 2:3], scalar2=cbias_s[:, blk:blk + 1],
                                                                          op0=ALU.mult, op1=ALU.add), reads=[u, cw_s, cbias_s], writes=[c])
                    kb.op("dve", lambda u=u, blk=blk, c=c: V.scalar_tensor_tensor(out=c[:], in0=u[:, 1:129], scalar=cw_s[:, blk, 1:2], in1=c[:], op0=ALU.mult, op1=ALU.add),
                          reads=[u, cw_s], writes=[c])
                    kb.op("dve", lambda u=u, blk=blk, c=c: V.scalar_tensor_tensor(out=c[:], in0=u[:, 0:128], scalar=cw_s[:, blk, 0:1], in1=c[:], op0=ALU.mult, op1=ALU.add),
                          reads=[u, cw_s], writes=[c])
                    if i == 0:
                        kb.op("dve", lambda u=u, blk=blk: V.tensor_scalar(out=carry[:, blk, :], in0=u[:, 128:130], scalar1=flag_s[:, 0:1], scalar2=None, op0=ALU.mult),
                              reads=[u, flag_s], writes=[carry])
                    else:
                        kb.op("dve", lambda u=u, blk=blk: V.tensor_copy(out=carry[:, blk, :], in_=u[:, 128:130]), reads=[u], writes=[carry])
                kb.op("act", lambda: A.activation(out=sil[:], in_=cg[:], func=AF.Silu), reads=[cg], writes=[sil])
                kb.op("dve", lambda fb=fb: V.tensor_tensor(out=act[:, fb, :], in0=sil[:], in1=cv[:], op=ALU.mult), reads=[sil, cv], writes=[act])
            ybanks = []
            for half in range(2):
                b = nb()
                kb.mm(b[:, :], [(act[:, fb, :], wdn_s[:, fb, half * 512:(half + 1) * 512]) for fb in range(22)], [act, wdn_s], b)
                ybanks.append(b)
            deepnorm_ln(lambda half: ybanks[half], x1_t, g2bc, lng2, lnb2, r2, xn2, oo)
            if i > 0:
                kb.store("sp", oo, o_out[t0 - 128:t0, :], oo[:])
        kb.finish()
    return nc


_PROGS = {}
SPLITS = np.cumsum([0, 512, 512, 512, 512, 64, 8, 256, 128, 32, 1024, 1024])
PERM32 = np.concatenate([np.arange(16, 32), np.arange(0, 16)])


def _prog(kind, arg):
    key = (kind, arg)
    if key not in _PROGS:
        _PROGS[key] = {"P": build_P, "A": build_A, "F": build_F}[kind](arg)
    return _PROGS[key]


def _run(nc, in_maps):
    res = run_bass_kernel_spmd(nc, in_maps, core_ids=list(range(8)))
    return res.results


def _t5_bucket(rel):
    n = np.maximum(rel, 0)
    lr = np.log(np.maximum(n, 1).astype(np.float32) / np.float32(16)) / np.float32(np.log(128 / 16))
    large = 16 + (lr * np.float32(16)).astype(np.int32)
    large = np.minimum(large, 31)
    return np.where(n < 16, n, large)


def _consts():
    p = np.arange(128)
    cst = np.zeros((128, 4), np.float32)
    cst[:, 0] = (np.float32(10000.0) ** (-(p % 16).astype(np.float32) * np.float32(2.0 / 32))).astype(np.float32)
    cst[:, 1] = np.where((p % 32) < 16, -1.0, 1.0)
    cst[:, 2] = RMS_EPS
    return cst, np.eye(128, dtype=np.float32), np.ones((128, 128), np.float32)


def run_P(x, c, positions, w_ada, b_ada, w_in, q_norm_g, w_uq, kv_norm_g, w_ukv, S):
    T = S // 4
    cols = [w_in[:, SPLITS[i]:SPLITS[i + 1]] for i in range(11)]
    qa, ka, va, iq, ik, iw, cq, ckv, kr, ga, gb = cols
    w1 = np.ascontiguousarray(np.concatenate([qa, ka, iq, ga, gb, cq, ckv, ik, kr, kr[:, PERM32], iw], axis=1))
    wq = w_uq.reshape(256, 8, 96)
    wuq = np.ascontiguousarray(np.concatenate([wq[:, :, :64].reshape(256, 512), wq[:, :, 64:].reshape(256, 256),
                                               wq[:, :, 64:][:, :, PERM32].reshape(256, 256)], axis=1))
    wkv = w_ukv.reshape(128, 8, 128)
    wukv = np.ascontiguousarray(np.concatenate([wkv[:, :, :64].reshape(128, 512), wkv[:, :, 64:].reshape(128, 512)], axis=1))
    cst, ident, ones = _consts()
    maps = []
    for core in range(8):
        b, cc = divmod(core, 4)
        maps.append(dict(
            x=np.ascontiguousarray(x[b, cc * T:(cc + 1) * T]), ccol=np.ascontiguousarray(c[b].reshape(8, 128).T),
            wada=np.ascontiguousarray(w_ada[:, 0:2048]), bada=np.ascontiguousarray(b_ada[0:2048].reshape(16, 128).T),
            pos=np.ascontiguousarray(positions[b, cc * T:(cc + 1) * T].reshape(1, T)), cst=cst, ident=ident, ones=ones,
            w1=w1, wva=np.ascontiguousarray(va), wuq=wuq, qg=np.ascontiguousarray(q_norm_g.reshape(2, 128).T),
            wukv=wukv, kvg=np.ascontiguousarray(kv_norm_g.reshape(128, 1))))
    r = _run(_prog("P", T), maps)
    out = {}
    for name in ("o_qa", "o_ka", "o_iq", "o_ga", "o_gb", "o_ik", "o_kr", "o_iw", "o_qn", "o_qr", "o_kn"):
        out[name] = [np.concatenate([r[4 * b + cc][name] for cc in range(4)], axis=1) for b in range(2)]
    for name in ("o_va", "o_vm"):
        out[name] = [np.concatenate([r[4 * b + cc][name] for cc in range(4)], axis=0) for b in range(2)]
    return out


def _qtok(cc, S):
    NSB = S // 2048
    return np.concatenate([np.arange((4 * j + cc) * 512, (4 * j + cc + 1) * 512) for j in range(NSB)])


def run_A(P, rel_bias, S):
    T = S // 4
    NSUB = T // 128
    KL = S + 1536
    NKT = KL // 128
    bf = NPBF
    s_i = np.arange(128)[:, None]
    t_i = np.arange(128)[None, :]
    tri = (s_i <= t_i).astype(np.float32)
    tri4 = np.ascontiguousarray(np.tile(tri, (1, 4)).astype(bf))
    bk0 = _t5_bucket(t_i - s_i)
    bk1 = _t5_bucket(128 + t_i - s_i)
    g0 = np.ascontiguousarray(rel_bias[bk0].transpose(0, 2, 1).reshape(128, 1024))
    g1 = np.ascontiguousarray(rel_bias[bk1].transpose(0, 2, 1).reshape(128, 1024))
    b31 = np.ascontiguousarray(np.broadcast_to(rel_bias[31][None, :], (128, 8)))
    negI = (np.eye(128, dtype=np.float32) * NEG).astype(bf)
    pow2 = np.ascontiguousarray(np.broadcast_to((2.0 ** -(np.arange(NIT) + 1.0)).astype(np.float32)[None, :], (128, NIT)))
    cb = np.zeros((128, 4, 512), np.float32)
    sl = np.arange(512)[None, :]
    for tq in range(4):
        cb[:, tq, :] = np.where(sl <= tq * 128 + np.arange(128)[:, None], 0.0, NEG)
    maps = []
    for core in range(8):
        b, cc = divmod(core, 4)
        qt = _qtok(cc, S)
        pl, pr = (3 - cc) * 512, cc * 512

        def qside(arr, rows):
            a = arr.reshape(8, rows, S)[:, :, qt].reshape(8, rows, NSUB, 128)
            return np.ascontiguousarray(a.transpose(1, 2, 0, 3))

        def kside(arr, rows):
            a = np.pad(arr, ((0, 0), (0, 0), (pl, pr))).reshape(8, rows, NKT, 128)
            return np.ascontiguousarray(a.transpose(1, 2, 0, 3))

        def vside(arr):
            a = np.concatenate([arr.reshape(S, 8, 64), np.ones((S, 8, 1), arr.dtype)], axis=2)
            a = np.pad(a, ((pl, pr), (0, 0), (0, 0))).reshape(NKT, 128, 8, 65)
            return np.ascontiguousarray(a.transpose(1, 0, 2, 3))

        qn = P["o_qn"][b].reshape(8, 64, S)
        qr = P["o_qr"][b].reshape(8, 32, S)
        qm = np.concatenate([qn, qr], axis=1).reshape(8 * 96, S)
        kn = P["o_kn"][b].reshape(8, 64, S)
        krr = np.broadcast_to(P["o_kr"][b][None], (8, 32, S))
        km = np.concatenate([kn, krr], axis=1)
        padb = np.zeros((128, 1536), np.float32)
        padb[:, :pl] = NEG
        maps.append(dict(
            iq=qside(P["o_iq"][b], 64), iw=np.ascontiguousarray(P["o_iw"][b][:, qt].T), qa=qside(P["o_qa"][b], 64), qm=qside(qm, 96),
            ik=np.ascontiguousarray(np.pad(P["o_ik"][b], ((0, 0), (pl, pr)))), ka=kside(P["o_ka"][b].reshape(8, 64, S), 64),
            va=vside(P["o_va"][b]), km=kside(km, 96), vm=vside(P["o_vm"][b]),
            padb=padb, cb=cb, tri4=tri4, g0=g0, g1=g1, b31=b31, negI=negI, pow2=pow2))
    r = _run(_prog("A", S), maps)
    ya = [np.zeros((512, S), bf) for _ in range(2)]
    yb = [np.zeros((512, S), bf) for _ in range(2)]
    for core in range(8):
        b, cc = divmod(core, 4)
        qt = _qtok(cc, S)
        for name, dst in (("ya", ya), ("yb", yb)):
            a = r[core][name]
            dst[b][:, qt] = a.transpose(2, 0, 1, 3).reshape(512, T)
    return ya, yb


def run_F(x, c, P, ya, yb, w_ada, b_ada, w_branch_a, w_branch_b, w_out, ln1_g, ln1_b, w_up, conv_w, conv_b, w_down, ln2_g, ln2_b, S):
    T = S // 4
    _, ident, ones = _consts()
    maps = []
    for core in range(8):
        b, cc = divmod(core, 4)
        lo, hi = cc * T, (cc + 1) * T

        def halo_cols(a):
            h = a[:, lo - 128:lo] if cc > 0 else np.zeros((a.shape[0], 128), a.dtype)
            return np.ascontiguousarray(np.concatenate([h, a[:, lo:hi]], axis=1))

        xh = np.concatenate([x[b, lo - 128:lo] if cc > 0 else np.zeros((128, D), np.float32), x[b, lo:hi]], axis=0)
        maps.append(dict(
            x_h=np.ascontiguousarray(xh), ya_f=halo_cols(ya[b]), yb_f=halo_cols(yb[b]), sga=halo_cols(P["o_ga"][b]), sgb=halo_cols(P["o_gb"][b]),
            ccol=np.ascontiguousarray(c[b].reshape(8, 128).T),
            wada_g1=np.ascontiguousarray(w_ada[:, 2048:3072]), bada_g1=np.ascontiguousarray(b_ada[2048:3072].reshape(1, D)),
            wada_2=np.ascontiguousarray(w_ada[:, 3072:5120]), bada_2=np.ascontiguousarray(b_ada[3072:5120].reshape(16, 128).T),
            wada_g2=np.ascontiguousarray(w_ada[:, 5120:6144]), bada_g2=np.ascontiguousarray(b_ada[5120:6144].reshape(1, D)),
            wba=w_branch_a, wbb=w_branch_b, wout=w_out, ln1g=ln1_g.reshape(1, D), ln1b=ln1_b.reshape(1, D),
            wup=w_up, cw=np.ascontiguousarray(conv_w.T.reshape(44, 128, 3).transpose(1, 0, 2)),
            cbias=np.ascontiguousarray(conv_b.reshape(44, 128).T), wdn=w_down, ln2g=ln2_g.reshape(1, D), ln2b=ln2_b.reshape(1, D),
            flag=np.full((128, 1), 0.0 if cc == 0 else 1.0, np.float32), ident=ident, ones=ones))
    r = _run(_prog("F", T), maps)
    out = np.zeros((2, S, D), np.float32)
    for core in range(8):
        b, cc = divmod(core, 4)
        out[b, cc * T:(cc + 1) * T] = r[core]["o_out"]
    return out


def kernel(x, c, positions, rel_bias, w_ada, b_ada, w_in, q_norm_g, w_uq, kv_norm_g, w_ukv,
           w_branch_a, w_branch_b, w_out, ln1_g, ln1_b, w_up, conv_w, conv_b, w_down, ln2_g, ln2_b):
    f = lambda a: np.ascontiguousarray(np.asarray(a))
    x, c, positions, rel_bias = f(x), f(c), f(positions), f(rel_bias)
    S = x.shape[1]
    depth = np.asarray(w_in).shape[0]
    for l in range(depth):
        g = lambda a: f(np.asarray(a)[l])
        P = run_P(x, c, positions, g(w_ada), g(b_ada), g(w_in), g(q_norm_g), g(w_uq), g(kv_norm_g), g(w_ukv), S)
        ya, yb = run_A(P, rel_bias, S)
        x = run_F(x, c, P, ya, yb, g(w_ada), g(b_ada), g(w_branch_a), g(w_branch_b), g(w_out), g(ln1_g), g(ln1_b),
                  g(w_up), g(conv_w), g(conv_b), g(w_down), g(ln2_g), g(ln2_b), S)
    return x.astype(np.float32)
```

```python
import numpy as np
import ml_dtypes
from contextlib import ExitStack
import concourse.bass as bass
import concourse.mybir as mybir
from concourse.bass_utils import run_bass_kernel_spmd

F32 = mybir.dt.float32
BF16 = mybir.dt.bfloat16
I32 = mybir.dt.int32
ALU = mybir.AluOpType
AF = mybir.ActivationFunctionType
AX = mybir.AxisListType
NPBF = ml_dtypes.bfloat16

D = 1024
NHEAD = 8
DFF = 2816
NEG = -30000.0
NIT = 22
LN_EPS = 1e-5
RMS_EPS = 1e-6
ALPHA = 4 ** 0.25
TWO_PI = float(2 * np.pi)
MAGIC = 12582912.0


class Buf:
    def __init__(self, t, name):
        self.t = t
        self.name = name
        self.w = None
        self.r = []
        self.dsem = None
        self.ssem = None
        self.fresh = True

    def __getitem__(self, idx):
        return self.t[idx]


class KB:
    def __init__(self, nc, es):
        self.nc = nc
        self.es = es
        self.E = dict(pe=nc.tensor, act=nc.scalar, dve=nc.vector, pool=nc.gpsimd, sp=nc.sync)
        self.sem = {k: es.enter_context(nc.semaphore("s_" + k)) for k in self.E}
        self.cnt = {k: 0 for k in self.E}
        self.waited = {}
        self.nsem = 0
        self.stores = {}

    def sbuf(self, name, shape, dt, es=None):
        return Buf((es or self.es).enter_context(self.nc.sbuf_tensor(name, shape, dt)), name)

    def psum(self, name):
        return Buf(self.es.enter_context(self.nc.psum_tensor(name, [128, 512], F32)), name)

    def newsem(self, name):
        self.nsem += 1
        key = "%s_%d" % (name, self.nsem)
        return [self.es.enter_context(self.nc.semaphore(key)), 0, key]

    def wait(self, eng, toks):
        best = {}
        for t in toks:
            if t is None:
                continue
            if t[2] not in best or best[t[2]][1] < t[1]:
                best[t[2]] = t
        for t in best.values():
            sem, val, key = t
            if eng == "pe" and key == "pe":
                continue
            if self.waited.get((eng, key), 0) >= val:
                continue
            self.E[eng].wait_ge(sem, val)
            self.waited[(eng, key)] = val

    def deps(self, reads, writes):
        d = []
        for b in reads:
            d.append(b.w)
        for b in writes:
            d.append(b.w)
            d.extend(b.r)
        return d

    def commit(self, tok, reads, writes):
        for b in reads:
            b.r.append(tok)
        for b in writes:
            b.w = tok
            b.r = []

    def op(self, eng, fn, reads=(), writes=()):
        self.wait(eng, self.deps(reads, writes))
        ins = fn()
        self.cnt[eng] += 1
        ins.then_inc(self.sem[eng], 1)
        tok = (self.sem[eng], self.cnt[eng], eng)
        self.commit(tok, reads, writes)
        return tok

    def mm(self, out_ap, pairs, reads, ps):
        self.wait("pe", self.deps(reads, [ps]))
        n = len(pairs)
        ins = None
        for i, (l, r) in enumerate(pairs):
            ins = self.mm_raw(ps, out_ap, l, r)
        self.cnt["pe"] += 1
        ins.then_inc(self.sem["pe"], 1)
        tok = (self.sem["pe"], self.cnt["pe"], "pe")
        self.commit(tok, reads, [ps])
        return tok

    def mm_raw(self, ps, out_ap, l, r):
        st = ps.fresh
        ps.fresh = False
        return self.nc.tensor.matmul(out_ap, lhsT=l, rhs=r, start=st, stop=True, skip_group_check=True)

    def mark_pe(self, ins, reads, writes):
        self.cnt["pe"] += 1
        ins.then_inc(self.sem["pe"], 1)
        tok = (self.sem["pe"], self.cnt["pe"], "pe")
        self.commit(tok, reads, writes)
        return tok

    def load(self, q, buf, out_ap, in_ap, extra_reads=()):
        self.wait(q, self.deps(extra_reads, [buf]))
        if buf.dsem is None:
            buf.dsem = self.newsem("d")
        self.E[q].dma_start(out=out_ap, in_=in_ap).then_inc(buf.dsem[0], 16)
        buf.dsem[1] += 16
        tok = (buf.dsem[0], buf.dsem[1], buf.dsem[2])
        self.commit(tok, extra_reads, [buf])
        return tok

    def load_more(self, q, buf, out_ap, in_ap):
        self.E[q].dma_start(out=out_ap, in_=in_ap).then_inc(buf.dsem[0], 16)
        buf.dsem[1] += 16
        tok = (buf.dsem[0], buf.dsem[1], buf.dsem[2])
        buf.w = tok
        return tok

    def store(self, q, buf, out_ap, in_ap):
        self.wait(q, [buf.w])
        if buf.ssem is None:
            buf.ssem = self.newsem("s")
        self.E[q].dma_start(out=out_ap, in_=in_ap).then_inc(buf.ssem[0], 16)
        buf.ssem[1] += 16
        tok = (buf.ssem[0], buf.ssem[1], buf.ssem[2])
        buf.r.append(tok)
        self.stores[buf.ssem[2]] = tok
        return tok

    def finish(self):
        self.wait("sp", list(self.stores.values()))

    def barrier(self):
        toks = [(self.sem[k], self.cnt[k], k) for k in self.E if self.cnt[k] > 0]
        toks += list(self.stores.values())
        for k in self.E:
            self.wait(k, toks)


def _dram_in(nc, name, shape, dt):
    return nc.dram_tensor(name, list(shape), dt, kind="ExternalInput").ap()


def _dram_out(nc, name, shape, dt):
    return nc.dram_tensor(name, list(shape), dt, kind="ExternalOutput").ap()


def load_weight_bf16(kb, stg, dst_buf, dst_ap_fn, src_ap_fn, nrows, ncols, state):
    c0 = 0
    while c0 < ncols:
        w = min(2048, ncols - c0)
        s = stg[state[0] % 2]
        state[0] += 1
        kb.load("sp", s, s[0:nrows, 0:w], src_ap_fn(c0, w))
        eng = "dve"
        e = kb.E[eng]
        kb.op(eng, lambda e=e, s=s, c0=c0, w=w: e.tensor_copy(out=dst_ap_fn(c0, w), in_=s[0:nrows, 0:w]),
              reads=[s], writes=[dst_buf])
        c0 += w


NC1 = 512 * 3 + 1024 * 2 + 256 + 128 + 64 + 32 + 32 + 8
P_TT = 256


def build_P(T):
    nc = bass.Bass("TRN2", target_bir_lowering=False)
    TT = P_TT
    NT = T // TT
    NS = TT // 128
    x = _dram_in(nc, "x", [T, D], F32)
    ccol = _dram_in(nc, "ccol", [128, 8], F32)
    wada = _dram_in(nc, "wada", [D, 2048], F32)
    bada = _dram_in(nc, "bada", [128, 16], F32)
    pos = _dram_in(nc, "pos", [1, T], I32)
    cst = _dram_in(nc, "cst", [128, 4], F32)
    identd = _dram_in(nc, "ident", [128, 128], F32)
    onesd = _dram_in(nc, "ones", [128, 128], F32)
    w1 = _dram_in(nc, "w1", [D, NC1], F32)
    wva = _dram_in(nc, "wva", [D, 512], F32)
    wuq = _dram_in(nc, "wuq", [256, 1024], F32)
    qg = _dram_in(nc, "qg", [128, 2], F32)
    wukv = _dram_in(nc, "wukv", [128, 1024], F32)
    kvg = _dram_in(nc, "kvg", [128, 1], F32)
    o_qa = _dram_out(nc, "o_qa", [512, T], BF16)
    o_ka = _dram_out(nc, "o_ka", [512, T], BF16)
    o_iq = _dram_out(nc, "o_iq", [512, T], BF16)
    o_ga = _dram_out(nc, "o_ga", [1024, T], F32)
    o_gb = _dram_out(nc, "o_gb", [1024, T], F32)
    o_ik = _dram_out(nc, "o_ik", [64, T], BF16)
    o_kr = _dram_out(nc, "o_kr", [32, T], BF16)
    o_iw = _dram_out(nc, "o_iw", [8, T], F32)
    o_qn = _dram_out(nc, "o_qn", [512, T], BF16)
    o_qr = _dram_out(nc, "o_qr", [256, T], BF16)
    o_kn = _dram_out(nc, "o_kn", [512, T], BF16)
    o_va = _dram_out(nc, "o_va", [T, 512], BF16)
    o_vm = _dram_out(nc, "o_vm", [T, 512], BF16)

    with ExitStack() as es:
        kb = KB(nc, es)
        V, A, G = nc.vector, nc.scalar, nc.gpsimd
        banks = [kb.psum("bank%d" % i) for i in range(8)]
        bi = [0]

        def nb():
            b = banks[bi[0] % 8]
            bi[0] += 1
            b.fresh = True
            return b

        cst_s = kb.sbuf("cst_s", [128, 4], F32)
        kb.load("sp", cst_s, cst_s[:], cst)
        ident = kb.sbuf("ident_s", [128, 128], F32)
        kb.load("sp", ident, ident[:], identd)
        ones = kb.sbuf("ones_s", [128, 128], F32)
        kb.load("sp", ones, ones[:], onesd)
        qg_s = kb.sbuf("qg_s", [128, 2], F32)
        kb.load("sp", qg_s, qg_s[:], qg)
        kvg_s = kb.sbuf("kvg_s", [128, 1], F32)
        kb.load("sp", kvg_s, kvg_s[:], kvg)
        bada_s = kb.sbuf("bada_s", [128, 16], F32)
        kb.load("sp", bada_s, bada_s[:], bada)
        ccol_s = kb.sbuf("ccol_s", [128, 8], F32)
        kb.load("sp", ccol_s, ccol_s[:], ccol)
        silc = kb.sbuf("silc", [128, 8], F32)
        kb.op("act", lambda: A.activation(out=silc[:], in_=ccol_s[:], func=AF.Silu), reads=[ccol_s], writes=[silc])

        silc16 = kb.sbuf("silc16", [128, 8, 16], F32)
        for kc in range(8):
            kb.op("dve", lambda kc=kc: V.tensor_scalar(out=silc16[:, kc, :], in0=ones[:, 0:16], scalar1=silc[:, kc:kc + 1], scalar2=None, op0=ALU.mult),
                  reads=[ones, silc], writes=[silc16])
        stg = [kb.sbuf("stg%d" % i, [128, 2048], F32) for i in range(2)]
        wst = [0]
        mod1 = kb.sbuf("mod1", [128, 16], F32)
        mb = nb()
        for q4 in range(4):
            for half in range(2):
                s = stg[wst[0] % 2]
                wst[0] += 1
                kb.load("sp", s, s[:].rearrange("p (k c) -> p k c", k=4),
                        wada[half * 512:(half + 1) * 512, q4 * 512:(q4 + 1) * 512].rearrange("(k p) c -> p k c", p=128))
                for dc in range(4):
                    j = q4 * 4 + dc
                    kb.wait("pe", kb.deps([s, silc16], [mb]))
                    for k4 in range(4):
                        kc = half * 4 + k4
                        ins = kb.mm_raw(mb, mb[:, j * 16:(j + 1) * 16], s[:, k4 * 512 + dc * 128:k4 * 512 + (dc + 1) * 128], silc16[:, kc, :])
                    kb.mark_pe(ins, [s, silc16], [mb])
        kb.op("dve", lambda: V.tensor_tensor(out=mod1[:], in0=mb[:, 0:256].rearrange("p (j r) -> p j r", r=16)[:, :, 0], in1=bada_s[:], op=ALU.add),
              reads=[mb, bada_s], writes=[mod1])
        sc1p = kb.sbuf("sc1p", [128, 8], F32)
        kb.op("dve", lambda: V.tensor_scalar(out=sc1p[:], in0=mod1[:, 8:16], scalar1=1.0, scalar2=None, op0=ALU.add),
              reads=[mod1], writes=[sc1p])

        w1_s = kb.sbuf("w1_s", [128, 8, NC1], BF16)
        for kc in range(8):
            load_weight_bf16(kb, stg, w1_s, lambda c0, w, kc=kc: w1_s[:, kc, c0:c0 + w],
                             lambda c0, w, kc=kc: w1[kc * 128:(kc + 1) * 128, c0:c0 + w], 128, NC1, wst)
        wva_s = kb.sbuf("wva_s", [128, 8, 512], BF16)
        for kc in range(8):
            load_weight_bf16(kb, stg, wva_s, lambda c0, w, kc=kc: wva_s[:, kc, c0:c0 + w],
                             lambda c0, w, kc=kc: wva[kc * 128:(kc + 1) * 128, c0:c0 + w], 128, 512, wst)
        wuq_s = kb.sbuf("wuq_s", [128, 2, 1024], BF16)
        for kc in range(2):
            s = stg[wst[0] % 2]
            wst[0] += 1
            kb.load("sp", s, s[:, 0:1024], wuq[kc * 128:(kc + 1) * 128, :])
            kb.op("dve", lambda s=s, kc=kc: V.tensor_scalar(out=wuq_s[:, kc, :], in0=s[:, 0:1024], scalar1=qg_s[:, kc:kc + 1],
                                                            scalar2=None, op0=ALU.mult), reads=[s, qg_s], writes=[wuq_s])
        wukv_s = kb.sbuf("wukv_s", [128, 1024], BF16)
        s = stg[wst[0] % 2]
        wst[0] += 1
        kb.load("sp", s, s[:, 0:1024], wukv)
        kb.op("dve", lambda s=s: V.tensor_scalar(out=wukv_s[:], in0=s[:, 0:1024], scalar1=kvg_s[:, 0:1], scalar2=None, op0=ALU.mult),
              reads=[s, kvg_s], writes=[wukv_s])

        xt = kb.sbuf("xt", [128, NS, D], F32)
        hT = kb.sbuf("hT", [128, 8, TT], BF16)
        st_bf = kb.sbuf("st_bf", [128, 12, TT], BF16)
        st_g = [kb.sbuf("st_g%d" % i, [128, 8, TT], F32) for i in range(2)]
        st_ik = kb.sbuf("st_ik", [64, TT], BF16)
        st_kr = kb.sbuf("st_kr", [32, TT], BF16)
        st_iw = kb.sbuf("st_iw", [8, TT], F32)
        st_qn = kb.sbuf("st_qn", [128, 4, TT], BF16)
        st_qr = kb.sbuf("st_qr", [128, 2, TT], BF16)
        st_kn = kb.sbuf("st_kn", [128, 4, TT], BF16)
        st_va = kb.sbuf("st_va", [128, NS, 512], BF16)
        st_vm = kb.sbuf("st_vm", [128, NS, 512], BF16)
        posi = kb.sbuf("posi", [128, TT], I32)
        ang = kb.sbuf("ang", [128, TT], F32)
        kk = kb.sbuf("kk", [128, TT], F32)
        aab = kb.sbuf("aab", [128, TT], F32)
        Cc = kb.sbuf("Cc", [128, TT], F32)
        Ss = kb.sbuf("Ss", [128, TT], F32)
        cq_raw = kb.sbuf("cq_raw", [128, 3, TT], F32)
        sq = kb.sbuf("sq", [128, 3, TT], BF16)
        rstd = kb.sbuf("rstd", [128, 2, TT], F32)
        cqn = kb.sbuf("cqn", [128, 3, TT], BF16)
        rt1 = kb.sbuf("rt1", [128, TT], F32)
        rt2 = kb.sbuf("rt2", [128, TT], F32)
        ones_bf = kb.sbuf("ones_bf", [128, 128], BF16)
        kb.op("dve", lambda: V.tensor_copy(out=ones_bf[:], in_=ones[:]), reads=[ones], writes=[ones_bf])
        half_pi = kb.sbuf("half_pi", [128, 1], F32)
        kb.op("dve", lambda: V.memset(half_pi[:], float(np.pi / 2)), writes=[half_pi])
        evi = [0]

        def evac_copy(dst_buf, dst_ap, src_buf, src_ap):
            evi[0] += 1
            if evi[0] % 2 == 0:
                return kb.op("act", lambda: A.copy(out=dst_ap, in_=src_ap), reads=[src_buf], writes=[dst_buf])
            return kb.op("dve", lambda: V.tensor_copy(out=dst_ap, in_=src_ap), reads=[src_buf], writes=[dst_buf])

        import os
        PSTOP = int(os.environ.get("P_STOP", "99"))
        for it in range(NT if PSTOP > 1 else 0):
            t0 = it * TT
            kb.load("sp", xt, xt[:], x[t0:t0 + TT, :].rearrange("(s p) d -> p s d", p=128))
            kb.load("sp", posi, posi[:], pos[0:1, t0:t0 + TT].partition_broadcast(128))
            for kc in range(8):
                b = nb()
                kb.wait("pe", kb.deps([xt, ident], [b]))
                for s_ in range(NS):
                    ins = nc.tensor.transpose(out=b[:, s_ * 128:(s_ + 1) * 128], in_=xt[:, s_, kc * 128:(kc + 1) * 128], identity=ident[:])
                kb.mark_pe(ins, [xt, ident], [b])
                kb.op("act", lambda b=b, kc=kc: A.activation(out=hT[:, kc, :], in_=b[:, 0:TT], func=AF.Identity,
                                                             bias=mod1[:, kc:kc + 1], scale=sc1p[:, kc:kc + 1]),
                      reads=[b, mod1, sc1p], writes=[hT])
            kb.op("dve", lambda: V.tensor_copy(out=ang[:], in_=posi[:]), reads=[posi], writes=[ang])
            kb.op("dve", lambda: V.tensor_scalar(out=ang[:], in0=ang[:], scalar1=cst_s[:, 0:1], scalar2=None, op0=ALU.mult),
                  reads=[cst_s], writes=[ang])
            kb.op("dve", lambda: V.tensor_scalar(out=kk[:], in0=ang[:], scalar1=float(1.0 / TWO_PI), scalar2=MAGIC, op0=ALU.mult, op1=ALU.add),
                  reads=[ang], writes=[kk])
            kb.op("dve", lambda: V.tensor_scalar(out=kk[:], in0=kk[:], scalar1=-MAGIC, scalar2=None, op0=ALU.add), writes=[kk])
            kb.op("dve", lambda: V.scalar_tensor_tensor(out=ang[:], in0=kk[:], scalar=-6.28125, in1=ang[:], op0=ALU.mult, op1=ALU.add),
                  reads=[kk], writes=[ang])
            kb.op("dve", lambda: V.scalar_tensor_tensor(out=ang[:], in0=kk[:], scalar=-0.0019353071795864769, in1=ang[:], op0=ALU.mult, op1=ALU.add),
                  reads=[kk], writes=[ang])
            kb.op("dve", lambda: V.tensor_scalar(out=ang[:], in0=ang[:], scalar1=float(np.pi), scalar2=float(-np.pi), op0=ALU.min, op1=ALU.max),
                  writes=[ang])
            kb.op("dve", lambda: V.scalar_tensor_tensor(out=aab[:], in0=ang[:], scalar=-1.0, in1=ang[:], op0=ALU.mult, op1=ALU.max),
                  reads=[ang], writes=[aab])
            kb.op("act", lambda: A.activation(out=Ss[:], in_=ang[:], func=AF.Sin, scale=cst_s[:, 1:2]), reads=[ang, cst_s], writes=[Ss])
            kb.op("act", lambda: A.activation(out=Cc[:], in_=aab[:], func=AF.Sin, scale=-1.0, bias=half_pi[:]), reads=[aab, half_pi], writes=[Cc])

            if PSTOP == 2:
                continue

            def fm_block(c0, m, kcs=8, lhs=None, rhs=None):
                b = nb()
                lhs = lhs or (lambda kc: w1_s[:, kc, c0:c0 + m])
                rhs = rhs or (lambda kc: hT[:, kc, :])
                kb.mm(b[0:m, 0:TT], [(lhs(kc), rhs(kc)) for kc in range(kcs)], [hT, w1_s, wuq_s, wukv_s, cqn], b)
                return b

            for blk in range(12):
                b = fm_block(blk * 128, 128)
                evac_copy(st_bf, st_bf[:, blk, :], b, b[:, 0:TT])
            kb.store("sp", st_bf, o_qa[:, t0:t0 + TT].rearrange("(b p) t -> p b t", p=128), st_bf[:, 0:4, :])
            kb.store("sp", st_bf, o_ka[:, t0:t0 + TT].rearrange("(b p) t -> p b t", p=128), st_bf[:, 4:8, :])
            kb.store("sp", st_bf, o_iq[:, t0:t0 + TT].rearrange("(b p) t -> p b t", p=128), st_bf[:, 8:12, :])
            if PSTOP == 3:
                continue
            for gi, og in enumerate((o_ga, o_gb)):
                sg = st_g[gi]
                for blk in range(8):
                    b = fm_block(1536 + gi * 1024 + blk * 128, 128)
                    kb.op("act", lambda b=b, blk=blk, sg=sg: A.activation(out=sg[:, blk, :], in_=b[:, 0:TT], func=AF.Sigmoid),
                          reads=[b], writes=[sg])
                kb.store("sp", sg, og[:, t0:t0 + TT].rearrange("(b p) t -> p b t", p=128), sg[:])
            for j in range(3):
                b = fm_block(3584 + j * 128, 128)
                kb.op("dve", lambda b=b, j=j: V.tensor_copy(out=cq_raw[:, j, :], in_=b[:, 0:TT]), reads=[b], writes=[cq_raw])
                kb.op("dve", lambda j=j: V.tensor_tensor(out=sq[:, j, :], in0=cq_raw[:, j, :], in1=cq_raw[:, j, :], op=ALU.mult), reads=[cq_raw], writes=[sq])
            for j, (lo, hi, n) in enumerate(((0, 2, 256.0), (2, 3, 128.0))):
                b = nb()
                kb.mm(b[:, 0:TT], [(ones_bf[:], sq[:, c, :]) for c in range(lo, hi)], [ones_bf, sq], b)
                kb.op("dve", lambda b=b, n=n: V.tensor_scalar(out=rt1[:], in0=b[:, 0:TT], scalar1=float(1.0 / n), scalar2=float(RMS_EPS), op0=ALU.mult, op1=ALU.add),
                      reads=[b], writes=[rt1])
                kb.op("act", lambda: A.activation(out=rt2[:], in_=rt1[:], func=AF.Sqrt), reads=[rt1], writes=[rt2])
                kb.op("dve", lambda j=j: V.reciprocal(out=rstd[:, j, :], in_=rt2[:]), reads=[rt2], writes=[rstd])
                for c in range(lo, hi):
                    kb.op("dve", lambda c=c, j=j: V.tensor_tensor(out=cqn[:, c, :], in0=cq_raw[:, c, :], in1=rstd[:, j, :], op=ALU.mult),
                          reads=[cq_raw, rstd], writes=[cqn])
            if PSTOP == 5:
                continue
            b = fm_block(3968, 64)
            evac_copy(st_ik, st_ik[:], b, b[0:64, 0:TT])
            kb.store("sp", st_ik, o_ik[:, t0:t0 + TT], st_ik[:])
            b1 = fm_block(4032, 32)
            b2 = fm_block(4064, 32)
            kb.op("dve", lambda: V.tensor_tensor(out=rt1[0:32, :], in0=b1[0:32, 0:TT], in1=Cc[0:32, :], op=ALU.mult), reads=[b1, Cc], writes=[rt1])
            kb.op("dve", lambda: V.tensor_tensor(out=rt2[0:32, :], in0=b2[0:32, 0:TT], in1=Ss[0:32, :], op=ALU.mult), reads=[b2, Ss], writes=[rt2])
            kb.op("dve", lambda: V.tensor_tensor(out=st_kr[:], in0=rt1[0:32, :], in1=rt2[0:32, :], op=ALU.add), reads=[rt1, rt2], writes=[st_kr])
            kb.store("sp", st_kr, o_kr[:, t0:t0 + TT], st_kr[:])
            b = fm_block(4096, 8)
            evac_copy(st_iw, st_iw[:], b, b[0:8, 0:TT])
            kb.store("sp", st_iw, o_iw[:, t0:t0 + TT], st_iw[:])
            if PSTOP == 6:
                continue
            for s_ in range(NS):
                b = nb()
                kb.mm(b[:, :], [(hT[:, kc, s_ * 128:(s_ + 1) * 128], wva_s[:, kc, :]) for kc in range(8)], [hT, wva_s], b)
                evac_copy(st_va, st_va[:, s_, :], b, b[:, :])
            kb.store("sp", st_va, o_va[t0:t0 + TT, :].rearrange("(s p) c -> p s c", p=128), st_va[:])
            for blk in range(4):
                b = fm_block(0, 128, kcs=2, lhs=lambda kc, blk=blk: wuq_s[:, kc, blk * 128:(blk + 1) * 128], rhs=lambda kc: cqn[:, kc, :])
                evac_copy(st_qn, st_qn[:, blk, :], b, b[:, 0:TT])
            kb.store("sp", st_qn, o_qn[:, t0:t0 + TT].rearrange("(b p) t -> p b t", p=128), st_qn[:])
            for blk in range(2):
                b1 = fm_block(0, 128, kcs=2, lhs=lambda kc, blk=blk: wuq_s[:, kc, 512 + blk * 128:512 + (blk + 1) * 128], rhs=lambda kc: cqn[:, kc, :])
                b2 = fm_block(0, 128, kcs=2, lhs=lambda kc, blk=blk: wuq_s[:, kc, 768 + blk * 128:768 + (blk + 1) * 128], rhs=lambda kc: cqn[:, kc, :])
                kb.op("dve", lambda b1=b1: V.tensor_tensor(out=rt1[:], in0=b1[:, 0:TT], in1=Cc[:], op=ALU.mult), reads=[b1, Cc], writes=[rt1])
                kb.op("dve", lambda b2=b2: V.tensor_tensor(out=rt2[:], in0=b2[:, 0:TT], in1=Ss[:], op=ALU.mult), reads=[b2, Ss], writes=[rt2])
                kb.op("dve", lambda blk=blk: V.tensor_tensor(out=st_qr[:, blk, :], in0=rt1[:], in1=rt2[:], op=ALU.add), reads=[rt1, rt2], writes=[st_qr])
            kb.store("sp", st_qr, o_qr[:, t0:t0 + TT].rearrange("(b p) t -> p b t", p=128), st_qr[:])
            for blk in range(4):
                b = fm_block(0, 128, kcs=1, lhs=lambda kc, blk=blk: wukv_s[:, blk * 128:(blk + 1) * 128], rhs=lambda kc: cqn[:, 2, :])
                evac_copy(st_kn, st_kn[:, blk, :], b, b[:, 0:TT])
            kb.store("sp", st_kn, o_kn[:, t0:t0 + TT].rearrange("(b p) t -> p b t", p=128), st_kn[:])
            for s_ in range(NS):
                b = nb()
                kb.mm(b[:, :], [(cqn[:, 2, s_ * 128:(s_ + 1) * 128], wukv_s[:, 512:1024])], [cqn, wukv_s], b)
                evac_copy(st_vm, st_vm[:, s_, :], b, b[:, :])
            kb.store("sp", st_vm, o_vm[t0:t0 + TT, :].rearrange("(s p) c -> p s c", p=128), st_vm[:])
        kb.finish()
    return nc


def build_A(S):
    T = S // 4
    NSUB = T // 128
    NSB = T // 512
    KL = S + 1536
    NKT = KL // 128
    NKMAX = 2048 * NSB
    ISC = float(64 ** -0.5 * 8 ** -0.5)
    nc = bass.Bass("TRN2", target_bir_lowering=False)
    iq_d = _dram_in(nc, "iq", [64, NSUB, 8, 128], BF16)
    iw_d = _dram_in(nc, "iw", [T, 8], F32)
    qa_d = _dram_in(nc, "qa", [64, NSUB, 8, 128], BF16)
    qm_d = _dram_in(nc, "qm", [96, NSUB, 8, 128], BF16)
    ik_d = _dram_in(nc, "ik", [64, KL], BF16)
    ka_d = _dram_in(nc, "ka", [64, NKT, 8, 128], BF16)
    va_d = _dram_in(nc, "va", [128, NKT, 8, 65], BF16)
    km_d = _dram_in(nc, "km", [96, NKT, 8, 128], BF16)
    vm_d = _dram_in(nc, "vm", [128, NKT, 8, 65], BF16)
    padb_d = _dram_in(nc, "padb", [128, 1536], F32)
    cb_d = _dram_in(nc, "cb", [128, 4, 512], F32)
    tri_d = _dram_in(nc, "tri4", [128, 512], BF16)
    g0_d = _dram_in(nc, "g0", [128, 1024], F32)
    g1_d = _dram_in(nc, "g1", [128, 1024], F32)
    b31_d = _dram_in(nc, "b31", [128, 8], F32)
    negI_d = _dram_in(nc, "negI", [128, 128], BF16)
    pow2_d = _dram_in(nc, "pow2", [128, NIT], F32)
    ya_d = _dram_out(nc, "ya", [64, NSUB, 8, 128], BF16)
    yb_d = _dram_out(nc, "yb", [64, NSUB, 8, 128], BF16)
    scr = _dram_out(nc, "scr", [NSUB * 2, 1024], F32)

    with ExitStack() as es:
        kb = KB(nc, es)
        V, A, G = nc.vector, nc.scalar, nc.gpsimd
        wbanks = [kb.psum("wb%d" % i) for i in range(4)]
        accs = [kb.psum("acc%d" % i) for i in range(4)]
        wi = [0]

        def wb():
            b = wbanks[wi[0] % 4]
            wi[0] += 1
            b.fresh = True
            return b

        def ld_const(name, shape, dt, src):
            b = kb.sbuf(name, shape, dt)
            kb.load("sp", b, b[:], src)
            return b

        padb = ld_const("padb_s", [128, 1536], F32, padb_d)
        cb = ld_const("cb_s", [128, 4, 512], F32, cb_d)
        tri4 = ld_const("tri4_s", [128, 512], BF16, tri_d)
        g0 = ld_const("g0_s", [128, 1024], F32, g0_d)
        g1 = ld_const("g1_s", [128, 1024], F32, g1_d)
        b31 = ld_const("b31_s", [128, 8], F32, b31_d)
        negI = ld_const("negI_s", [128, 128], BF16, negI_d)
        pow2 = ld_const("pow2_s", [128, NIT], F32, pow2_d)
        nb31 = kb.sbuf("nb31", [128, 8], F32)
        kb.op("dve", lambda: V.tensor_scalar(out=nb31[:], in0=b31[:], scalar1=-1.0, scalar2=None, op0=ALU.mult), reads=[b31], writes=[nb31])
        E0 = kb.sbuf("E0", [128, 1024], BF16)
        E1 = kb.sbuf("E1", [128, 1024], BF16)
        for h in range(8):
            kb.op("act", lambda h=h: A.activation(out=E0[:, h * 128:(h + 1) * 128], in_=g0[:, h * 128:(h + 1) * 128], func=AF.Exp,
                                                  bias=nb31[:, h:h + 1], scale=1.0), reads=[g0, nb31], writes=[E0])
            kb.op("act", lambda h=h: A.activation(out=E1[:, h * 128:(h + 1) * 128], in_=g1[:, h * 128:(h + 1) * 128], func=AF.Exp,
                                                  bias=nb31[:, h:h + 1], scale=1.0), reads=[g1, nb31], writes=[E1])

        Ib = kb.sbuf("I", [128, NKMAX], F32)
        nm = kb.sbuf("nm", [128, NKMAX], BF16)
        ikc = [kb.sbuf("ikc%d" % i, [64, 2048], BF16) for i in range(2)]
        kvK = [kb.sbuf("kvK%d" % i, [96, 4, 8, 128], BF16) for i in range(2)]
        kvV = [kb.sbuf("kvV%d" % i, [128, 4, 8, 65], BF16) for i in range(2)]
        iq_s = kb.sbuf("iq_s", [64, 8, 128], BF16)
        qa_s = kb.sbuf("qa_s", [64, 8, 128], BF16)
        qm_s = kb.sbuf("qm_s", [96, 8, 128], BF16)
        iw_s = kb.sbuf("iw_s", [128, 8], F32)
        aw = kb.sbuf("aw", [128, 8], F32)
        sg = kb.sbuf("sg", [128, 8], F32)
        tmp = [kb.sbuf("tmp%d" % i, [128, 512], F32) for i in range(3)]
        Pb = [kb.sbuf("P%d" % i, [128, 512], BF16) for i in range(3)]
        mn = kb.sbuf("mn", [128, 1], F32)
        mx = kb.sbuf("mx", [128, 1], F32)
        wd = kb.sbuf("wd", [128, 1], F32)
        lo = kb.sbuf("lo", [128, 1], F32)
        mid = kb.sbuf("mid", [128, 1], F32)
        gg = kb.sbuf("gg", [128, 1], F32)
        steps = kb.sbuf("steps", [128, NIT], F32)
        cnt = kb.sbuf("cnt", [128, NIT], F32)
        rsum = [kb.sbuf("rsum0", [128, 1024], F32)] * 2
        bcs = [kb.sbuf("bcs0", [64, 1024], F32)] * 2
        junk8 = kb.sbuf("junk8", [128, NKMAX], mybir.dt.uint8)
        ybuf = [kb.sbuf("ybuf%d" % i, [64, 1024], BF16) for i in range(2)]
        ctr = dict(t=0, p=0, g=0, kv=0)

        def attend(sb, nkt, q_s, krows, Kd, Vd, scale, masked, accA, accB, br):
            nch = (nkt + 3) // 4
            bufs = {}
            accA.fresh = True
            accB.fresh = True

            def issue(c):
                i = ctr["kv"] % 2
                ctr["kv"] += 1
                n = min(4, nkt - 4 * c)
                kb.load("sp", kvK[i], kvK[i][0:krows, 0:n], Kd[:, 4 * c:4 * c + n])
                kb.load("sp", kvV[i], kvV[i][:, 0:n], Vd[:, 4 * c:4 * c + n])
                bufs[c] = (kvK[i], kvV[i], n)

            steps = []
            for c in range(nch):
                n_ = min(4, nkt - 4 * c)
                for w in range(n_):
                    for half in range(2):
                        steps.append((c, w, half, (w == n_ - 1 and half == 1)))
            for c in range(min(2, nch)):
                issue(c)
            DEPTH = 2
            pend = []

            def emit_pv(st, p):
                c, w, half, last = st
                kbuf, vbuf, n = bufs[c]
                acc = accA if half == 0 else accB
                kb.wait("pe", kb.deps([p, vbuf], [acc]))
                ins = None
                for hh in range(4):
                    h = 4 * half + hh
                    ins = kb.mm_raw(acc, acc[0:65, hh * 128:(hh + 1) * 128], vbuf[:, w, h, :], p[:, hh * 128:(hh + 1) * 128])
                kb.mark_pe(ins, [p, vbuf], [acc])
                if last and c + 2 < nch:
                    issue(c + 2)

            for st in steps:
                c, w, half, last = st
                kbuf, vbuf, n = bufs[c]
                kt = 4 * c + w
                b = wb()
                rds = [q_s, kbuf] + ([nm, negI] if masked else [])
                kb.wait("pe", kb.deps(rds, [b]))
                ins = None
                for hh in range(4):
                    h = 4 * half + hh
                    ins = kb.mm_raw(b, b[:, hh * 128:(hh + 1) * 128], kbuf[0:krows, w, h, :], q_s[0:krows, h, :])
                    if masked:
                        ins = kb.mm_raw(b, b[:, hh * 128:(hh + 1) * 128], nm[:, kt * 128:(kt + 1) * 128], negI[:])
                kb.mark_pe(ins, rds, [b])
                p = Pb[ctr["p"] % 3]
                ctr["p"] += 1
                kb.op("act", lambda p=p, b=b: A.activation(out=p[:], in_=b[:], func=AF.Exp, scale=scale), reads=[b], writes=[p])
                tab = None
                if masked and kt == nkt - 1:
                    tab = E0
                elif masked and kt == nkt - 2:
                    tab = E1
                elif (not masked) and kt == nkt - 1:
                    kb.op("pool", lambda p=p: G.tensor_tensor(out=p[:], in0=p[:], in1=tri4[:], op=ALU.mult), reads=[tri4], writes=[p])
                if tab is not None:
                    kb.op("pool", lambda p=p, tab=tab, half=half: G.tensor_tensor(out=p[:], in0=p[:], in1=tab[:, half * 512:(half + 1) * 512], op=ALU.mult),
                          reads=[tab], writes=[p])
                pend.append((st, p))
                if len(pend) > DEPTH:
                    emit_pv(*pend.pop(0))
            while pend:
                emit_pv(*pend.pop(0))
            return lambda: attend_norm(sb, accA, accB, br)

        def attend_norm(sb, accA, accB, br):
            rs, bc, y = rsum[br], bcs[br], ybuf[br]
            kb.op("dve", lambda: V.reciprocal(out=rs[64:65, 0:512], in_=accA[64:65, :]), reads=[accA], writes=[rs])
            kb.op("dve", lambda: V.reciprocal(out=rs[64:65, 512:1024], in_=accB[64:65, :]), reads=[accB], writes=[rs])
            row = sb * 2 + br
            stok = kb.store("sp", rs, scr[row:row + 1, :], rs[64:65, :])
            kb.wait("sp", [stok])
            kb.load("sp", bc, bc[:], scr[row:row + 1, :].partition_broadcast(64))
            kb.op("dve", lambda: V.tensor_tensor(out=y[:, 0:512], in0=accA[0:64, :], in1=bc[:, 0:512], op=ALU.mult), reads=[accA, bc], writes=[y])
            kb.op("dve", lambda: V.tensor_tensor(out=y[:, 512:1024], in0=accB[0:64, :], in1=bc[:, 512:1024], op=ALU.mult), reads=[accB, bc], writes=[y])
            yd = ya_d if br == 0 else yb_d
            kb.store("sp", y, yd[:, sb].rearrange("p h t -> p (h t)"), y[:])

        def phase_idx(sb):
            j, tq = sb // 4, sb % 4
            Nk = 2048 * (j + 1)
            nkt = 16 * (j + 1) - 3 + tq
            kb.load("sp", iq_s, iq_s[:], iq_d[:, sb])
            kb.load("sp", iw_s, iw_s[:], iw_d[sb * 128:(sb + 1) * 128, :])
            kb.op("dve", lambda: V.tensor_scalar(out=sg[:], in0=iw_s[:], scalar1=0.0, scalar2=2.0, op0=ALU.is_ge, op1=ALU.mult), reads=[iw_s], writes=[sg])
            kb.op("dve", lambda: V.tensor_scalar(out=sg[:], in0=sg[:], scalar1=-1.0, scalar2=None, op0=ALU.add), writes=[sg])
            kb.op("dve", lambda: V.tensor_tensor(out=aw[:], in0=iw_s[:], in1=sg[:], op=ALU.mult), reads=[iw_s, sg], writes=[aw])
            kb.op("dve", lambda: V.tensor_scalar(out=aw[:], in0=aw[:], scalar1=ISC, scalar2=None, op0=ALU.mult), writes=[aw])
            for g in range(j + 1):
                ikb = ikc[ctr["g"] % 2]
                ctr["g"] += 1
                kb.load("sp", ikb, ikb[:], ik_d[:, g * 2048:(g + 1) * 2048])
                for c4 in range(4):
                    c = g * 4 + c4
                    for h in range(8):
                        b = wb()
                        kb.mm(b[:, :], [(iq_s[:, h, :], ikb[:, c4 * 512:(c4 + 1) * 512])], [iq_s, ikb], b)
                        t = tmp[ctr["t"] % 3]
                        ctr["t"] += 1
                        kb.op("act", lambda t=t, b=b, h=h: A.activation(out=t[:], in_=b[:], func=AF.Relu, scale=aw[:, h:h + 1]), reads=[b, aw], writes=[t])
                        if h == 0:
                            kb.op("dve", lambda t=t, c=c: V.tensor_scalar(out=Ib[:, c * 512:(c + 1) * 512], in0=t[:], scalar1=sg[:, 0:1], scalar2=None, op0=ALU.mult),
                                  reads=[t, sg], writes=[Ib])
                        else:
                            kb.op("dve", lambda t=t, c=c, h=h: V.scalar_tensor_tensor(out=Ib[:, c * 512:(c + 1) * 512], in0=t[:], scalar=sg[:, h:h + 1],
                                                                                     in1=Ib[:, c * 512:(c + 1) * 512], op0=ALU.mult, op1=ALU.add),
                                  reads=[t, sg], writes=[Ib])

        def phase_thr(sb):
            j, tq = sb // 4, sb % 4
            Nk = 2048 * (j + 1)
            nkt = 16 * (j + 1) - 3 + tq
            kb.op("dve", lambda: V.tensor_reduce(out=mn[:], in_=Ib[:, 0:Nk], axis=AX.X, op=ALU.min), reads=[Ib], writes=[mn])
            kb.op("dve", lambda: V.tensor_tensor(out=Ib[:, 0:1536], in0=Ib[:, 0:1536], in1=padb[:], op=ALU.add), reads=[padb], writes=[Ib])
            kb.op("dve", lambda: V.tensor_tensor(out=Ib[:, Nk - 512:Nk], in0=Ib[:, Nk - 512:Nk], in1=cb[:, tq, :], op=ALU.add), reads=[cb], writes=[Ib])
            kb.op("dve", lambda: V.tensor_reduce(out=mx[:], in_=Ib[:, 0:Nk], axis=AX.X, op=ALU.max), reads=[Ib], writes=[mx])
            kb.op("dve", lambda: V.tensor_tensor(out=wd[:], in0=mx[:], in1=mn[:], op=ALU.subtract), reads=[mx, mn], writes=[wd])
            kb.op("dve", lambda: V.tensor_scalar(out=steps[:], in0=pow2[:], scalar1=wd[:, 0:1], scalar2=None, op0=ALU.mult), reads=[pow2, wd], writes=[steps])
            kb.op("dve", lambda: V.tensor_copy(out=lo[:], in_=mn[:]), reads=[mn], writes=[lo])
            kb.op("dve", lambda: V.memset(cnt[:], 0.0), writes=[cnt])
            for k in range(NIT):
                kb.op("dve", lambda k=k: V.tensor_tensor(out=mid[:], in0=lo[:], in1=steps[:, k:k + 1], op=ALU.add), reads=[lo, steps], writes=[mid])
                kb.op("dve", lambda k=k: V.tensor_scalar(out=junk8[:, 0:Nk], in0=Ib[:, 0:Nk], scalar1=mid[:, 0:1], scalar2=0.0, op0=ALU.is_ge, op1=ALU.add,
                                                         accum_out=cnt[:, k:k + 1]), reads=[Ib, mid], writes=[junk8, cnt])
                kb.op("dve", lambda k=k: V.tensor_scalar(out=gg[:], in0=cnt[:, k:k + 1], scalar1=255.5, scalar2=None, op0=ALU.is_gt), reads=[cnt], writes=[gg])
                kb.op("dve", lambda k=k: V.scalar_tensor_tensor(out=lo[:], in0=gg[:], scalar=steps[:, k:k + 1], in1=lo[:], op0=ALU.mult, op1=ALU.add),
                      reads=[gg, steps], writes=[lo])

        def phase_nm(sb):
            j, tq = sb // 4, sb % 4
            Nk = 2048 * (j + 1)
            nkt = 16 * (j + 1) - 3 + tq
            kb.op("dve", lambda: V.tensor_scalar(out=nm[:, 0:nkt * 128], in0=Ib[:, 0:nkt * 128], scalar1=lo[:, 0:1], scalar2=None, op0=ALU.is_lt),
                  reads=[Ib, lo], writes=[nm])

        def phase_att(sb):
            j, tq = sb // 4, sb % 4
            Nk = 2048 * (j + 1)
            nkt = 16 * (j + 1) - 3 + tq
            kb.load("sp", qa_s, qa_s[:], qa_d[:, sb])
            kb.load("sp", qm_s, qm_s[:], qm_d[:, sb])
            n1 = attend(sb, nkt, qm_s, 96, km_d, vm_d, float(96 ** -0.5), False, accs[2], accs[3], 1)
            n0 = attend(sb, nkt, qa_s, 64, ka_d, va_d, 0.125, True, accs[0], accs[1], 0)
            n1()
            n0()

        phase_idx(0)
        phase_thr(0)
        phase_nm(0)
        for sb in range(NSUB):
            if sb + 1 < NSUB:
                phase_idx(sb + 1)
                phase_thr(sb + 1)
            phase_att(sb)
            if sb + 1 < NSUB:
                phase_nm(sb + 1)
        kb.finish()
    return nc


def adaln_cols(kb, nc, stg, wst, silc, wada2, bada_s, mb, mod_out):
    for q4 in range(4):
        for half in range(2):
            s = stg[wst[0] % 2]
            wst[0] += 1
            kb.load("sp", s, s[:].rearrange("p (k c) -> p k c", k=4),
                    wada2[half * 512:(half + 1) * 512, q4 * 512:(q4 + 1) * 512].rearrange("(k p) c -> p k c", p=128))
            for dc in range(4):
                j = q4 * 4 + dc
                kb.wait("pe", kb.deps([s, silc], [mb]))
                ins = None
                for k4 in range(4):
                    kc = half * 4 + k4
                    ins = kb.mm_raw(mb, mb[:, j * 16:(j + 1) * 16], s[:, k4 * 512 + dc * 128:k4 * 512 + (dc + 1) * 128], silc[:, kc, 0:16])
                kb.mark_pe(ins, [s, silc], [mb])
    kb.op("dve", lambda: nc.vector.tensor_tensor(out=mod_out[:], in0=mb[:, 0:256].rearrange("p (j r) -> p j r", r=16)[:, :, 0], in1=bada_s[:], op=ALU.add),
          reads=[mb, bada_s], writes=[mod_out])


def adaln_bcast(kb, nc, stg, wst, rep, wg, bg_bc, b0, b1, out_bc):
    for kc in range(8):
        s = stg[wst[0] % 2]
        wst[0] += 1
        kb.load("sp", s, s[:, 0:1024], wg[kc * 128:(kc + 1) * 128, :])
        for half, b in enumerate((b0, b1)):
            kb.wait("pe", kb.deps([s, rep], [b]))
            ins = kb.mm_raw(b, b[:, :], rep[:, kc, :], s[:, half * 512:(half + 1) * 512])
            kb.mark_pe(ins, [s, rep], [b])
    for half, b in enumerate((b0, b1)):
        kb.op("dve", lambda half=half, b=b: nc.vector.tensor_tensor(out=out_bc[:, half * 512:(half + 1) * 512], in0=b[:, :],
                                                                   in1=bg_bc[:, half * 512:(half + 1) * 512], op=ALU.add),
              reads=[b, bg_bc], writes=[out_bc])


def build_F(T):
    TH = T + 128
    NTH = TH // 128
    nc = bass.Bass("TRN2", target_bir_lowering=False)
    x_h = _dram_in(nc, "x_h", [TH, D], F32)
    ya_f = _dram_in(nc, "ya_f", [512, TH], BF16)
    yb_f = _dram_in(nc, "yb_f", [512, TH], BF16)
    sga = _dram_in(nc, "sga", [D, TH], F32)
    sgb = _dram_in(nc, "sgb", [D, TH], F32)
    ccol = _dram_in(nc, "ccol", [128, 8], F32)
    wada_g1 = _dram_in(nc, "wada_g1", [D, D], F32)
    bada_g1 = _dram_in(nc, "bada_g1", [1, D], F32)
    wada_2 = _dram_in(nc, "wada_2", [D, 2048], F32)
    bada_2 = _dram_in(nc, "bada_2", [128, 16], F32)
    wada_g2 = _dram_in(nc, "wada_g2", [D, D], F32)
    bada_g2 = _dram_in(nc, "bada_g2", [1, D], F32)
    wba = _dram_in(nc, "wba", [512, D], F32)
    wbb = _dram_in(nc, "wbb", [512, D], F32)
    wout = _dram_in(nc, "wout", [D, D], F32)
    ln1g = _dram_in(nc, "ln1g", [1, D], F32)
    ln1b = _dram_in(nc, "ln1b", [1, D], F32)
    wup = _dram_in(nc, "wup", [D, 2 * DFF], F32)
    cw = _dram_in(nc, "cw", [128, 44, 3], F32)
    cbias = _dram_in(nc, "cbias", [128, 44], F32)
    wdn = _dram_in(nc, "wdn", [DFF, D], F32)
    ln2g = _dram_in(nc, "ln2g", [1, D], F32)
    ln2b = _dram_in(nc, "ln2b", [1, D], F32)
    flag = _dram_in(nc, "flag", [128, 1], F32)
    identd = _dram_in(nc, "ident", [128, 128], F32)
    onesd = _dram_in(nc, "ones", [128, 128], F32)
    o_x1 = _dram_out(nc, "o_x1", [TH, D], F32)
    o_out = _dram_out(nc, "o_out", [T, D], F32)

    with ExitStack() as es:
        kb = KB(nc, es)
        V, A, G = nc.vector, nc.scalar, nc.gpsimd
        banks = [kb.psum("bank%d" % i) for i in range(8)]
        bi = [0]

        def nb():
            b = banks[bi[0] % 8]
            bi[0] += 1
            b.fresh = True
            return b

        def ld_const(name, shape, dt, src, es_=None):
            b = kb.sbuf(name, shape, dt, es_)
            kb.load("sp", b, b[:], src)
            return b

        ident = ld_const("ident_s", [128, 128], F32, identd)
        ones = ld_const("ones_s", [128, 128], F32, onesd)
        ccol_s = ld_const("ccol_s", [128, 8], F32, ccol)
        flag_s = ld_const("flag_s", [128, 1], F32, flag)
        bada2_s = ld_const("bada2_s", [128, 16], F32, bada_2)
        cw_s = ld_const("cw_s", [128, 44, 3], F32, cw)
        cbias_s = ld_const("cbias_s", [128, 44], F32, cbias)
        eps_s = kb.sbuf("eps_s", [128, 1], F32)
        kb.op("dve", lambda: V.memset(eps_s[:], LN_EPS), writes=[eps_s])
        silc = kb.sbuf("silc", [128, 8], F32)
        kb.op("act", lambda: A.activation(out=silc[:], in_=ccol_s[:], func=AF.Silu), reads=[ccol_s], writes=[silc])
        rep = kb.sbuf("rep", [128, 8, 128], F32)
        for kc in range(8):
            kb.op("dve", lambda kc=kc: V.tensor_scalar(out=rep[:, kc, :], in0=ones[:], scalar1=silc[:, kc:kc + 1], scalar2=None, op0=ALU.mult),
                  reads=[ones, silc], writes=[rep])
        stg = [kb.sbuf("stg%d" % i, [128, 2048], F32) for i in range(2)]
        wst = [0]
        mod2 = kb.sbuf("mod2", [128, 16], F32)
        adaln_cols(kb, nc, stg, wst, rep, wada_2, bada2_s, nb(), mod2)
        sc2p = kb.sbuf("sc2p", [128, 8], F32)
        kb.op("dve", lambda: V.tensor_scalar(out=sc2p[:], in0=mod2[:, 8:16], scalar1=1.0, scalar2=None, op0=ALU.add), reads=[mod2], writes=[sc2p])
        g1bc = kb.sbuf("g1bc", [128, D], F32)
        g2bc = kb.sbuf("g2bc", [128, D], F32)
        btmp = kb.sbuf("btmp", [128, D], F32)
        kb.load("sp", btmp, btmp[:], bada_g1.partition_broadcast(128))
        adaln_bcast(kb, nc, stg, wst, rep, wada_g1, btmp, nb(), nb(), g1bc)
        kb.load("sp", btmp, btmp[:], bada_g2.partition_broadcast(128))
        adaln_bcast(kb, nc, stg, wst, rep, wada_g2, btmp, nb(), nb(), g2bc)

        st = kb.sbuf("st", [128, 2, 6], F32)
        mv = kb.sbuf("mv", [128, 4], F32)
        tmpy = kb.sbuf("tmpy", [128, 512], F32)

        def deepnorm_ln(src_bank_fn, resid, gbc, lng, lnb, r, xn, dst):
            for half in range(2):
                b = src_bank_fn(half)
                sl = slice(half * 512, (half + 1) * 512)
                kb.op("dve", lambda b=b, sl=sl: V.tensor_tensor(out=tmpy[:], in0=b[:, :], in1=gbc[:, sl], op=ALU.mult), reads=[b, gbc], writes=[tmpy])
                kb.op("dve", lambda sl=sl: V.scalar_tensor_tensor(out=r[:, sl], in0=resid[:, sl], scalar=float(ALPHA), in1=tmpy[:], op0=ALU.mult, op1=ALU.add),
                      reads=[resid, tmpy], writes=[r])
                kb.op("dve", lambda half=half, sl=sl: V.bn_stats(out=st[:, half, :], in_=r[:, sl]), reads=[r], writes=[st])
            kb.op("dve", lambda: V.bn_aggr(out=mv[:, 0:2], in_=st[:]), reads=[st], writes=[mv])
            kb.op("act", lambda: A.activation(out=mv[:, 2:3], in_=mv[:, 1:2], func=AF.Sqrt, bias=eps_s[:], scale=1.0), reads=[eps_s], writes=[mv])
            kb.op("dve", lambda: V.reciprocal(out=mv[:, 3:4], in_=mv[:, 2:3]), writes=[mv])
            kb.op("dve", lambda: V.tensor_scalar(out=xn[:], in0=r[:], scalar1=mv[:, 0:1], scalar2=mv[:, 3:4], op0=ALU.subtract, op1=ALU.mult),
                  reads=[r, mv], writes=[xn])
            kb.op("dve", lambda: V.tensor_tensor(out=xn[:], in0=xn[:], in1=lng[:], op=ALU.mult), reads=[lng], writes=[xn])
            kb.op("dve", lambda: V.tensor_tensor(out=dst[:], in0=xn[:], in1=lnb[:], op=ALU.add), reads=[xn, lnb], writes=[dst])

        with ExitStack() as es1:
            wba_s = kb.sbuf("wba_s", [128, 4, D], BF16, es1)
            wbb_s = kb.sbuf("wbb_s", [128, 4, D], BF16, es1)
            wout_s = kb.sbuf("wout_s", [128, 8, D], BF16, es1)
            for kc in range(4):
                load_weight_bf16(kb, stg, wba_s, lambda c0, w, kc=kc: wba_s[:, kc, c0:c0 + w], lambda c0, w, kc=kc: wba[kc * 128:(kc + 1) * 128, c0:c0 + w], 128, D, wst)
                load_weight_bf16(kb, stg, wbb_s, lambda c0, w, kc=kc: wbb_s[:, kc, c0:c0 + w], lambda c0, w, kc=kc: wbb[kc * 128:(kc + 1) * 128, c0:c0 + w], 128, D, wst)
            for kc in range(8):
                load_weight_bf16(kb, stg, wout_s, lambda c0, w, kc=kc: wout_s[:, kc, c0:c0 + w], lambda c0, w, kc=kc: wout[kc * 128:(kc + 1) * 128, c0:c0 + w], 128, D, wst)
            lng = kb.sbuf("ln1g_s", [128, D], F32, es1)
            kb.load("sp", lng, lng[:], ln1g.partition_broadcast(128))
            lnb = kb.sbuf("ln1b_s", [128, D], F32, es1)
            kb.load("sp", lnb, lnb[:], ln1b.partition_broadcast(128))
            ya_t = kb.sbuf("ya_t", [128, 4, 128], BF16, es1)
            yb_t = kb.sbuf("yb_t", [128, 4, 128], BF16, es1)
            sga_t = kb.sbuf("sga_t", [128, 8, 128], F32, es1)
            sgb_t = kb.sbuf("sgb_t", [128, 8, 128], F32, es1)
            x_t = kb.sbuf("x_t", [128, D], F32, es1)
            mT = kb.sbuf("mT", [128, 8, 128], BF16, es1)
            t1 = kb.sbuf("t1", [128, 128], F32, es1)
            t2 = kb.sbuf("t2", [128, 128], F32, es1)
            r1 = kb.sbuf("r1", [128, D], F32, es1)
            xn1 = kb.sbuf("xn1", [128, D], F32, es1)
            x1o = kb.sbuf("x1o", [128, D], F32, es1)
            for i in range(NTH):
                t0 = i * 128
                kb.load("sp", ya_t, ya_t[:], ya_f[:, t0:t0 + 128].rearrange("(k p) t -> p k t", p=128))
                kb.load("sp", yb_t, yb_t[:], yb_f[:, t0:t0 + 128].rearrange("(k p) t -> p k t", p=128))
                kb.load("sp", sga_t, sga_t[:], sga[:, t0:t0 + 128].rearrange("(k p) t -> p k t", p=128))
                kb.load("sp", sgb_t, sgb_t[:], sgb[:, t0:t0 + 128].rearrange("(k p) t -> p k t", p=128))
                kb.load("sp", x_t, x_t[:], x_h[t0:t0 + 128, :])
                for cc in range(8):
                    bA = nb()
                    kb.mm(bA[:, 0:128], [(wba_s[:, k, cc * 128:(cc + 1) * 128], ya_t[:, k, :]) for k in range(4)], [wba_s, ya_t], bA)
                    bB = nb()
                    kb.mm(bB[:, 0:128], [(wbb_s[:, k, cc * 128:(cc + 1) * 128], yb_t[:, k, :]) for k in range(4)], [wbb_s, yb_t], bB)
                    kb.op("dve", lambda bA=bA, cc=cc: V.tensor_tensor(out=t1[:], in0=bA[:, 0:128], in1=sga_t[:, cc, :], op=ALU.mult), reads=[bA, sga_t], writes=[t1])
                    kb.op("dve", lambda bB=bB, cc=cc: V.tensor_tensor(out=t2[:], in0=bB[:, 0:128], in1=sgb_t[:, cc, :], op=ALU.mult), reads=[bB, sgb_t], writes=[t2])
                    kb.op("dve", lambda cc=cc: V.tensor_tensor(out=mT[:, cc, :], in0=t1[:], in1=t2[:], op=ALU.add), reads=[t1, t2], writes=[mT])
                ybanks = []
                for half in range(2):
                    b = nb()
                    kb.mm(b[:, :], [(mT[:, kc, :], wout_s[:, kc, half * 512:(half + 1) * 512]) for kc in range(8)], [mT, wout_s], b)
                    ybanks.append(b)
                deepnorm_ln(lambda half: ybanks[half], x_t, g1bc, lng, lnb, r1, xn1, x1o)
                kb.store("sp", x1o, o_x1[t0:t0 + 128, :], x1o[:])
            kb.barrier()

        wup_s = kb.sbuf("wup_s", [128, 8, 2 * DFF], BF16)
        wdn_s = kb.sbuf("wdn_s", [128, 22, D], BF16)
        for kc in range(8):
            load_weight_bf16(kb, stg, wup_s, lambda c0, w, kc=kc: wup_s[:, kc, c0:c0 + w], lambda c0, w, kc=kc: wup[kc * 128:(kc + 1) * 128, c0:c0 + w], 128, 2 * DFF, wst)
        for fb in range(22):
            load_weight_bf16(kb, stg, wdn_s, lambda c0, w, fb=fb: wdn_s[:, fb, c0:c0 + w], lambda c0, w, fb=fb: wdn[fb * 128:(fb + 1) * 128, c0:c0 + w], 128, D, wst)
        lng2 = kb.sbuf("ln2g_s", [128, D], F32)
        kb.load("sp", lng2, lng2[:], ln2g.partition_broadcast(128))
        lnb2 = kb.sbuf("ln2b_s", [128, D], F32)
        kb.load("sp", lnb2, lnb2[:], ln2b.partition_broadcast(128))
        x1_t = kb.sbuf("x1_t", [128, D], F32)
        h2T = kb.sbuf("h2T", [128, 8, 128], BF16)
        ub = [kb.sbuf("ub%d" % i, [128, 130], F32) for i in range(4)]
        cg = kb.sbuf("cg", [128, 128], F32)
        cv = kb.sbuf("cv", [128, 128], F32)
        sil = kb.sbuf("sil", [128, 128], F32)
        act = kb.sbuf("act", [128, 22, 128], BF16)
        carry = kb.sbuf("carry", [128, 44, 2], F32)
        kb.op("dve", lambda: V.memset(carry[:], 0.0), writes=[carry])
        r2 = kb.sbuf("r2", [128, D], F32)
        xn2 = kb.sbuf("xn2", [128, D], F32)
        oo = kb.sbuf("oo", [128, D], F32)
        ui = [0]
        for i in range(NTH):
            t0 = i * 128
            kb.load("sp", x1_t, x1_t[:], o_x1[t0:t0 + 128, :])
            for kc in range(8):
                b = nb()
                kb.wait("pe", kb.deps([x1_t, ident], [b]))
                ins = nc.tensor.transpose(out=b[:, 0:128], in_=x1_t[:, kc * 128:(kc + 1) * 128], identity=ident[:])
                kb.mark_pe(ins, [x1_t, ident], [b])
                kb.op("act", lambda b=b, kc=kc: A.activation(out=h2T[:, kc, :], in_=b[:, 0:128], func=AF.Identity, bias=mod2[:, kc:kc + 1], scale=sc2p[:, kc:kc + 1]),
                      reads=[b, mod2, sc2p], writes=[h2T])
            for fb in range(22):
                for which, blk in ((0, fb), (1, fb + 22)):
                    b = nb()
                    kb.mm(b[:, 0:128], [(wup_s[:, kc, blk * 128:(blk + 1) * 128], h2T[:, kc, :]) for kc in range(8)], [wup_s, h2T], b)
                    u = ub[ui[0] % 4]
                    ui[0] += 1
                    kb.op("act", lambda b=b, u=u: A.copy(out=u[:, 2:130], in_=b[:, 0:128]), reads=[b], writes=[u])
                    kb.op("dve", lambda u=u, blk=blk: V.tensor_copy(out=u[:, 0:2], in_=carry[:, blk, :]), reads=[carry], writes=[u])
                    c = cg if which == 0 else cv
                    kb.op("dve", lambda u=u, blk=blk, c=c: V.tensor_scalar(out=c[:], in0=u[:, 2:130], scalar1=cw_s[:, blk, 2:3], scalar2=cbias_s[:, blk:blk + 1],
                                                                          op0=ALU.mult, op1=ALU.add), reads=[u, cw_s, cbias_s], writes=[c])
                    kb.op("dve", lambda u=u, blk=blk, c=c: V.scalar_tensor_tensor(out=c[:], in0=u[:, 1:129], scalar=cw_s[:, blk, 1:2], in1=c[:], op0=ALU.mult, op1=ALU.add),
                          reads=[u, cw_s], writes=[c])
                    kb.op("dve", lambda u=u, blk=blk, c=c: V.scalar_tensor_tensor(out=c[:], in0=u[:, 0:128], scalar=cw_s[:, blk, 0:1], in1=c[:], op0=ALU.mult, op1=ALU.add),
                          reads=[u, cw_s], writes=[c])
                    if i == 0:
                        kb.op("dve", lambda u=u, blk=blk: V.tensor_scalar(out=carry[:, blk, :], in0=u[:, 128:130], scalar1=flag_s[:, 0:1], scalar2=None, op0=ALU.mult),
                              reads=[u, flag_s], writes=[carry])
                    else:
                        kb.op("dve", lambda u=u, blk=blk: V.tensor_copy(out=carry[:, blk, :], in_=u[:, 128:130]), reads=[u], writes=[carry])
                kb.op("act", lambda: A.activation(out=sil[:], in_=cg[:], func=AF.Silu), reads=[cg], writes=[sil])
                kb.op("dve", lambda fb=fb: V.tensor_tensor(out=act[:, fb, :], in0=sil[:], in1=cv[:], op=ALU.mult), reads=[sil, cv], writes=[act])
            ybanks = []
            for half in range(2):
                b = nb()
                kb.mm(b[:, :], [(act[:, fb, :], wdn_s[:, fb, half * 512:(half + 1) * 512]) for fb in range(22)], [act, wdn_s], b)
                ybanks.append(b)
            deepnorm_ln(lambda half: ybanks[half], x1_t, g2bc, lng2, lnb2, r2, xn2, oo)
            if i > 0:
                kb.store("sp", oo, o_out[t0 - 128:t0, :], oo[:])
        kb.finish()
    return nc


_PROGS = {}
SPLITS = np.cumsum([0, 512, 512, 512, 512, 64, 8, 256, 128, 32, 1024, 1024])
PERM32 = np.concatenate([np.arange(16, 32), np.arange(0, 16)])


def _prog(kind, arg):
    key = (kind, arg)
    if key not in _PROGS:
        _PROGS[key] = {"P": build_P, "A": build_A, "F": build_F}[kind](arg)
    return _PROGS[key]


def _run(nc, in_maps):
    res = run_bass_kernel_spmd(nc, in_maps, core_ids=list(range(8)))
    return res.results


def _t5_bucket(rel):
    n = np.maximum(rel, 0)
    lr = np.log(np.maximum(n, 1).astype(np.float32) / np.float32(16)) / np.float32(np.log(128 / 16))
    large = 16 + (lr * np.float32(16)).astype(np.int32)
    large = np.minimum(large, 31)
    return np.where(n < 16, n, large)


def _consts():
    p = np.arange(128)
    cst = np.zeros((128, 4), np.float32)
    cst[:, 0] = (np.float32(10000.0) ** (-(p % 16).astype(np.float32) * np.float32(2.0 / 32))).astype(np.float32)
    cst[:, 1] = np.where((p % 32) < 16, -1.0, 1.0)
    cst[:, 2] = RMS_EPS
    return cst, np.eye(128, dtype=np.float32), np.ones((128, 128), np.float32)


def run_P(x, c, positions, w_ada, b_ada, w_in, q_norm_g, w_uq, kv_norm_g, w_ukv, S):
    T = S // 4
    cols = [w_in[:, SPLITS[i]:SPLITS[i + 1]] for i in range(11)]
    qa, ka, va, iq, ik, iw, cq, ckv, kr, ga, gb = cols
    w1 = np.ascontiguousarray(np.concatenate([qa, ka, iq, ga, gb, cq, ckv, ik, kr, kr[:, PERM32], iw], axis=1))
    wq = w_uq.reshape(256, 8, 96)
    wuq = np.ascontiguousarray(np.concatenate([wq[:, :, :64].reshape(256, 512), wq[:, :, 64:].reshape(256, 256),
                                               wq[:, :, 64:][:, :, PERM32].reshape(256, 256)], axis=1))
    wkv = w_ukv.reshape(128, 8, 128)
    wukv = np.ascontiguousarray(np.concatenate([wkv[:, :, :64].reshape(128, 512), wkv[:, :, 64:].reshape(128, 512)], axis=1))
    cst, ident, ones = _consts()
    maps = []
    for core in range(8):
        b, cc = divmod(core, 4)
        maps.append(dict(
            x=np.ascontiguousarray(x[b, cc * T:(cc + 1) * T]), ccol=np.ascontiguousarray(c[b].reshape(8, 128).T),
            wada=np.ascontiguousarray(w_ada[:, 0:2048]), bada=np.ascontiguousarray(b_ada[0:2048].reshape(16, 128).T),
            pos=np.ascontiguousarray(positions[b, cc * T:(cc + 1) * T].reshape(1, T)), cst=cst, ident=ident, ones=ones,
            w1=w1, wva=np.ascontiguousarray(va), wuq=wuq, qg=np.ascontiguousarray(q_norm_g.reshape(2, 128).T),
            wukv=wukv, kvg=np.ascontiguousarray(kv_norm_g.reshape(128, 1))))
    r = _run(_prog("P", T), maps)
    out = {}
    for name in ("o_qa", "o_ka", "o_iq", "o_ga", "o_gb", "o_ik", "o_kr", "o_iw", "o_qn", "o_qr", "o_kn"):
        out[name] = [np.concatenate([r[4 * b + cc][name] for cc in range(4)], axis=1) for b in range(2)]
    for name in ("o_va", "o_vm"):
        out[name] = [np.concatenate([r[4 * b + cc][name] for cc in range(4)], axis=0) for b in range(2)]
    return out


def _qtok(cc, S):
    NSB = S // 2048
    return np.concatenate([np.arange((4 * j + cc) * 512, (4 * j + cc + 1) * 512) for j in range(NSB)])


def run_A(P, rel_bias, S):
    T = S // 4
    NSUB = T // 128
    KL = S + 1536
    NKT = KL // 128
    bf = NPBF
    s_i = np.arange(128)[:, None]
    t_i = np.arange(128)[None, :]
    tri = (s_i <= t_i).astype(np.float32)
    tri4 = np.ascontiguousarray(np.tile(tri, (1, 4)).astype(bf))
    bk0 = _t5_bucket(t_i - s_i)
    bk1 = _t5_bucket(128 + t_i - s_i)
    g0 = np.ascontiguousarray(rel_bias[bk0].transpose(0, 2, 1).reshape(128, 1024))
    g1 = np.ascontiguousarray(rel_bias[bk1].transpose(0, 2, 1).reshape(128, 1024))
    b31 = np.ascontiguousarray(np.broadcast_to(rel_bias[31][None, :], (128, 8)))
    negI = (np.eye(128, dtype=np.float32) * NEG).astype(bf)
    pow2 = np.ascontiguousarray(np.broadcast_to((2.0 ** -(np.arange(NIT) + 1.0)).astype(np.float32)[None, :], (128, NIT)))
    cb = np.zeros((128, 4, 512), np.float32)
    sl = np.arange(512)[None, :]
    for tq in range(4):
        cb[:, tq, :] = np.where(sl <= tq * 128 + np.arange(128)[:, None], 0.0, NEG)
    maps = []
    for core in range(8):
        b, cc = divmod(core, 4)
        qt = _qtok(cc, S)
        pl, pr = (3 - cc) * 512, cc * 512

        def qside(arr, rows):
            a = arr.reshape(8, rows, S)[:, :, qt].reshape(8, rows, NSUB, 128)
            return np.ascontiguousarray(a.transpose(1, 2, 0, 3))

        def kside(arr, rows):
            a = np.pad(arr, ((0, 0), (0, 0), (pl, pr))).reshape(8, rows, NKT, 128)
            return np.ascontiguousarray(a.transpose(1, 2, 0, 3))

        def vside(arr):
            a = np.concatenate([arr.reshape(S, 8, 64), np.ones((S, 8, 1), arr.dtype)], axis=2)
            a = np.pad(a, ((pl, pr), (0, 0), (0, 0))).reshape(NKT, 128, 8, 65)
            return np.ascontiguousarray(a.transpose(1, 0, 2, 3))

        qn = P["o_qn"][b].reshape(8, 64, S)
        qr = P["o_qr"][b].reshape(8, 32, S)
        qm = np.concatenate([qn, qr], axis=1).reshape(8 * 96, S)
        kn = P["o_kn"][b].reshape(8, 64, S)
        krr = np.broadcast_to(P["o_kr"][b][None], (8, 32, S))
        km = np.concatenate([kn, krr], axis=1)
        padb = np.zeros((128, 1536), np.float32)
        padb[:, :pl] = NEG
        maps.append(dict(
            iq=qside(P["o_iq"][b], 64), iw=np.ascontiguousarray(P["o_iw"][b][:, qt].T), qa=qside(P["o_qa"][b], 64), qm=qside(qm, 96),
            ik=np.ascontiguousarray(np.pad(P["o_ik"][b], ((0, 0), (pl, pr)))), ka=kside(P["o_ka"][b].reshape(8, 64, S), 64),
            va=vside(P["o_va"][b]), km=kside(km, 96), vm=vside(P["o_vm"][b]),
            padb=padb, cb=cb, tri4=tri4, g0=g0, g1=g1, b31=b31, negI=negI, pow2=pow2))
    r = _run(_prog("A", S), maps)
    ya = [np.zeros((512, S), bf) for _ in range(2)]
    yb = [np.zeros((512, S), bf) for _ in range(2)]
    for core in range(8):
        b, cc = divmod(core, 4)
        qt = _qtok(cc, S)
        for name, dst in (("ya", ya), ("yb", yb)):
            a = r[core][name]
            dst[b][:, qt] = a.transpose(2, 0, 1, 3).reshape(512, T)
    return ya, yb


def run_F(x, c, P, ya, yb, w_ada, b_ada, w_branch_a, w_branch_b, w_out, ln1_g, ln1_b, w_up, conv_w, conv_b, w_down, ln2_g, ln2_b, S):
    T = S // 4
    _, ident, ones = _consts()
    maps = []
    for core in range(8):
        b, cc = divmod(core, 4)
        lo, hi = cc * T, (cc + 1) * T

        def halo_cols(a):
            h = a[:, lo - 128:lo] if cc > 0 else np.zeros((a.shape[0], 128), a.dtype)
            return np.ascontiguousarray(np.concatenate([h, a[:, lo:hi]], axis=1))

        xh = np.concatenate([x[b, lo - 128:lo] if cc > 0 else np.zeros((128, D), np.float32), x[b, lo:hi]], axis=0)
        maps.append(dict(
            x_h=np.ascontiguousarray(xh), ya_f=halo_cols(ya[b]), yb_f=halo_cols(yb[b]), sga=halo_cols(P["o_ga"][b]), sgb=halo_cols(P["o_gb"][b]),
            ccol=np.ascontiguousarray(c[b].reshape(8, 128).T),
            wada_g1=np.ascontiguousarray(w_ada[:, 2048:3072]), bada_g1=np.ascontiguousarray(b_ada[2048:3072].reshape(1, D)),
            wada_2=np.ascontiguousarray(w_ada[:, 3072:5120]), bada_2=np.ascontiguousarray(b_ada[3072:5120].reshape(16, 128).T),
            wada_g2=np.ascontiguousarray(w_ada[:, 5120:6144]), bada_g2=np.ascontiguousarray(b_ada[5120:6144].reshape(1, D)),
            wba=w_branch_a, wbb=w_branch_b, wout=w_out, ln1g=ln1_g.reshape(1, D), ln1b=ln1_b.reshape(1, D),
            wup=w_up, cw=np.ascontiguousarray(conv_w.T.reshape(44, 128, 3).transpose(1, 0, 2)),
            cbias=np.ascontiguousarray(conv_b.reshape(44, 128).T), wdn=w_down, ln2g=ln2_g.reshape(1, D), ln2b=ln2_b.reshape(1, D),
            flag=np.full((128, 1), 0.0 if cc == 0 else 1.0, np.float32), ident=ident, ones=ones))
    r = _run(_prog("F", T), maps)
    out = np.zeros((2, S, D), np.float32)
    for core in range(8):
        b, cc = divmod(core, 4)
        out[b, cc * T:(cc + 1) * T] = r[core]["o_out"]
    return out


def kernel(x, c, positions, rel_bias, w_ada, b_ada, w_in, q_norm_g, w_uq, kv_norm_g, w_ukv,
           w_branch_a, w_branch_b, w_out, ln1_g, ln1_b, w_up, conv_w, conv_b, w_down, ln2_g, ln2_b):
    f = lambda a: np.ascontiguousarray(np.asarray(a))
    x, c, positions, rel_bias = f(x), f(c), f(positions), f(rel_bias)
    S = x.shape[1]
    depth = np.asarray(w_in).shape[0]
    for l in range(depth):
        g = lambda a: f(np.asarray(a)[l])
        P = run_P(x, c, positions, g(w_ada), g(b_ada), g(w_in), g(q_norm_g), g(w_uq), g(kv_norm_g), g(w_ukv), S)
        ya, yb = run_A(P, rel_bias, S)
        x = run_F(x, c, P, ya, yb, g(w_ada), g(b_ada), g(w_branch_a), g(w_branch_b), g(w_out), g(ln1_g), g(ln1_b),
                  g(w_up), g(conv_w), g(conv_b), g(w_down), g(ln2_g), g(ln2_b), S)
    return x.astype(np.float32)
```

```python
import numpy as np
import ml_dtypes
from contextlib import ExitStack
import concourse.bass as bass
import concourse.mybir as mybir
from concourse.bass_utils import run_bass_kernel_spmd

F32 = mybir.dt.float32
BF16 = mybir.dt.bfloat16
I32 = mybir.dt.int32
ALU = mybir.AluOpType
AF = mybir.ActivationFunctionType
AX = mybir.AxisListType
NPBF = ml_dtypes.bfloat16

D = 1024
NHEAD = 8
DFF = 2816
NEG = -30000.0
NIT = 22
LN_EPS = 1e-5
RMS_EPS = 1e-6
ALPHA = 4 ** 0.25
TWO_PI = float(2 * np.pi)
MAGIC = 12582912.0


class Buf:
    def __init__(self, t, name):
        self.t = t
        self.name = name
        self.w = None
        self.r = []
        self.dsem = None
        self.ssem = None
        self.fresh = True

    def __getitem__(self, idx):
        return self.t[idx]


class KB:
    def __init__(self, nc, es):
        self.nc = nc
        self.es = es
        self.E = dict(pe=nc.tensor, act=nc.scalar, dve=nc.vector, pool=nc.gpsimd, sp=nc.sync)
        self.sem = {k: es.enter_context(nc.semaphore("s_" + k)) for k in self.E}
        self.cnt = {k: 0 for k in self.E}
        self.waited = {}
        self.nsem = 0
        self.stores = {}

    def sbuf(self, name, shape, dt, es=None):
        return Buf((es or self.es).enter_context(self.nc.sbuf_tensor(name, shape, dt)), name)

    def psum(self, name):
        return Buf(self.es.enter_context(self.nc.psum_tensor(name, [128, 512], F32)), name)

    def newsem(self, name):
        self.nsem += 1
        key = "%s_%d" % (name, self.nsem)
        return [self.es.enter_context(self.nc.semaphore(key)), 0, key]

    def wait(self, eng, toks):
        best = {}
        for t in toks:
            if t is None:
                continue
            if t[2] not in best or best[t[2]][1] < t[1]:
                best[t[2]] = t
        for t in best.values():
            sem, val, key = t
            if eng == "pe" and key == "pe":
                continue
            if self.waited.get((eng, key), 0) >= val:
                continue
            self.E[eng].wait_ge(sem, val)
            self.waited[(eng, key)] = val

    def deps(self, reads, writes):
        d = []
        for b in reads:
            d.append(b.w)
        for b in writes:
            d.append(b.w)
            d.extend(b.r)
        return d

    def commit(self, tok, reads, writes):
        for b in reads:
            b.r.append(tok)
        for b in writes:
            b.w = tok
            b.r = []

    def op(self, eng, fn, reads=(), writes=()):
        self.wait(eng, self.deps(reads, writes))
        ins = fn()
        self.cnt[eng] += 1
        ins.then_inc(self.sem[eng], 1)
        tok = (self.sem[eng], self.cnt[eng], eng)
        self.commit(tok, reads, writes)
        return tok

    def mm(self, out_ap, pairs, reads, ps):
        self.wait("pe", self.deps(reads, [ps]))
        n = len(pairs)
        ins = None
        for i, (l, r) in enumerate(pairs):
            ins = self.mm_raw(ps, out_ap, l, r)
        self.cnt["pe"] += 1
        ins.then_inc(self.sem["pe"], 1)
        tok = (self.sem["pe"], self.cnt["pe"], "pe")
        self.commit(tok, reads, [ps])
        return tok

    def mm_raw(self, ps, out_ap, l, r):
        st = ps.fresh
        ps.fresh = False
        return self.nc.tensor.matmul(out_ap, lhsT=l, rhs=r, start=st, stop=True, skip_group_check=True)

    def mark_pe(self, ins, reads, writes):
        self.cnt["pe"] += 1
        ins.then_inc(self.sem["pe"], 1)
        tok = (self.sem["pe"], self.cnt["pe"], "pe")
        self.commit(tok, reads, writes)
        return tok

    def load(self, q, buf, out_ap, in_ap, extra_reads=()):
        self.wait(q, self.deps(extra_reads, [buf]))
        if buf.dsem is None:
            buf.dsem = self.newsem("d")
        self.E[q].dma_start(out=out_ap, in_=in_ap).then_inc(buf.dsem[0], 16)
        buf.dsem[1] += 16
        tok = (buf.dsem[0], buf.dsem[1], buf.dsem[2])
        self.commit(tok, extra_reads, [buf])
        return tok

    def load_more(self, q, buf, out_ap, in_ap):
        self.E[q].dma_start(out=out_ap, in_=in_ap).then_inc(buf.dsem[0], 16)
        buf.dsem[1] += 16
        tok = (buf.dsem[0], buf.dsem[1], buf.dsem[2])
        buf.w = tok
        return tok

    def store(self, q, buf, out_ap, in_ap):
        self.wait(q, [buf.w])
        if buf.ssem is None:
            buf.ssem = self.newsem("s")
        self.E[q].dma_start(out=out_ap, in_=in_ap).then_inc(buf.ssem[0], 16)
        buf.ssem[1] += 16
        tok = (buf.ssem[0], buf.ssem[1], buf.ssem[2])
        buf.r.append(tok)
        self.stores[buf.ssem[2]] = tok
        return tok

    def finish(self):
        self.wait("sp", list(self.stores.values()))

    def barrier(self):
        toks = [(self.sem[k], self.cnt[k], k) for k in self.E if self.cnt[k] > 0]
        toks += list(self.stores.values())
        for k in self.E:
            self.wait(k, toks)


def _dram_in(nc, name, shape, dt):
    return nc.dram_tensor(name, list(shape), dt, kind="ExternalInput").ap()


def _dram_out(nc, name, shape, dt):
    return nc.dram_tensor(name, list(shape), dt, kind="ExternalOutput").ap()


def load_weight_bf16(kb, stg, dst_buf, dst_ap_fn, src_ap_fn, nrows, ncols, state):
    c0 = 0
    while c0 < ncols:
        w = min(2048, ncols - c0)
        s = stg[state[0] % 2]
        state[0] += 1
        kb.load("sp", s, s[0:nrows, 0:w], src_ap_fn(c0, w))
        eng = "dve"
        e = kb.E[eng]
        kb.op(eng, lambda e=e, s=s, c0=c0, w=w: e.tensor_copy(out=dst_ap_fn(c0, w), in_=s[0:nrows, 0:w]),
              reads=[s], writes=[dst_buf])
        c0 += w


NC1 = 512 * 3 + 1024 * 2 + 256 + 128 + 64 + 32 + 32 + 8
P_TT = 256


def build_P(T):
    nc = bass.Bass("TRN2", target_bir_lowering=False)
    TT = P_TT
    NT = T // TT
    NS = TT // 128
    x = _dram_in(nc, "x", [T, D], F32)
    ccol = _dram_in(nc, "ccol", [128, 8], F32)
    wada = _dram_in(nc, "wada", [D, 2048], F32)
    bada = _dram_in(nc, "bada", [128, 16], F32)
    pos = _dram_in(nc, "pos", [1, T], I32)
    cst = _dram_in(nc, "cst", [128, 4], F32)
    identd = _dram_in(nc, "ident", [128, 128], F32)
    onesd = _dram_in(nc, "ones", [128, 128], F32)
    w1 = _dram_in(nc, "w1", [D, NC1], F32)
    wva = _dram_in(nc, "wva", [D, 512], F32)
    wuq = _dram_in(nc, "wuq", [256, 1024], F32)
    qg = _dram_in(nc, "qg", [128, 2], F32)
    wukv = _dram_in(nc, "wukv", [128, 1024], F32)
    kvg = _dram_in(nc, "kvg", [128, 1], F32)
    o_qa = _dram_out(nc, "o_qa", [512, T], BF16)
    o_ka = _dram_out(nc, "o_ka", [512, T], BF16)
    o_iq = _dram_out(nc, "o_iq", [512, T], BF16)
    o_ga = _dram_out(nc, "o_ga", [1024, T], F32)
    o_gb = _dram_out(nc, "o_gb", [1024, T], F32)
    o_ik = _dram_out(nc, "o_ik", [64, T], BF16)
    o_kr = _dram_out(nc, "o_kr", [32, T], BF16)
    o_iw = _dram_out(nc, "o_iw", [8, T], F32)
    o_qn = _dram_out(nc, "o_qn", [512, T], BF16)
    o_qr = _dram_out(nc, "o_qr", [256, T], BF16)
    o_kn = _dram_out(nc, "o_kn", [512, T], BF16)
    o_va = _dram_out(nc, "o_va", [T, 512], BF16)
    o_vm = _dram_out(nc, "o_vm", [T, 512], BF16)

    with ExitStack() as es:
        kb = KB(nc, es)
        V, A, G = nc.vector, nc.scalar, nc.gpsimd
        banks = [kb.psum("bank%d" % i) for i in range(8)]
        bi = [0]

        def nb():
            b = banks[bi[0] % 8]
            bi[0] += 1
            b.fresh = True
            return b

        cst_s = kb.sbuf("cst_s", [128, 4], F32)
        kb.load("sp", cst_s, cst_s[:], cst)
        ident = kb.sbuf("ident_s", [128, 128], F32)
        kb.load("sp", ident, ident[:], identd)
        ones = kb.sbuf("ones_s", [128, 128], F32)
        kb.load("sp", ones, ones[:], onesd)
        qg_s = kb.sbuf("qg_s", [128, 2], F32)
        kb.load("sp", qg_s, qg_s[:], qg)
        kvg_s = kb.sbuf("kvg_s", [128, 1], F32)
        kb.load("sp", kvg_s, kvg_s[:], kvg)
        bada_s = kb.sbuf("bada_s", [128, 16], F32)
        kb.load("sp", bada_s, bada_s[:], bada)
        ccol_s = kb.sbuf("ccol_s", [128, 8], F32)
        kb.load("sp", ccol_s, ccol_s[:], ccol)
        silc = kb.sbuf("silc", [128, 8], F32)
        kb.op("act", lambda: A.activation(out=silc[:], in_=ccol_s[:], func=AF.Silu), reads=[ccol_s], writes=[silc])

        silc16 = kb.sbuf("silc16", [128, 8, 16], F32)
        for kc in range(8):
            kb.op("dve", lambda kc=kc: V.tensor_scalar(out=silc16[:, kc, :], in0=ones[:, 0:16], scalar1=silc[:, kc:kc + 1], scalar2=None, op0=ALU.mult),
                  reads=[ones, silc], writes=[silc16])
        stg = [kb.sbuf("stg%d" % i, [128, 2048], F32) for i in range(2)]
        wst = [0]
        mod1 = kb.sbuf("mod1", [128, 16], F32)
        mb = nb()
        for q4 in range(4):
            for half in range(2):
                s = stg[wst[0] % 2]
                wst[0] += 1
                kb.load("sp", s, s[:].rearrange("p (k c) -> p k c", k=4),
                        wada[half * 512:(half + 1) * 512, q4 * 512:(q4 + 1) * 512].rearrange("(k p) c -> p k c", p=128))
                for dc in range(4):
                    j = q4 * 4 + dc
                    kb.wait("pe", kb.deps([s, silc16], [mb]))
                    for k4 in range(4):
                        kc = half * 4 + k4
                        ins = kb.mm_raw(mb, mb[:, j * 16:(j + 1) * 16], s[:, k4 * 512 + dc * 128:k4 * 512 + (dc + 1) * 128], silc16[:, kc, :])
                    kb.mark_pe(ins, [s, silc16], [mb])
        kb.op("dve", lambda: V.tensor_tensor(out=mod1[:], in0=mb[:, 0:256].rearrange("p (j r) -> p j r", r=16)[:, :, 0], in1=bada_s[:], op=ALU.add),
              reads=[mb, bada_s], writes=[mod1])
        sc1p = kb.sbuf("sc1p", [128, 8], F32)
        kb.op("dve", lambda: V.tensor_scalar(out=sc1p[:], in0=mod1[:, 8:16], scalar1=1.0, scalar2=None, op0=ALU.add),
              reads=[mod1], writes=[sc1p])

        w1_s = kb.sbuf("w1_s", [128, 8, NC1], BF16)
        for kc in range(8):
            load_weight_bf16(kb, stg, w1_s, lambda c0, w, kc=kc: w1_s[:, kc, c0:c0 + w],
                             lambda c0, w, kc=kc: w1[kc * 128:(kc + 1) * 128, c0:c0 + w], 128, NC1, wst)
        wva_s = kb.sbuf("wva_s", [128, 8, 512], BF16)
        for kc in range(8):
            load_weight_bf16(kb, stg, wva_s, lambda c0, w, kc=kc: wva_s[:, kc, c0:c0 + w],
                             lambda c0, w, kc=kc: wva[kc * 128:(kc + 1) * 128, c0:c0 + w], 128, 512, wst)
        wuq_s = kb.sbuf("wuq_s", [128, 2, 1024], BF16)
        for kc in range(2):
            s = stg[wst[0] % 2]
            wst[0] += 1
            kb.load("sp", s, s[:, 0:1024], wuq[kc * 128:(kc + 1) * 128, :])
            kb.op("dve", lambda s=s, kc=kc: V.tensor_scalar(out=wuq_s[:, kc, :], in0=s[:, 0:1024], scalar1=qg_s[:, kc:kc + 1],
                                                            scalar2=None, op0=ALU.mult), reads=[s, qg_s], writes=[wuq_s])
        wukv_s = kb.sbuf("wukv_s", [128, 1024], BF16)
        s = stg[wst[0] % 2]
        wst[0] += 1
        kb.load("sp", s, s[:, 0:1024], wukv)
        kb.op("dve", lambda s=s: V.tensor_scalar(out=wukv_s[:], in0=s[:, 0:1024], scalar1=kvg_s[:, 0:1], scalar2=None, op0=ALU.mult),
              reads=[s, kvg_s], writes=[wukv_s])

        xt = kb.sbuf("xt", [128, NS, D], F32)
        hT = kb.sbuf("hT", [128, 8, TT], BF16)
        st_bf = kb.sbuf("st_bf", [128, 12, TT], BF16)
        st_g = [kb.sbuf("st_g%d" % i, [128, 8, TT], F32) for i in range(2)]
        st_ik = kb.sbuf("st_ik", [64, TT], BF16)
        st_kr = kb.sbuf("st_kr", [32, TT], BF16)
        st_iw = kb.sbuf("st_iw", [8, TT], F32)
        st_qn = kb.sbuf("st_qn", [128, 4, TT], BF16)
        st_qr = kb.sbuf("st_qr", [128, 2, TT], BF16)
        st_kn = kb.sbuf("st_kn", [128, 4, TT], BF16)
        st_va = kb.sbuf("st_va", [128, NS, 512], BF16)
        st_vm = kb.sbuf("st_vm", [128, NS, 512], BF16)
        posi = kb.sbuf("posi", [128, TT], I32)
        ang = kb.sbuf("ang", [128, TT], F32)
        kk = kb.sbuf("kk", [128, TT], F32)
        aab = kb.sbuf("aab", [128, TT], F32)
        Cc = kb.sbuf("Cc", [128, TT], F32)
        Ss = kb.sbuf("Ss", [128, TT], F32)
        cq_raw = kb.sbuf("cq_raw", [128, 3, TT], F32)
        sq = kb.sbuf("sq", [128, 3, TT], BF16)
        rstd = kb.sbuf("rstd", [128, 2, TT], F32)
        cqn = kb.sbuf("cqn", [128, 3, TT], BF16)
        rt1 = kb.sbuf("rt1", [128, TT], F32)
        rt2 = kb.sbuf("rt2", [128, TT], F32)
        ones_bf = kb.sbuf("ones_bf", [128, 128], BF16)
        kb.op("dve", lambda: V.tensor_copy(out=ones_bf[:], in_=ones[:]), reads=[ones], writes=[ones_bf])
        half_pi = kb.sbuf("half_pi", [128, 1], F32)
        kb.op("dve", lambda: V.memset(half_pi[:], float(np.pi / 2)), writes=[half_pi])
        evi = [0]

        def evac_copy(dst_buf, dst_ap, src_buf, src_ap):
            evi[0] += 1
            if evi[0] % 2 == 0:
                return kb.op("act", lambda: A.copy(out=dst_ap, in_=src_ap), reads=[src_buf], writes=[dst_buf])
            return kb.op("dve", lambda: V.tensor_copy(out=dst_ap, in_=src_ap), reads=[src_buf], writes=[dst_buf])

        import os
        PSTOP = int(os.environ.get("P_STOP", "99"))
        for it in range(NT if PSTOP > 1 else 0):
            t0 = it * TT
            kb.load("sp", xt, xt[:], x[t0:t0 + TT, :].rearrange("(s p) d -> p s d", p=128))
            kb.load("sp", posi, posi[:], pos[0:1, t0:t0 + TT].partition_broadcast(128))
            for kc in range(8):
                b = nb()
                kb.wait("pe", kb.deps([xt, ident], [b]))
                for s_ in range(NS):
                    ins = nc.tensor.transpose(out=b[:, s_ * 128:(s_ + 1) * 128], in_=xt[:, s_, kc * 128:(kc + 1) * 128], identity=ident[:])
                kb.mark_pe(ins, [xt, ident], [b])
                kb.op("act", lambda b=b, kc=kc: A.activation(out=hT[:, kc, :], in_=b[:, 0:TT], func=AF.Identity,
                                                             bias=mod1[:, kc:kc + 1], scale=sc1p[:, kc:kc + 1]),
                      reads=[b, mod1, sc1p], writes=[hT])
            kb.op("dve", lambda: V.tensor_copy(out=ang[:], in_=posi[:]), reads=[posi], writes=[ang])
            kb.op("dve", lambda: V.tensor_scalar(out=ang[:], in0=ang[:], scalar1=cst_s[:, 0:1], scalar2=None, op0=ALU.mult),
                  reads=[cst_s], writes=[ang])
            kb.op("dve", lambda: V.tensor_scalar(out=kk[:], in0=ang[:], scalar1=float(1.0 / TWO_PI), scalar2=MAGIC, op0=ALU.mult, op1=ALU.add),
                  reads=[ang], writes=[kk])
            kb.op("dve", lambda: V.tensor_scalar(out=kk[:], in0=kk[:], scalar1=-MAGIC, scalar2=None, op0=ALU.add), writes=[kk])
            kb.op("dve", lambda: V.scalar_tensor_tensor(out=ang[:], in0=kk[:], scalar=-6.28125, in1=ang[:], op0=ALU.mult, op1=ALU.add),
                  reads=[kk], writes=[ang])
            kb.op("dve", lambda: V.scalar_tensor_tensor(out=ang[:], in0=kk[:], scalar=-0.0019353071795864769, in1=ang[:], op0=ALU.mult, op1=ALU.add),
                  reads=[kk], writes=[ang])
            kb.op("dve", lambda: V.tensor_scalar(out=ang[:], in0=ang[:], scalar1=float(np.pi), scalar2=float(-np.pi), op0=ALU.min, op1=ALU.max),
                  writes=[ang])
            kb.op("dve", lambda: V.scalar_tensor_tensor(out=aab[:], in0=ang[:], scalar=-1.0, in1=ang[:], op0=ALU.mult, op1=ALU.max),
                  reads=[ang], writes=[aab])
            kb.op("act", lambda: A.activation(out=Ss[:], in_=ang[:], func=AF.Sin, scale=cst_s[:, 1:2]), reads=[ang, cst_s], writes=[Ss])
            kb.op("act", lambda: A.activation(out=Cc[:], in_=aab[:], func=AF.Sin, scale=-1.0, bias=half_pi[:]), reads=[aab, half_pi], writes=[Cc])

            if PSTOP == 2:
                continue

            def fm_block(c0, m, kcs=8, lhs=None, rhs=None):
                b = nb()
                lhs = lhs or (lambda kc: w1_s[:, kc, c0:c0 + m])
                rhs = rhs or (lambda kc: hT[:, kc, :])
                kb.mm(b[0:m, 0:TT], [(lhs(kc), rhs(kc)) for kc in range(kcs)], [hT, w1_s, wuq_s, wukv_s, cqn], b)
                return b

            for blk in range(12):
                b = fm_block(blk * 128, 128)
                evac_copy(st_bf, st_bf[:, blk, :], b, b[:, 0:TT])
            kb.store("sp", st_bf, o_qa[:, t0:t0 + TT].rearrange("(b p) t -> p b t", p=128), st_bf[:, 0:4, :])
            kb.store("sp", st_bf, o_ka[:, t0:t0 + TT].rearrange("(b p) t -> p b t", p=128), st_bf[:, 4:8, :])
            kb.store("sp", st_bf, o_iq[:, t0:t0 + TT].rearrange("(b p) t -> p b t", p=128), st_bf[:, 8:12, :])
            if PSTOP == 3:
                continue
            for gi, og in enumerate((o_ga, o_gb)):
                sg = st_g[gi]
                for blk in range(8):
                    b = fm_block(1536 + gi * 1024 + blk * 128, 128)
                    kb.op("act", lambda b=b, blk=blk, sg=sg: A.activation(out=sg[:, blk, :], in_=b[:, 0:TT], func=AF.Sigmoid),
                          reads=[b], writes=[sg])
                kb.store("sp", sg, og[:, t0:t0 + TT].rearrange("(b p) t -> p b t", p=128), sg[:])
            for j in range(3):
                b = fm_block(3584 + j * 128, 128)
                kb.op("dve", lambda b=b, j=j: V.tensor_copy(out=cq_raw[:, j, :], in_=b[:, 0:TT]), reads=[b], writes=[cq_raw])
                kb.op("dve", lambda j=j: V.tensor_tensor(out=sq[:, j, :], in0=cq_raw[:, j, :], in1=cq_raw[:, j, :], op=ALU.mult), reads=[cq_raw], writes=[sq])
            for j, (lo, hi, n) in enumerate(((0, 2, 256.0), (2, 3, 128.0))):
                b = nb()
                kb.mm(b[:, 0:TT], [(ones_bf[:], sq[:, c, :]) for c in range(lo, hi)], [ones_bf, sq], b)
                kb.op("dve", lambda b=b, n=n: V.tensor_scalar(out=rt1[:], in0=b[:, 0:TT], scalar1=float(1.0 / n), scalar2=float(RMS_EPS), op0=ALU.mult, op1=ALU.add),
                      reads=[b], writes=[rt1])
                kb.op("act", lambda: A.activation(out=rt2[:], in_=rt1[:], func=AF.Sqrt), reads=[rt1], writes=[rt2])
                kb.op("dve", lambda j=j: V.reciprocal(out=rstd[:, j, :], in_=rt2[:]), reads=[rt2], writes=[rstd])
                for c in range(lo, hi):
                    kb.op("dve", lambda c=c, j=j: V.tensor_tensor(out=cqn[:, c, :], in0=cq_raw[:, c, :], in1=rstd[:, j, :], op=ALU.mult),
                          reads=[cq_raw, rstd], writes=[cqn])
            if PSTOP == 5:
                continue
            b = fm_block(3968, 64)
            evac_copy(st_ik, st_ik[:], b, b[0:64, 0:TT])
            kb.store("sp", st_ik, o_ik[:, t0:t0 + TT], st_ik[:])
            b1 = fm_block(4032, 32)
            b2 = fm_block(4064, 32)
            kb.op("dve", lambda: V.tensor_tensor(out=rt1[0:32, :], in0=b1[0:32, 0:TT], in1=Cc[0:32, :], op=ALU.mult), reads=[b1, Cc], writes=[rt1])
            kb.op("dve", lambda: V.tensor_tensor(out=rt2[0:32, :], in0=b2[0:32, 0:TT], in1=Ss[0:32, :], op=ALU.mult), reads=[b2, Ss], writes=[rt2])
            kb.op("dve", lambda: V.tensor_tensor(out=st_kr[:], in0=rt1[0:32, :], in1=rt2[0:32, :], op=ALU.add), reads=[rt1, rt2], writes=[st_kr])
            kb.store("sp", st_kr, o_kr[:, t0:t0 + TT], st_kr[:])
            b = fm_block(4096, 8)
            evac_copy(st_iw, st_iw[:], b, b[0:8, 0:TT])
            kb.store("sp", st_iw, o_iw[:, t0:t0 + TT], st_iw[:])
            if PSTOP == 6:
                continue
            for s_ in range(NS):
                b = nb()
                kb.mm(b[:, :], [(hT[:, kc, s_ * 128:(s_ + 1) * 128], wva_s[:, kc, :]) for kc in range(8)], [hT, wva_s], b)
                evac_copy(st_va, st_va[:, s_, :], b, b[:, :])
            kb.store("sp", st_va, o_va[t0:t0 + TT, :].rearrange("(s p) c -> p s c", p=128), st_va[:])
            for blk in range(4):
                b = fm_block(0, 128, kcs=2, lhs=lambda kc, blk=blk: wuq_s[:, kc, blk * 128:(blk + 1) * 128], rhs=lambda kc: cqn[:, kc, :])
                evac_copy(st_qn, st_qn[:, blk, :], b, b[:, 0:TT])
            kb.store("sp", st_qn, o_qn[:, t0:t0 + TT].rearrange("(b p) t -> p b t", p=128), st_qn[:])
            for blk in range(2):
                b1 = fm_block(0, 128, kcs=2, lhs=lambda kc, blk=blk: wuq_s[:, kc, 512 + blk * 128:512 + (blk + 1) * 128], rhs=lambda kc: cqn[:, kc, :])
                b2 = fm_block(0, 128, kcs=2, lhs=lambda kc, blk=blk: wuq_s[:, kc, 768 + blk * 128:768 + (blk + 1) * 128], rhs=lambda kc: cqn[:, kc, :])
                kb.op("dve", lambda b1=b1: V.tensor_tensor(out=rt1[:], in0=b1[:, 0:TT], in1=Cc[:], op=ALU.mult), reads=[b1, Cc], writes=[rt1])
                kb.op("dve", lambda b2=b2: V.tensor_tensor(out=rt2[:], in0=b2[:, 0:TT], in1=Ss[:], op=ALU.mult), reads=[b2, Ss], writes=[rt2])
                kb.op("dve", lambda blk=blk: V.tensor_tensor(out=st_qr[:, blk, :], in0=rt1[:], in1=rt2[:], op=ALU.add), reads=[rt1, rt2], writes=[st_qr])
            kb.store("sp", st_qr, o_qr[:, t0:t0 + TT].rearrange("(b p) t -> p b t", p=128), st_qr[:])
            for blk in range(4):
                b = fm_block(0, 128, kcs=1, lhs=lambda kc, blk=blk: wukv_s[:, blk * 128:(blk + 1) * 128], rhs=lambda kc: cqn[:, 2, :])
                evac_copy(st_kn, st_kn[:, blk, :], b, b[:, 0:TT])
            kb.store("sp", st_kn, o_kn[:, t0:t0 + TT].rearrange("(b p) t -> p b t", p=128), st_kn[:])
            for s_ in range(NS):
                b = nb()
                kb.mm(b[:, :], [(cqn[:, 2, s_ * 128:(s_ + 1) * 128], wukv_s[:, 512:1024])], [cqn, wukv_s], b)
                evac_copy(st_vm, st_vm[:, s_, :], b, b[:, :])
            kb.store("sp", st_vm, o_vm[t0:t0 + TT, :].rearrange("(s p) c -> p s c", p=128), st_vm[:])
        kb.finish()
    return nc


def build_A(S):
    T = S // 4
    NSUB = T // 128
    NSB = T // 512
    KL = S + 1536
    NKT = KL // 128
    NKMAX = 2048 * NSB
    ISC = float(64 ** -0.5 * 8 ** -0.5)
    nc = bass.Bass("TRN2", target_bir_lowering=False)
    iq_d = _dram_in(nc, "iq", [64, NSUB, 8, 128], BF16)
    iw_d = _dram_in(nc, "iw", [T, 8], F32)
    qa_d = _dram_in(nc, "qa", [64, NSUB, 8, 128], BF16)
    qm_d = _dram_in(nc, "qm", [96, NSUB, 8, 128], BF16)
    ik_d = _dram_in(nc, "ik", [64, KL], BF16)
    ka_d = _dram_in(nc, "ka", [64, NKT, 8, 128], BF16)
    va_d = _dram_in(nc, "va", [128, NKT, 8, 65], BF16)
    km_d = _dram_in(nc, "km", [96, NKT, 8, 128], BF16)
    vm_d = _dram_in(nc, "vm", [128, NKT, 8, 65], BF16)
    padb_d = _dram_in(nc, "padb", [128, 1536], F32)
    cb_d = _dram_in(nc, "cb", [128, 4, 512], F32)
    tri_d = _dram_in(nc, "tri4", [128, 512], BF16)
    g0_d = _dram_in(nc, "g0", [128, 1024], F32)
    g1_d = _dram_in(nc, "g1", [128, 1024], F32)
    b31_d = _dram_in(nc, "b31", [128, 8], F32)
    negI_d = _dram_in(nc, "negI", [128, 128], BF16)
    pow2_d = _dram_in(nc, "pow2", [128, NIT], F32)
    ya_d = _dram_out(nc, "ya", [64, NSUB, 8, 128], BF16)
    yb_d = _dram_out(nc, "yb", [64, NSUB, 8, 128], BF16)
    scr = _dram_out(nc, "scr", [NSUB * 2, 1024], F32)

    with ExitStack() as es:
        kb = KB(nc, es)
        V, A, G = nc.vector, nc.scalar, nc.gpsimd
        wbanks = [kb.psum("wb%d" % i) for i in range(4)]
        accs = [kb.psum("acc%d" % i) for i in range(4)]
        wi = [0]

        def wb():
            b = wbanks[wi[0] % 4]
            wi[0] += 1
            b.fresh = True
            return b

        def ld_const(name, shape, dt, src):
            b = kb.sbuf(name, shape, dt)
            kb.load("sp", b, b[:], src)
            return b

        padb = ld_const("padb_s", [128, 1536], F32, padb_d)
        cb = ld_const("cb_s", [128, 4, 512], F32, cb_d)
        tri4 = ld_const("tri4_s", [128, 512], BF16, tri_d)
        g0 = ld_const("g0_s", [128, 1024], F32, g0_d)
        g1 = ld_const("g1_s", [128, 1024], F32, g1_d)
        b31 = ld_const("b31_s", [128, 8], F32, b31_d)
        negI = ld_const("negI_s", [128, 128], BF16, negI_d)
        pow2 = ld_const("pow2_s", [128, NIT], F32, pow2_d)
        nb31 = kb.sbuf("nb31", [128, 8], F32)
        kb.op("dve", lambda: V.tensor_scalar(out=nb31[:], in0=b31[:], scalar1=-1.0, scalar2=None, op0=ALU.mult), reads=[b31], writes=[nb31])
        E0 = kb.sbuf("E0", [128, 1024], BF16)
        E1 = kb.sbuf("E1", [128, 1024], BF16)
        for h in range(8):
            kb.op("act", lambda h=h: A.activation(out=E0[:, h * 128:(h + 1) * 128], in_=g0[:, h * 128:(h + 1) * 128], func=AF.Exp,
                                                  bias=nb31[:, h:h + 1], scale=1.0), reads=[g0, nb31], writes=[E0])
            kb.op("act", lambda h=h: A.activation(out=E1[:, h * 128:(h + 1) * 128], in_=g1[:, h * 128:(h + 1) * 128], func=AF.Exp,
                                                  bias=nb31[:, h:h + 1], scale=1.0), reads=[g1, nb31], writes=[E1])

        Ib = kb.sbuf("I", [128, NKMAX], F32)
        nm = kb.sbuf("nm", [128, NKMAX], BF16)
        ikc = [kb.sbuf("ikc%d" % i, [64, 2048], BF16) for i in range(2)]
        kvK = [kb.sbuf("kvK%d" % i, [96, 4, 8, 128], BF16) for i in range(2)]
        kvV = [kb.sbuf("kvV%d" % i, [128, 4, 8, 65], BF16) for i in range(2)]
        iq_s = kb.sbuf("iq_s", [64, 8, 128], BF16)
        qa_s = kb.sbuf("qa_s", [64, 8, 128], BF16)
        qm_s = kb.sbuf("qm_s", [96, 8, 128], BF16)
        iw_s = kb.sbuf("iw_s", [128, 8], F32)
        aw = kb.sbuf("aw", [128, 8], F32)
        sg = kb.sbuf("sg", [128, 8], F32)
        tmp = [kb.sbuf("tmp%d" % i, [128, 512], F32) for i in range(3)]
        Pb = [kb.sbuf("P%d" % i, [128, 512], BF16) for i in range(3)]
        mn = kb.sbuf("mn", [128, 1], F32)
        mx = kb.sbuf("mx", [128, 1], F32)
        wd = kb.sbuf("wd", [128, 1], F32)
        lo = kb.sbuf("lo", [128, 1], F32)
        mid = kb.sbuf("mid", [128, 1], F32)
        gg = kb.sbuf("gg", [128, 1], F32)
        steps = kb.sbuf("steps", [128, NIT], F32)
        cnt = kb.sbuf("cnt", [128, NIT], F32)
        rsum = [kb.sbuf("rsum0", [128, 1024], F32)] * 2
        bcs = [kb.sbuf("bcs0", [64, 1024], F32)] * 2
        junk8 = kb.sbuf("junk8", [128, NKMAX], mybir.dt.uint8)
        ybuf = [kb.sbuf("ybuf%d" % i, [64, 1024], BF16) for i in range(2)]
        ctr = dict(t=0, p=0, g=0, kv=0)

        def attend(sb, nkt, q_s, krows, Kd, Vd, scale, masked, accA, accB, br):
            nch = (nkt + 3) // 4
            bufs = {}
            accA.fresh = True
            accB.fresh = True

            def issue(c):
                i = ctr["kv"] % 2
                ctr["kv"] += 1
                n = min(4, nkt - 4 * c)
                kb.load("sp", kvK[i], kvK[i][0:krows, 0:n], Kd[:, 4 * c:4 * c + n])
                kb.load("sp", kvV[i], kvV[i][:, 0:n], Vd[:, 4 * c:4 * c + n])
                bufs[c] = (kvK[i], kvV[i], n)

            steps = []
            for c in range(nch):
                n_ = min(4, nkt - 4 * c)
                for w in range(n_):
                    for half in range(2):
                        steps.append((c, w, half, (w == n_ - 1 and half == 1)))
            for c in range(min(2, nch)):
                issue(c)
            DEPTH = 2
            pend = []

            def emit_pv(st, p):
                c, w, half, last = st
                kbuf, vbuf, n = bufs[c]
                acc = accA if half == 0 else accB
                kb.wait("pe", kb.deps([p, vbuf], [acc]))
                ins = None
                for hh in range(4):
                    h = 4 * half + hh
                    ins = kb.mm_raw(acc, acc[0:65, hh * 128:(hh + 1) * 128], vbuf[:, w, h, :], p[:, hh * 128:(hh + 1) * 128])
                kb.mark_pe(ins, [p, vbuf], [acc])
                if last and c + 2 < nch:
                    issue(c + 2)

            for st in steps:
                c, w, half, last = st
                kbuf, vbuf, n = bufs[c]
                kt = 4 * c + w
                b = wb()
                rds = [q_s, kbuf] + ([nm, negI] if masked else [])
                kb.wait("pe", kb.deps(rds, [b]))
                ins = None
                for hh in range(4):
                    h = 4 * half + hh
                    ins = kb.mm_raw(b, b[:, hh * 128:(hh + 1) * 128], kbuf[0:krows, w, h, :], q_s[0:krows, h, :])
                    if masked:
                        ins = kb.mm_raw(b, b[:, hh * 128:(hh + 1) * 128], nm[:, kt * 128:(kt + 1) * 128], negI[:])
                kb.mark_pe(ins, rds, [b])
                p = Pb[ctr["p"] % 3]
                ctr["p"] += 1
                kb.op("act", lambda p=p, b=b: A.activation(out=p[:], in_=b[:], func=AF.Exp, scale=scale), reads=[b], writes=[p])
                tab = None
                if masked and kt == nkt - 1:
                    tab = E0
                elif masked and kt == nkt - 2:
                    tab = E1
                elif (not masked) and kt == nkt - 1:
                    kb.op("pool", lambda p=p: G.tensor_tensor(out=p[:], in0=p[:], in1=tri4[:], op=ALU.mult), reads=[tri4], writes=[p])
                if tab is not None:
                    kb.op("pool", lambda p=p, tab=tab, half=half: G.tensor_tensor(out=p[:], in0=p[:], in1=tab[:, half * 512:(half + 1) * 512], op=ALU.mult),
                          reads=[tab], writes=[p])
                pend.append((st, p))
                if len(pend) > DEPTH:
                    emit_pv(*pend.pop(0))
            while pend:
                emit_pv(*pend.pop(0))
            return lambda: attend_norm(sb, accA, accB, br)

        def attend_norm(sb, accA, accB, br):
            rs, bc, y = rsum[br], bcs[br], ybuf[br]
            kb.op("dve", lambda: V.reciprocal(out=rs[64:65, 0:512], in_=accA[64:65, :]), reads=[accA], writes=[rs])
            kb.op("dve", lambda: V.reciprocal(out=rs[64:65, 512:1024], in_=accB[64:65, :]), reads=[accB], writes=[rs])
            row = sb * 2 + br
            stok = kb.store("sp", rs, scr[row:row + 1, :], rs[64:65, :])
            kb.wait("sp", [stok])
            kb.load("sp", bc, bc[:], scr[row:row + 1, :].partition_broadcast(64))
            kb.op("dve", lambda: V.tensor_tensor(out=y[:, 0:512], in0=accA[0:64, :], in1=bc[:, 0:512], op=ALU.mult), reads=[accA, bc], writes=[y])
            kb.op("dve", lambda: V.tensor_tensor(out=y[:, 512:1024], in0=accB[0:64, :], in1=bc[:, 512:1024], op=ALU.mult), reads=[accB, bc], writes=[y])
            yd = ya_d if br == 0 else yb_d
            kb.store("sp", y, yd[:, sb].rearrange("p h t -> p (h t)"), y[:])

        def phase_idx(sb):
            j, tq = sb // 4, sb % 4
            Nk = 2048 * (j + 1)
            nkt = 16 * (j + 1) - 3 + tq
            kb.load("sp", iq_s, iq_s[:], iq_d[:, sb])
            kb.load("sp", iw_s, iw_s[:], iw_d[sb * 128:(sb + 1) * 128, :])
            kb.op("dve", lambda: V.tensor_scalar(out=sg[:], in0=iw_s[:], scalar1=0.0, scalar2=2.0, op0=ALU.is_ge, op1=ALU.mult), reads=[iw_s], writes=[sg])
            kb.op("dve", lambda: V.tensor_scalar(out=sg[:], in0=sg[:], scalar1=-1.0, scalar2=None, op0=ALU.add), writes=[sg])
            kb.op("dve", lambda: V.tensor_tensor(out=aw[:], in0=iw_s[:], in1=sg[:], op=ALU.mult), reads=[iw_s, sg], writes=[aw])
            kb.op("dve", lambda: V.tensor_scalar(out=aw[:], in0=aw[:], scalar1=ISC, scalar2=None, op0=ALU.mult), writes=[aw])
            for g in range(j + 1):
                ikb = ikc[ctr["g"] % 2]
                ctr["g"] += 1
                kb.load("sp", ikb, ikb[:], ik_d[:, g * 2048:(g + 1) * 2048])
                for c4 in range(4):
                    c = g * 4 + c4
                    for h in range(8):
                        b = wb()
                        kb.mm(b[:, :], [(iq_s[:, h, :], ikb[:, c4 * 512:(c4 + 1) * 512])], [iq_s, ikb], b)
                        t = tmp[ctr["t"] % 3]
                        ctr["t"] += 1
                        kb.op("act", lambda t=t, b=b, h=h: A.activation(out=t[:], in_=b[:], func=AF.Relu, scale=aw[:, h:h + 1]), reads=[b, aw], writes=[t])
                        if h == 0:
                            kb.op("dve", lambda t=t, c=c: V.tensor_scalar(out=Ib[:, c * 512:(c + 1) * 512], in0=t[:], scalar1=sg[:, 0:1], scalar2=None, op0=ALU.mult),
                                  reads=[t, sg], writes=[Ib])
                        else:
                            kb.op("dve", lambda t=t, c=c, h=h: V.scalar_tensor_tensor(out=Ib[:, c * 512:(c + 1) * 512], in0=t[:], scalar=sg[:, h:h + 1],
                                                                                     in1=Ib[:, c * 512:(c + 1) * 512], op0=ALU.mult, op1=ALU.add),
                                  reads=[t, sg], writes=[Ib])

        def phase_thr(sb):
            j, tq = sb // 4, sb % 4
            Nk = 2048 * (j + 1)
            nkt = 16 * (j + 1) - 3 + tq
            kb.op("dve", lambda: V.tensor_reduce(out=mn[:], in_=Ib[:, 0:Nk], axis=AX.X, op=ALU.min), reads=[Ib], writes=[mn])
            kb.op("dve", lambda: V.tensor_tensor(out=Ib[:, 0:1536], in0=Ib[:, 0:1536], in1=padb[:], op=ALU.add), reads=[padb], writes=[Ib])
            kb.op("dve", lambda: V.tensor_tensor(out=Ib[:, Nk - 512:Nk], in0=Ib[:, Nk - 512:Nk], in1=cb[:, tq, :], op=ALU.add), reads=[cb], writes=[Ib])
            kb.op("dve", lambda: V.tensor_reduce(out=mx[:], in_=Ib[:, 0:Nk], axis=AX.X, op=ALU.max), reads=[Ib], writes=[mx])
            kb.op("dve", lambda: V.tensor_tensor(out=wd[:], in0=mx[:], in1=mn[:], op=ALU.subtract), reads=[mx, mn], writes=[wd])
            kb.op("dve", lambda: V.tensor_scalar(out=steps[:], in0=pow2[:], scalar1=wd[:, 0:1], scalar2=None, op0=ALU.mult), reads=[pow2, wd], writes=[steps])
            kb.op("dve", lambda: V.tensor_copy(out=lo[:], in_=mn[:]), reads=[mn], writes=[lo])
            kb.op("dve", lambda: V.memset(cnt[:], 0.0), writes=[cnt])
            for k in range(NIT):
                kb.op("dve", lambda k=k: V.tensor_tensor(out=mid[:], in0=lo[:], in1=steps[:, k:k + 1], op=ALU.add), reads=[lo, steps], writes=[mid])
                kb.op("dve", lambda k=k: V.tensor_scalar(out=junk8[:, 0:Nk], in0=Ib[:, 0:Nk], scalar1=mid[:, 0:1], scalar2=0.0, op0=ALU.is_ge, op1=ALU.add,
                                                         accum_out=cnt[:, k:k + 1]), reads=[Ib, mid], writes=[junk8, cnt])
                kb.op("dve", lambda k=k: V.tensor_scalar(out=gg[:], in0=cnt[:, k:k + 1], scalar1=255.5, scalar2=None, op0=ALU.is_gt), reads=[cnt], writes=[gg])
                kb.op("dve", lambda k=k: V.scalar_tensor_tensor(out=lo[:], in0=gg[:], scalar=steps[:, k:k + 1], in1=lo[:], op0=ALU.mult, op1=ALU.add),
                      reads=[gg, steps], writes=[lo])

        def phase_nm(sb):
            j, tq = sb // 4, sb % 4
            Nk = 2048 * (j + 1)
            nkt = 16 * (j + 1) - 3 + tq
            kb.op("dve", lambda: V.tensor_scalar(out=nm[:, 0:nkt * 128], in0=Ib[:, 0:nkt * 128], scalar1=lo[:, 0:1], scalar2=None, op0=ALU.is_lt),
                  reads=[Ib, lo], writes=[nm])

        def phase_att(sb):
            j, tq = sb // 4, sb % 4
            Nk = 2048 * (j + 1)
            nkt = 16 * (j + 1) - 3 + tq
            kb.load("sp", qa_s, qa_s[:], qa_d[:, sb])
            kb.load("sp", qm_s, qm_s[:], qm_d[:, sb])
            n1 = attend(sb, nkt, qm_s, 96, km_d, vm_d, float(96 ** -0.5), False, accs[2], accs[3], 1)
            n0 = attend(sb, nkt, qa_s, 64, ka_d, va_d, 0.125, True, accs[0], accs[1], 0)
            n1()
            n0()

        phase_idx(0)
        phase_thr(0)
        phase_nm(0)
        for sb in range(NSUB):
            if sb + 1 < NSUB:
                phase_idx(sb + 1)
                phase_thr(sb + 1)
            phase_att(sb)
            if sb + 1 < NSUB:
                phase_nm(sb + 1)
        kb.finish()
    return nc


def adaln_cols(kb, nc, stg, wst, silc, wada2, bada_s, mb, mod_out):
    for q4 in range(4):
        for half in range(2):
            s = stg[wst[0] % 2]
            wst[0] += 1
            kb.load("sp", s, s[:].rearrange("p (k c) -> p k c", k=4),
                    wada2[half * 512:(half + 1) * 512, q4 * 512:(q4 + 1) * 512].rearrange("(k p) c -> p k c", p=128))
            for dc in range(4):
                j = q4 * 4 + dc
                kb.wait("pe", kb.deps([s, silc], [mb]))
                ins = None
                for k4 in range(4):
                    kc = half * 4 + k4
                    ins = kb.mm_raw(mb, mb[:, j * 16:(j + 1) * 16], s[:, k4 * 512 + dc * 128:k4 * 512 + (dc + 1) * 128], silc[:, kc, 0:16])
                kb.mark_pe(ins, [s, silc], [mb])
    kb.op("dve", lambda: nc.vector.tensor_tensor(out=mod_out[:], in0=mb[:, 0:256].rearrange("p (j r) -> p j r", r=16)[:, :, 0], in1=bada_s[:], op=ALU.add),
          reads=[mb, bada_s], writes=[mod_out])


def adaln_bcast(kb, nc, stg, wst, rep, wg, bg_bc, b0, b1, out_bc):
    for kc in range(8):
        s = stg[wst[0] % 2]
        wst[0] += 1
        kb.load("sp", s, s[:, 0:1024], wg[kc * 128:(kc + 1) * 128, :])
        for half, b in enumerate((b0, b1)):
            kb.wait("pe", kb.deps([s, rep], [b]))
            ins = kb.mm_raw(b, b[:, :], rep[:, kc, :], s[:, half * 512:(half + 1) * 512])
            kb.mark_pe(ins, [s, rep], [b])
    for half, b in enumerate((b0, b1)):
        kb.op("dve", lambda half=half, b=b: nc.vector.tensor_tensor(out=out_bc[:, half * 512:(half + 1) * 512], in0=b[:, :],
                                                                   in1=bg_bc[:, half * 512:(half + 1) * 512], op=ALU.add),
              reads=[b, bg_bc], writes=[out_bc])


def build_F(T):
    TH = T + 128
    NTH = TH // 128
    nc = bass.Bass("TRN2", target_bir_lowering=False)
    x_h = _dram_in(nc, "x_h", [TH, D], F32)
    ya_f = _dram_in(nc, "ya_f", [512, TH], BF16)
    yb_f = _dram_in(nc, "yb_f", [512, TH], BF16)
    sga = _dram_in(nc, "sga", [D, TH], F32)
    sgb = _dram_in(nc, "sgb", [D, TH], F32)
    ccol = _dram_in(nc, "ccol", [128, 8], F32)
    wada_g1 = _dram_in(nc, "wada_g1", [D, D], F32)
    bada_g1 = _dram_in(nc, "bada_g1", [1, D], F32)
    wada_2 = _dram_in(nc, "wada_2", [D, 2048], F32)
    bada_2 = _dram_in(nc, "bada_2", [128, 16], F32)
    wada_g2 = _dram_in(nc, "wada_g2", [D, D], F32)
    bada_g2 = _dram_in(nc, "bada_g2", [1, D], F32)
    wba = _dram_in(nc, "wba", [512, D], F32)
    wbb = _dram_in(nc, "wbb", [512, D], F32)
    wout = _dram_in(nc, "wout", [D, D], F32)
    ln1g = _dram_in(nc, "ln1g", [1, D], F32)
    ln1b = _dram_in(nc, "ln1b", [1, D], F32)
    wup = _dram_in(nc, "wup", [D, 2 * DFF], F32)
    cw = _dram_in(nc, "cw", [128, 44, 3], F32)
    cbias = _dram_in(nc, "cbias", [128, 44], F32)
    wdn = _dram_in(nc, "wdn", [DFF, D], F32)
    ln2g = _dram_in(nc, "ln2g", [1, D], F32)
    ln2b = _dram_in(nc, "ln2b", [1, D], F32)
    flag = _dram_in(nc, "flag", [128, 1], F32)
    identd = _dram_in(nc, "ident", [128, 128], F32)
    onesd = _dram_in(nc, "ones", [128, 128], F32)
    o_x1 = _dram_out(nc, "o_x1", [TH, D], F32)
    o_out = _dram_out(nc, "o_out", [T, D], F32)

    with ExitStack() as es:
        kb = KB(nc, es)
        V, A, G = nc.vector, nc.scalar, nc.gpsimd
        banks = [kb.psum("bank%d" % i) for i in range(8)]
        bi = [0]

        def nb():
            b = banks[bi[0] % 8]
            bi[0] += 1
            b.fresh = True
            return b

        def ld_const(name, shape, dt, src, es_=None):
            b = kb.sbuf(name, shape, dt, es_)
            kb.load("sp", b, b[:], src)
            return b

        ident = ld_const("ident_s", [128, 128], F32, identd)
        ones = ld_const("ones_s", [128, 128], F32, onesd)
        ccol_s = ld_const("ccol_s", [128, 8], F32, ccol)
        flag_s = ld_const("flag_s", [128, 1], F32, flag)
        bada2_s = ld_const("bada2_s", [128, 16], F32, bada_2)
        cw_s = ld_const("cw_s", [128, 44, 3], F32, cw)
        cbias_s = ld_const("cbias_s", [128, 44], F32, cbias)
        eps_s = kb.sbuf("eps_s", [128, 1], F32)
        kb.op("dve", lambda: V.memset(eps_s[:], LN_EPS), writes=[eps_s])
        silc = kb.sbuf("silc", [128, 8], F32)
        kb.op("act", lambda: A.activation(out=silc[:], in_=ccol_s[:], func=AF.Silu), reads=[ccol_s], writes=[silc])
        rep = kb.sbuf("rep", [128, 8, 128], F32)
        for kc in range(8):
            kb.op("dve", lambda kc=kc: V.tensor_scalar(out=rep[:, kc, :], in0=ones[:], scalar1=silc[:, kc:kc + 1], scalar2=None, op0=ALU.mult),
                  reads=[ones, silc], writes=[rep])
        stg = [kb.sbuf("stg%d" % i, [128, 2048], F32) for i in range(2)]
        wst = [0]
        mod2 = kb.sbuf("mod2", [128, 16], F32)
        adaln_cols(kb, nc, stg, wst, rep, wada_2, bada2_s, nb(), mod2)
        sc2p = kb.sbuf("sc2p", [128, 8], F32)
        kb.op("dve", lambda: V.tensor_scalar(out=sc2p[:], in0=mod2[:, 8:16], scalar1=1.0, scalar2=None, op0=ALU.add), reads=[mod2], writes=[sc2p])
        g1bc = kb.sbuf("g1bc", [128, D], F32)
        g2bc = kb.sbuf("g2bc", [128, D], F32)
        btmp = kb.sbuf("btmp", [128, D], F32)
        kb.load("sp", btmp, btmp[:], bada_g1.partition_broadcast(128))
        adaln_bcast(kb, nc, stg, wst, rep, wada_g1, btmp, nb(), nb(), g1bc)
        kb.load("sp", btmp, btmp[:], bada_g2.partition_broadcast(128))
        adaln_bcast(kb, nc, stg, wst, rep, wada_g2, btmp, nb(), nb(), g2bc)

        st = kb.sbuf("st", [128, 2, 6], F32)
        mv = kb.sbuf("mv", [128, 4], F32)
        tmpy = kb.sbuf("tmpy", [128, 512], F32)

        def deepnorm_ln(src_bank_fn, resid, gbc, lng, lnb, r, xn, dst):
            for half in range(2):
                b = src_bank_fn(half)
                sl = slice(half * 512, (half + 1) * 512)
                kb.op("dve", lambda b=b, sl=sl: V.tensor_tensor(out=tmpy[:], in0=b[:, :], in1=gbc[:, sl], op=ALU.mult), reads=[b, gbc], writes=[tmpy])
                kb.op("dve", lambda sl=sl: V.scalar_tensor_tensor(out=r[:, sl], in0=resid[:, sl], scalar=float(ALPHA), in1=tmpy[:], op0=ALU.mult, op1=ALU.add),
                      reads=[resid, tmpy], writes=[r])
                kb.op("dve", lambda half=half, sl=sl: V.bn_stats(out=st[:, half, :], in_=r[:, sl]), reads=[r], writes=[st])
            kb.op("dve", lambda: V.bn_aggr(out=mv[:, 0:2], in_=st[:]), reads=[st], writes=[mv])
            kb.op("act", lambda: A.activation(out=mv[:, 2:3], in_=mv[:, 1:2], func=AF.Sqrt, bias=eps_s[:], scale=1.0), reads=[eps_s], writes=[mv])
            kb.op("dve", lambda: V.reciprocal(out=mv[:, 3:4], in_=mv[:, 2:3]), writes=[mv])
            kb.op("dve", lambda: V.tensor_scalar(out=xn[:], in0=r[:], scalar1=mv[:, 0:1], scalar2=mv[:, 3:4], op0=ALU.subtract, op1=ALU.mult),
                  reads=[r, mv], writes=[xn])
            kb.op("pool", lambda: G.tensor_tensor(out=xn[:], in0=xn[:], in1=lng[:], op=ALU.mult), reads=[lng], writes=[xn])
            kb.op("pool", lambda: G.tensor_tensor(out=dst[:], in0=xn[:], in1=lnb[:], op=ALU.add), reads=[xn, lnb], writes=[dst])

        with ExitStack() as es1:
            wba_s = kb.sbuf("wba_s", [128, 4, D], BF16, es1)
            wbb_s = kb.sbuf("wbb_s", [128, 4, D], BF16, es1)
            wout_s = kb.sbuf("wout_s", [128, 8, D], BF16, es1)
            for kc in range(4):
                load_weight_bf16(kb, stg, wba_s, lambda c0, w, kc=kc: wba_s[:, kc, c0:c0 + w], lambda c0, w, kc=kc: wba[kc * 128:(kc + 1) * 128, c0:c0 + w], 128, D, wst)
                load_weight_bf16(kb, stg, wbb_s, lambda c0, w, kc=kc: wbb_s[:, kc, c0:c0 + w], lambda c0, w, kc=kc: wbb[kc * 128:(kc + 1) * 128, c0:c0 + w], 128, D, wst)
            for kc in range(8):
                load_weight_bf16(kb, stg, wout_s, lambda c0, w, kc=kc: wout_s[:, kc, c0:c0 + w], lambda c0, w, kc=kc: wout[kc * 128:(kc + 1) * 128, c0:c0 + w], 128, D, wst)
            lng = kb.sbuf("ln1g_s", [128, D], F32, es1)
            kb.load("sp", lng, lng[:], ln1g.partition_broadcast(128))
            lnb = kb.sbuf("ln1b_s", [128, D], F32, es1)
            kb.load("sp", lnb, lnb[:], ln1b.partition_broadcast(128))
            ya_t = kb.sbuf("ya_t", [128, 4, 128], BF16, es1)
            yb_t = kb.sbuf("yb_t", [128, 4, 128], BF16, es1)
            sga_t = kb.sbuf("sga_t", [128, 8, 128], F32, es1)
            sgb_t = kb.sbuf("sgb_t", [128, 8, 128], F32, es1)
            x_t = kb.sbuf("x_t", [128, D], F32, es1)
            mT = kb.sbuf("mT", [128, 8, 128], BF16, es1)
            t1 = kb.sbuf("t1", [128, 128], F32, es1)
            t2 = kb.sbuf("t2", [128, 128], F32, es1)
            r1 = kb.sbuf("r1", [128, D], F32, es1)
            xn1 = kb.sbuf("xn1", [128, D], F32, es1)
            x1o = kb.sbuf("x1o", [128, D], F32, es1)
            for i in range(NTH):
                t0 = i * 128
                kb.load("sp", ya_t, ya_t[:], ya_f[:, t0:t0 + 128].rearrange("(k p) t -> p k t", p=128))
                kb.load("sp", yb_t, yb_t[:], yb_f[:, t0:t0 + 128].rearrange("(k p) t -> p k t", p=128))
                kb.load("sp", sga_t, sga_t[:], sga[:, t0:t0 + 128].rearrange("(k p) t -> p k t", p=128))
                kb.load("sp", sgb_t, sgb_t[:], sgb[:, t0:t0 + 128].rearrange("(k p) t -> p k t", p=128))
                kb.load("sp", x_t, x_t[:], x_h[t0:t0 + 128, :])
                for cc in range(8):
                    bA = nb()
                    kb.mm(bA[:, 0:128], [(wba_s[:, k, cc * 128:(cc + 1) * 128], ya_t[:, k, :]) for k in range(4)], [wba_s, ya_t], bA)
                    bB = nb()
                    kb.mm(bB[:, 0:128], [(wbb_s[:, k, cc * 128:(cc + 1) * 128], yb_t[:, k, :]) for k in range(4)], [wbb_s, yb_t], bB)
                    kb.op("dve", lambda bA=bA, cc=cc: V.tensor_tensor(out=t1[:], in0=bA[:, 0:128], in1=sga_t[:, cc, :], op=ALU.mult), reads=[bA, sga_t], writes=[t1])
                    kb.op("dve", lambda bB=bB, cc=cc: V.tensor_tensor(out=t2[:], in0=bB[:, 0:128], in1=sgb_t[:, cc, :], op=ALU.mult), reads=[bB, sgb_t], writes=[t2])
                    kb.op("pool", lambda cc=cc: G.tensor_tensor(out=mT[:, cc, :], in0=t1[:], in1=t2[:], op=ALU.add), reads=[t1, t2], writes=[mT])
                ybanks = []
                for half in range(2):
                    b = nb()
                    kb.mm(b[:, :], [(mT[:, kc, :], wout_s[:, kc, half * 512:(half + 1) * 512]) for kc in range(8)], [mT, wout_s], b)
                    ybanks.append(b)
                deepnorm_ln(lambda half: ybanks[half], x_t, g1bc, lng, lnb, r1, xn1, x1o)
                kb.store("sp", x1o, o_x1[t0:t0 + 128, :], x1o[:])
            kb.barrier()

        wup_s = kb.sbuf("wup_s", [128, 8, 2 * DFF], BF16)
        wdn_s = kb.sbuf("wdn_s", [128, 22, D], BF16)
        for kc in range(8):
            load_weight_bf16(kb, stg, wup_s, lambda c0, w, kc=kc: wup_s[:, kc, c0:c0 + w], lambda c0, w, kc=kc: wup[kc * 128:(kc + 1) * 128, c0:c0 + w], 128, 2 * DFF, wst)
        for fb in range(22):
            load_weight_bf16(kb, stg, wdn_s, lambda c0, w, fb=fb: wdn_s[:, fb, c0:c0 + w], lambda c0, w, fb=fb: wdn[fb * 128:(fb + 1) * 128, c0:c0 + w], 128, D, wst)
        lng2 = kb.sbuf("ln2g_s", [128, D], F32)
        kb.load("sp", lng2, lng2[:], ln2g.partition_broadcast(128))
        lnb2 = kb.sbuf("ln2b_s", [128, D], F32)
        kb.load("sp", lnb2, lnb2[:], ln2b.partition_broadcast(128))
        x1_t = kb.sbuf("x1_t", [128, D], F32)
        h2T = kb.sbuf("h2T", [128, 8, 128], BF16)
        ub = [kb.sbuf("ub%d" % i, [128, 130], F32) for i in range(4)]
        cg = kb.sbuf("cg", [128, 128], F32)
        cv = kb.sbuf("cv", [128, 128], F32)
        sil = kb.sbuf("sil", [128, 128], F32)
        act = kb.sbuf("act", [128, 22, 128], BF16)
        carry = kb.sbuf("carry", [128, 44, 2], F32)
        kb.op("dve", lambda: V.memset(carry[:], 0.0), writes=[carry])
        r2 = kb.sbuf("r2", [128, D], F32)
        xn2 = kb.sbuf("xn2", [128, D], F32)
        oo = kb.sbuf("oo", [128, D], F32)
        ui = [0]
        for i in range(NTH):
            t0 = i * 128
            kb.load("sp", x1_t, x1_t[:], o_x1[t0:t0 + 128, :])
            for kc in range(8):
                b = nb()
                kb.wait("pe", kb.deps([x1_t, ident], [b]))
                ins = nc.tensor.transpose(out=b[:, 0:128], in_=x1_t[:, kc * 128:(kc + 1) * 128], identity=ident[:])
                kb.mark_pe(ins, [x1_t, ident], [b])
                kb.op("act", lambda b=b, kc=kc: A.activation(out=h2T[:, kc, :], in_=b[:, 0:128], func=AF.Identity, bias=mod2[:, kc:kc + 1], scale=sc2p[:, kc:kc + 1]),
                      reads=[b, mod2, sc2p], writes=[h2T])
            for fb in range(22):
                for which, blk in ((0, fb), (1, fb + 22)):
                    b = nb()
                    kb.mm(b[:, 0:128], [(wup_s[:, kc, blk * 128:(blk + 1) * 128], h2T[:, kc, :]) for kc in range(8)], [wup_s, h2T], b)
                    u = ub[ui[0] % 4]
                    ui[0] += 1
                    kb.op("act", lambda b=b, u=u: A.copy(out=u[:, 2:130], in_=b[:, 0:128]), reads=[b], writes=[u])
                    kb.op("pool", lambda u=u, blk=blk: G.tensor_copy(out=u[:, 0:2], in_=carry[:, blk, :]), reads=[carry], writes=[u])
                    c = cg if which == 0 else cv
                    kb.op("dve", lambda u=u, blk=blk, c=c: V.tensor_scalar(out=c[:], in0=u[:, 2:130], scalar1=cw_s[:, blk, 2:3], scalar2=cbias_s[:, blk:blk + 1],
                                                                          op0=ALU.mult, op1=ALU.add), reads=[u, cw_s, cbias_s], writes=[c])
                    kb.op("dve", lambda u=u, blk=blk, c=c: V.scalar_tensor_tensor(out=c[:], in0=u[:, 1:129], scalar=cw_s[:, blk, 1:2], in1=c[:], op0=ALU.mult, op1=ALU.add),
                          reads=[u, cw_s], writes=[c])
                    kb.op("dve", lambda u=u, blk=blk, c=c: V.scalar_tensor_tensor(out=c[:], in0=u[:, 0:128], scalar=cw_s[:, blk, 0:1], in1=c[:], op0=ALU.mult, op1=ALU.add),
                          reads=[u, cw_s], writes=[c])
                    if i == 0:
                        kb.op("dve", lambda u=u, blk=blk: V.tensor_scalar(out=carry[:, blk, :], in0=u[:, 128:130], scalar1=flag_s[:, 0:1], scalar2=None, op0=ALU.mult),
                              reads=[u, flag_s], writes=[carry])
                    else:
                        kb.op("pool", lambda u=u, blk=blk: G.tensor_copy(out=carry[:, blk, :], in_=u[:, 128:130]), reads=[u], writes=[carry])
                kb.op("act", lambda: A.activation(out=sil[:], in_=cg[:], func=AF.Silu), reads=[cg], writes=[sil])
                kb.op("pool", lambda fb=fb: G.tensor_tensor(out=act[:, fb, :], in0=sil[:], in1=cv[:], op=ALU.mult), reads=[sil, cv], writes=[act])
            ybanks = []
            for half in range(2):
                b = nb()
                kb.mm(b[:, :], [(act[:, fb, :], wdn_s[:, fb, half * 512:(half + 1) * 512]) for fb in range(22)], [act, wdn_s], b)
                ybanks.append(b)
            deepnorm_ln(lambda half: ybanks[half], x1_t, g2bc, lng2, lnb2, r2, xn2, oo)
            if i > 0:
                kb.store("sp", oo, o_out[t0 - 128:t0, :], oo[:])
        kb.finish()
    return nc


_PROGS = {}
SPLITS = np.cumsum([0, 512, 512, 512, 512, 64, 8, 256, 128, 32, 1024, 1024])
PERM32 = np.concatenate([np.arange(16, 32), np.arange(0, 16)])


def _prog(kind, arg):
    key = (kind, arg)
    if key not in _PROGS:
        _PROGS[key] = {"P": build_P, "A": build_A, "F": build_F}[kind](arg)
    return _PROGS[key]


def _run(nc, in_maps):
    res = run_bass_kernel_spmd(nc, in_maps, core_ids=list(range(8)))
    return res.results


def _t5_bucket(rel):
    n = np.maximum(rel, 0)
    lr = np.log(np.maximum(n, 1).astype(np.float32) / np.float32(16)) / np.float32(np.log(128 / 16))
    large = 16 + (lr * np.float32(16)).astype(np.int32)
    large = np.minimum(large, 31)
    return np.where(n < 16, n, large)


def _consts():
    p = np.arange(128)
    cst = np.zeros((128, 4), np.float32)
    cst[:, 0] = (np.float32(10000.0) ** (-(p % 16).astype(np.float32) * np.float32(2.0 / 32))).astype(np.float32)
    cst[:, 1] = np.where((p % 32) < 16, -1.0, 1.0)
    cst[:, 2] = RMS_EPS
    return cst, np.eye(128, dtype=np.float32), np.ones((128, 128), np.float32)


def run_P(x, c, positions, w_ada, b_ada, w_in, q_norm_g, w_uq, kv_norm_g, w_ukv, S):
    T = S // 4
    cols = [w_in[:, SPLITS[i]:SPLITS[i + 1]] for i in range(11)]
    qa, ka, va, iq, ik, iw, cq, ckv, kr, ga, gb = cols
    w1 = np.ascontiguousarray(np.concatenate([qa, ka, iq, ga, gb, cq, ckv, ik, kr, kr[:, PERM32], iw], axis=1))
    wq = w_uq.reshape(256, 8, 96)
    wuq = np.ascontiguousarray(np.concatenate([wq[:, :, :64].reshape(256, 512), wq[:, :, 64:].reshape(256, 256),
                                               wq[:, :, 64:][:, :, PERM32].reshape(256, 256)], axis=1))
    wkv = w_ukv.reshape(128, 8, 128)
    wukv = np.ascontiguousarray(np.concatenate([wkv[:, :, :64].reshape(128, 512), wkv[:, :, 64:].reshape(128, 512)], axis=1))
    cst, ident, ones = _consts()
    maps = []
    for core in range(8):
        b, cc = divmod(core, 4)
        maps.append(dict(
            x=np.ascontiguousarray(x[b, cc * T:(cc + 1) * T]), ccol=np.ascontiguousarray(c[b].reshape(8, 128).T),
            wada=np.ascontiguousarray(w_ada[:, 0:2048]), bada=np.ascontiguousarray(b_ada[0:2048].reshape(16, 128).T),
            pos=np.ascontiguousarray(positions[b, cc * T:(cc + 1) * T].reshape(1, T)), cst=cst, ident=ident, ones=ones,
            w1=w1, wva=np.ascontiguousarray(va), wuq=wuq, qg=np.ascontiguousarray(q_norm_g.reshape(2, 128).T),
            wukv=wukv, kvg=np.ascontiguousarray(kv_norm_g.reshape(128, 1))))
    r = _run(_prog("P", T), maps)
    out = {}
    for name in ("o_qa", "o_ka", "o_iq", "o_ga", "o_gb", "o_ik", "o_kr", "o_iw", "o_qn", "o_qr", "o_kn"):
        out[name] = [np.concatenate([r[4 * b + cc][name] for cc in range(4)], axis=1) for b in range(2)]
    for name in ("o_va", "o_vm"):
        out[name] = [np.concatenate([r[4 * b + cc][name] for cc in range(4)], axis=0) for b in range(2)]
    return out


def _qtok(cc, S):
    NSB = S // 2048
    return np.concatenate([np.arange((4 * j + cc) * 512, (4 * j + cc + 1) * 512) for j in range(NSB)])


def run_A(P, rel_bias, S):
    T = S // 4
    NSUB = T // 128
    KL = S + 1536
    NKT = KL // 128
    bf = NPBF
    s_i = np.arange(128)[:, None]
    t_i = np.arange(128)[None, :]
    tri = (s_i <= t_i).astype(np.float32)
    tri4 = np.ascontiguousarray(np.tile(tri, (1, 4)).astype(bf))
    bk0 = _t5_bucket(t_i - s_i)
    bk1 = _t5_bucket(128 + t_i - s_i)
    g0 = np.ascontiguousarray(rel_bias[bk0].transpose(0, 2, 1).reshape(128, 1024))
    g1 = np.ascontiguousarray(rel_bias[bk1].transpose(0, 2, 1).reshape(128, 1024))
    b31 = np.ascontiguousarray(np.broadcast_to(rel_bias[31][None, :], (128, 8)))
    negI = (np.eye(128, dtype=np.float32) * NEG).astype(bf)
    pow2 = np.ascontiguousarray(np.broadcast_to((2.0 ** -(np.arange(NIT) + 1.0)).astype(np.float32)[None, :], (128, NIT)))
    cb = np.zeros((128, 4, 512), np.float32)
    sl = np.arange(512)[None, :]
    for tq in range(4):
        cb[:, tq, :] = np.where(sl <= tq * 128 + np.arange(128)[:, None], 0.0, NEG)
    maps = []
    for core in range(8):
        b, cc = divmod(core, 4)
        qt = _qtok(cc, S)
        pl, pr = (3 - cc) * 512, cc * 512

        def qside(arr, rows):
            a = arr.reshape(8, rows, S)[:, :, qt].reshape(8, rows, NSUB, 128)
            return np.ascontiguousarray(a.transpose(1, 2, 0, 3))

        def kside(arr, rows):
            a = np.pad(arr, ((0, 0), (0, 0), (pl, pr))).reshape(8, rows, NKT, 128)
            return np.ascontiguousarray(a.transpose(1, 2, 0, 3))

        def vside(arr):
            a = np.concatenate([arr.reshape(S, 8, 64), np.ones((S, 8, 1), arr.dtype)], axis=2)
            a = np.pad(a, ((pl, pr), (0, 0), (0, 0))).reshape(NKT, 128, 8, 65)
            return np.ascontiguousarray(a.transpose(1, 0, 2, 3))

        qn = P["o_qn"][b].reshape(8, 64, S)
        qr = P["o_qr"][b].reshape(8, 32, S)
        qm = np.concatenate([qn, qr], axis=1).reshape(8 * 96, S)
        kn = P["o_kn"][b].reshape(8, 64, S)
        krr = np.broadcast_to(P["o_kr"][b][None], (8, 32, S))
        km = np.concatenate([kn, krr], axis=1)
        padb = np.zeros((128, 1536), np.float32)
        padb[:, :pl] = NEG
        maps.append(dict(
            iq=qside(P["o_iq"][b], 64), iw=np.ascontiguousarray(P["o_iw"][b][:, qt].T), qa=qside(P["o_qa"][b], 64), qm=qside(qm, 96),
            ik=np.ascontiguousarray(np.pad(P["o_ik"][b], ((0, 0), (pl, pr)))), ka=kside(P["o_ka"][b].reshape(8, 64, S), 64),
            va=vside(P["o_va"][b]), km=kside(km, 96), vm=vside(P["o_vm"][b]),
            padb=padb, cb=cb, tri4=tri4, g0=g0, g1=g1, b31=b31, negI=negI, pow2=pow2))
    r = _run(_prog("A", S), maps)
    ya = [np.zeros((512, S), bf) for _ in range(2)]
    yb = [np.zeros((512, S), bf) for _ in range(2)]
    for core in range(8):
        b, cc = divmod(core, 4)
        qt = _qtok(cc, S)
        for name, dst in (("ya", ya), ("yb", yb)):
            a = r[core][name]
            dst[b][:, qt] = a.transpose(2, 0, 1, 3).reshape(512, T)
    return ya, yb


def run_F(x, c, P, ya, yb, w_ada, b_ada, w_branch_a, w_branch_b, w_out, ln1_g, ln1_b, w_up, conv_w, conv_b, w_down, ln2_g, ln2_b, S):
    T = S // 4
    _, ident, ones = _consts()
    maps = []
    for core in range(8):
        b, cc = divmod(core, 4)
        lo, hi = cc * T, (cc + 1) * T

        def halo_cols(a):
            h = a[:, lo - 128:lo] if cc > 0 else np.zeros((a.shape[0], 128), a.dtype)
            return np.ascontiguousarray(np.concatenate([h, a[:, lo:hi]], axis=1))

        xh = np.concatenate([x[b, lo - 128:lo] if cc > 0 else np.zeros((128, D), np.float32), x[b, lo:hi]], axis=0)
        maps.append(dict(
            x_h=np.ascontiguousarray(xh), ya_f=halo_cols(ya[b]), yb_f=halo_cols(yb[b]), sga=halo_cols(P["o_ga"][b]), sgb=halo_cols(P["o_gb"][b]),
            ccol=np.ascontiguousarray(c[b].reshape(8, 128).T),
            wada_g1=np.ascontiguousarray(w_ada[:, 2048:3072]), bada_g1=np.ascontiguousarray(b_ada[2048:3072].reshape(1, D)),
            wada_2=np.ascontiguousarray(w_ada[:, 3072:5120]), bada_2=np.ascontiguousarray(b_ada[3072:5120].reshape(16, 128).T),
            wada_g2=np.ascontiguousarray(w_ada[:, 5120:6144]), bada_g2=np.ascontiguousarray(b_ada[5120:6144].reshape(1, D)),
            wba=w_branch_a, wbb=w_branch_b, wout=w_out, ln1g=ln1_g.reshape(1, D), ln1b=ln1_b.reshape(1, D),
            wup=w_up, cw=np.ascontiguousarray(conv_w.T.reshape(44, 128, 3).transpose(1, 0, 2)),
            cbias=np.ascontiguousarray(conv_b.reshape(44, 128).T), wdn=w_down, ln2g=ln2_g.reshape(1, D), ln2b=ln2_b.reshape(1, D),
            flag=np.full((128, 1), 0.0 if cc == 0 else 1.0, np.float32), ident=ident, ones=ones))
    r = _run(_prog("F", T), maps)
    out = np.zeros((2, S, D), np.float32)
    for core in range(8):
        b, cc = divmod(core, 4)
        out[b, cc * T:(cc + 1) * T] = r[core]["o_out"]
    return out


def kernel(x, c, positions, rel_bias, w_ada, b_ada, w_in, q_norm_g, w_uq, kv_norm_g, w_ukv,
           w_branch_a, w_branch_b, w_out, ln1_g, ln1_b, w_up, conv_w, conv_b, w_down, ln2_g, ln2_b):
    f = lambda a: np.ascontiguousarray(np.asarray(a))
    x, c, positions, rel_bias = f(x), f(c), f(positions), f(rel_bias)
    S = x.shape[1]
    depth = np.asarray(w_in).shape[0]
    for l in range(depth):
        g = lambda a: f(np.asarray(a)[l])
        P = run_P(x, c, positions, g(w_ada), g(b_ada), g(w_in), g(q_norm_g), g(w_uq), g(kv_norm_g), g(w_ukv), S)
        ya, yb = run_A(P, rel_bias, S)
        x = run_F(x, c, P, ya, yb, g(w_ada), g(b_ada), g(w_branch_a), g(w_branch_b), g(w_out), g(ln1_g), g(ln1_b),
                  g(w_up), g(conv_w), g(conv_b), g(w_down), g(ln2_g), g(ln2_b), S)
    return x.astype(np.float32)
```
